# Optimizing a Trainium2 kernel written in Bass

```python
import jax, jax.numpy as jnp
from jax import lax
import numpy as np

D_MODEL = 4096
BATCH = 4
SEQ = 2048
DEPTH = 2

N_META = 16
N_MIXERS = 2
N_RWKV = (DEPTH + N_MIXERS - 1) // N_MIXERS
N_GLA = DEPTH // N_MIXERS
NORM_EPS = 1e-6

RW_HEAD = 64
RW_HEADS = D_MODEL // RW_HEAD
RW_DECAY_LORA = max(32, int(round(D_MODEL ** 0.5 * 1.8 / 32)) * 32)
RW_AAA_LORA = max(32, int(round(D_MODEL ** 0.5 * 1.8 / 32)) * 32)
RW_GATE_LORA = max(32, int(round(D_MODEL ** 0.8 * 0.6 / 32)) * 32)
RW_LNX_EPS = 64e-5
N_SHIFT_MIX = 6

GLA_HEADS = max(4, D_MODEL // 512)
GLA_DK = D_MODEL // 2
GLA_DV = D_MODEL
GLA_HEAD_K = GLA_DK // GLA_HEADS
GLA_HEAD_V = GLA_DV // GLA_HEADS
GLA_GATE_LORA = 16
GLA_GATE_TAU = 16.0
GLA_CHUNK = 64
GLA_PAD = GLA_CHUNK - N_META
GLA_IN = 2 * GLA_DK + 2 * GLA_DV

D_FF = int(round(D_MODEL * 8 / 3 / 256)) * 256
CONV_W = 3

kernel_name = "hybrid_rwkv7_gla_convffn_sandwich_meta"


def _rmsnorm(x, g):
    x32 = x.astype(jnp.float32)
    y = x32 * lax.rsqrt(jnp.mean(x32 * x32, axis=-1, keepdims=True) + NORM_EPS)
    return (y * g.astype(jnp.float32)).astype(x.dtype)


def _rwkv7_mix(h, mu, w0, w1, w2, a0, a1, a2, g1, g2, k_k, k_a, r_k, w_r, w_k, w_v, w_o, lnx_g, lnx_b):
    B, L, D = h.shape
    xx = jnp.pad(h, ((0, 0), (1, 0), (0, 0)))[:, :-1] - h
    xr, xw, xk, xv, xa, xg = [h + xx * mu[i] for i in range(N_SHIFT_MIX)]
    r = xr @ w_r
    k = xk @ w_k
    v = xv @ w_v
    w = -jax.nn.softplus(-(w0 + jnp.tanh(xw @ w1) @ w2).astype(jnp.float32)) - 0.5
    a = jax.nn.sigmoid((a0 + (xa @ a1) @ a2).astype(jnp.float32))
    g = jax.nn.sigmoid(xg @ g1) @ g2

    heads = lambda t: t.astype(jnp.float32).reshape(B, L, RW_HEADS, RW_HEAD)
    kk = heads(k * k_k)
    kk = kk / jnp.maximum(jnp.sqrt(jnp.sum(kk * kk, axis=-1, keepdims=True)), 1e-12)
    k_h = heads(k.astype(jnp.float32) * (1.0 + (a - 1.0) * k_a.astype(jnp.float32)))
    r_h, v_h, a_h = heads(r), heads(v), heads(a)
    decay = jnp.exp(-jnp.exp(heads(w)))

    def step(S, inp):
        r_t, d_t, k_t, v_t, ra_t, rb_t = inp
        sa = jnp.einsum('bhij,bhj->bhi', S, ra_t)
        S = S * d_t[..., None, :] + sa[..., :, None] * rb_t[..., None, :] + v_t[..., :, None] * k_t[..., None, :]
        return S, jnp.einsum('bhij,bhj->bhi', S, r_t)

    xs = tuple(jnp.moveaxis(t, 1, 0) for t in (r_h, decay, k_h, v_h, -kk, kk * a_h))
    S0 = jnp.zeros((B, RW_HEADS, RW_HEAD, RW_HEAD), jnp.float32)
    _, y = lax.scan(step, S0, xs)
    y = jnp.moveaxis(y, 0, 1)
    mean = jnp.mean(y, axis=-1, keepdims=True)
    var = jnp.mean(jnp.square(y - mean), axis=-1, keepdims=True)
    y = ((y - mean) * lax.rsqrt(var + RW_LNX_EPS)).reshape(B, L, D)
    y = y * lnx_g.astype(jnp.float32) + lnx_b.astype(jnp.float32)
    bonus = jnp.sum(r_h * k_h * r_k.astype(jnp.float32), axis=-1, keepdims=True) * v_h
    y = y + bonus.reshape(B, L, D)
    return (y * g.astype(jnp.float32)).astype(h.dtype) @ w_o


def _to_chunks(t, head_dim):
    B, L = t.shape[:2]
    t = t.astype(jnp.float32).reshape(B, L, GLA_HEADS, head_dim)
    t = jnp.pad(t, ((0, 0), (GLA_PAD, 0), (0, 0), (0, 0)))
    n = (L + GLA_PAD) // GLA_CHUNK
    return t.reshape(B, n, GLA_CHUNK, GLA_HEADS, head_dim).transpose(1, 0, 3, 2, 4)


def _gla_mix(h, w_in, a1, a2, a_b, r_b, norm_g, w_o):
    B, L, D = h.shape
    proj = h @ w_in
    q, k, v, r = jnp.split(proj, [GLA_DK, 2 * GLA_DK, 2 * GLA_DK + GLA_DV], axis=-1)
    log_a = jax.nn.log_sigmoid(((h @ a1) @ a2 + a_b).astype(jnp.float32)) / GLA_GATE_TAU
    qc = _to_chunks(q.astype(jnp.float32) * (GLA_HEAD_K ** -0.5), GLA_HEAD_K)
    kc = _to_chunks(k, GLA_HEAD_K)
    vc = _to_chunks(v, GLA_HEAD_V)
    gc = _to_chunks(log_a, GLA_HEAD_K)
    causal = jnp.tril(jnp.ones((GLA_CHUNK, GLA_CHUNK), dtype=bool))

    def chunk_step(S, inp):
        q_c, k_c, v_c, g_c = inp
        b = jnp.cumsum(g_c, axis=2)
        o_inter = jnp.einsum('bhtd,bhdv->bhtv', q_c * jnp.exp(b), S)
        diff = b[:, :, :, None, :] - b[:, :, None, :, :]
        dec = jnp.exp(jnp.where(causal[:, :, None], diff, -jnp.inf))
        A = jnp.einsum('bhtd,bhsd,bhtsd->bhts', q_c, k_c, dec)
        o_intra = jnp.einsum('bhts,bhsv->bhtv', A, v_c)
        b_last = b[:, :, -1:, :]
        S = S * jnp.exp(b_last[:, :, 0, :, None]) + jnp.einsum('bhsd,bhsv->bhdv', k_c * jnp.exp(b_last - b), v_c)
        return S, o_inter + o_intra

    S0 = jnp.zeros((B, GLA_HEADS, GLA_HEAD_K, GLA_HEAD_V), jnp.float32)
    _, o = lax.scan(chunk_step, S0, (qc, kc, vc, gc))
    n = o.shape[0]
    o = o.transpose(1, 0, 3, 2, 4).reshape(B, n * GLA_CHUNK, GLA_HEADS, GLA_HEAD_V)[:, GLA_PAD:]
    o = o * lax.rsqrt(jnp.mean(o * o, axis=-1, keepdims=True) + NORM_EPS) * norm_g.astype(jnp.float32)
    o = o.reshape(B, L, GLA_DV) * jax.nn.silu((r + r_b).astype(jnp.float32))
    return o.astype(h.dtype) @ w_o


def _conv_ffn(h, w_up, w_gate, conv_w, conv_b, w_down):
    L = h.shape[1]
    u = h @ w_up
    z = h @ w_gate
    zp = jnp.pad(z, ((0, 0), (CONV_W - 1, 0), (0, 0)))
    zc = sum(zp[:, i:i + L] * conv_w[i] for i in range(CONV_W)) + conv_b
    return (jax.nn.silu(zc) * u) @ w_down


def setup_inputs(seed: int = 0) -> dict:
    key = jax.random.key(seed)
    ks = iter(jax.random.split(key, 40))
    nrm = lambda shape, scale: scale * jax.random.normal(next(ks), shape, jnp.float32)
    D, F = D_MODEL, D_FF
    return {
        "x": nrm((BATCH, SEQ, D), 1.0),
        "meta": nrm((N_META, D), 1.0),
        "norm_g": 1.0 + nrm((DEPTH, 4, D), 0.02),
        "rw_mu": jax.random.uniform(next(ks), (N_RWKV, N_SHIFT_MIX, D), jnp.float32),
        "rw_w0": 0.5 + nrm((N_RWKV, D), 0.5),
        "rw_w1": nrm((N_RWKV, D, RW_DECAY_LORA), D ** -0.5),
        "rw_w2": nrm((N_RWKV, RW_DECAY_LORA, D), 0.5 * RW_DECAY_LORA ** -0.5),
        "rw_a0": nrm((N_RWKV, D), 0.3),
        "rw_a1": nrm((N_RWKV, D, RW_AAA_LORA), D ** -0.5),
        "rw_a2": nrm((N_RWKV, RW_AAA_LORA, D), 0.5 * RW_AAA_LORA ** -0.5),
        "rw_g1": nrm((N_RWKV, D, RW_GATE_LORA), D ** -0.5),
        "rw_g2": nrm((N_RWKV, RW_GATE_LORA, D), RW_GATE_LORA ** -0.5),
        "rw_k_k": 0.85 + nrm((N_RWKV, D), 0.05),
        "rw_k_a": 1.0 + nrm((N_RWKV, D), 0.05),
        "rw_r_k": nrm((N_RWKV, RW_HEADS, RW_HEAD), 0.1),
        "rw_wr": nrm((N_RWKV, D, D), D ** -0.5),
        "rw_wk": nrm((N_RWKV, D, D), D ** -0.5),
        "rw_wv": nrm((N_RWKV, D, D), D ** -0.5),
        "rw_wo": nrm((N_RWKV, D, D), D ** -0.5),
        "rw_lnx_g": 1.0 + nrm((N_RWKV, D), 0.02),
        "rw_lnx_b": nrm((N_RWKV, D), 0.01),
        "gla_w_in": nrm((N_GLA, D, GLA_IN), D ** -0.5),
        "gla_a1": nrm((N_GLA, D, GLA_GATE_LORA), D ** -0.5),
        "gla_a2": nrm((N_GLA, GLA_GATE_LORA, GLA_DK), GLA_GATE_LORA ** -0.5),
        "gla_a_b": 1.0 + nrm((N_GLA, GLA_DK), 0.5),
        "gla_r_b": nrm((N_GLA, GLA_DV), 0.01),
        "gla_norm_g": 1.0 + nrm((N_GLA, GLA_HEAD_V), 0.02),
        "gla_wo": nrm((N_GLA, GLA_DV, D), GLA_DV ** -0.5),
        "ffn_up": nrm((DEPTH, D, F), D ** -0.5),
        "ffn_gate": nrm((DEPTH, D, F), D ** -0.5),
        "ffn_conv": nrm((DEPTH, CONV_W, F), CONV_W ** -0.5),
        "ffn_conv_b": nrm((DEPTH, F), 0.01),
        "ffn_down": nrm((DEPTH, F, D), F ** -0.5),
    }


def reference(x, meta, norm_g, rw_mu, rw_w0, rw_w1, rw_w2, rw_a0, rw_a1, rw_a2, rw_g1, rw_g2,
              rw_k_k, rw_k_a, rw_r_k, rw_wr, rw_wk, rw_wv, rw_wo, rw_lnx_g, rw_lnx_b,
              gla_w_in, gla_a1, gla_a2, gla_a_b, gla_r_b, gla_norm_g, gla_wo,
              ffn_up, ffn_gate, ffn_conv, ffn_conv_b, ffn_down):
    B = x.shape[0]
    m = jnp.broadcast_to(meta.astype(x.dtype)[None], (B, N_META, D_MODEL))
    h = jnp.concatenate([m, x], axis=1)
    for i in range(DEPTH):
        j = i // N_MIXERS
        pre = _rmsnorm(h, norm_g[i, 0])
        if i % N_MIXERS == 0:
            mix = _rwkv7_mix(pre, rw_mu[j], rw_w0[j], rw_w1[j], rw_w2[j], rw_a0[j], rw_a1[j], rw_a2[j],
                             rw_g1[j], rw_g2[j], rw_k_k[j], rw_k_a[j], rw_r_k[j], rw_wr[j], rw_wk[j],
                             rw_wv[j], rw_wo[j], rw_lnx_g[j], rw_lnx_b[j])
        else:
            mix = _gla_mix(pre, gla_w_in[j], gla_a1[j], gla_a2[j], gla_a_b[j], gla_r_b[j],
                           gla_norm_g[j], gla_wo[j])
        h = h + _rmsnorm(mix, norm_g[i, 1])
        f = _conv_ffn(_rmsnorm(h, norm_g[i, 2]), ffn_up[i], ffn_gate[i], ffn_conv[i], ffn_conv_b[i], ffn_down[i])
        h = h + _rmsnorm(f, norm_g[i, 3])
    return h[:, N_META:]
```

```python
import os
import numpy as np
import concourse.bass as bass
import concourse.mybir as mybir
from concourse.bass_utils import run_bass_kernel_spmd

F32 = mybir.dt.float32
BF16 = mybir.dt.bfloat16
AF = mybir.ActivationFunctionType
ALU = mybir.AluOpType
EPOCH = 30000
NDS = 24


class Cfg:
    def __init__(self, D=4096, SEQ=2048, NMETA=16, F=11008, LW=128, LA=128, LG=480,
                 GH=8, GLORA=16):
        self.D, self.SEQ, self.NMETA, self.F = D, SEQ, NMETA, F
        self.T = SEQ + NMETA
        self.DC = D // 128
        self.FC = F // 128
        self.LW, self.LA, self.LG = LW, LA, LG
        self.LGP = ((LG + 127) // 128) * 128
        self.GH = GH
        self.DK = D // 2
        self.DV = D
        self.HK = self.DK // GH
        self.HV = self.DV // GH
        self.GLORA = GLORA
        self.GIN = 2 * self.DK + 2 * self.DV
        self.tiles = []
        t = 0
        while t < self.T:
            n = min(448, self.T - t)
            self.tiles.append((t, n))
            t += n
        nt = 6 if self.T % 6 == 0 and self.T // 6 <= 448 else 0
        if nt:
            w = self.T // 6
            tl6 = [(i * w, w) for i in range(6)]
            self.down_blocks = [tl6[0:2], tl6[2:4], tl6[4:6]]
        else:
            self.down_blocks = [[t_] for t_ in self.tiles]
        self.chunks = []
        t = 0
        while t < self.T:
            n = min(64, self.T - t)
            self.chunks.append((t, n))
            t += n


class Rec:
    ENG = ("pe", "act", "dve", "pool", "sp")

    def __init__(self, nc):
        self.nc = nc
        self.ops = {e: [] for e in self.ENG}
        self.seq = {e: 0 for e in self.ENG + ("peA", "peB")}
        self.csem = {e: {} for e in self.ENG + ("peA", "peB")}
        self.dsem = {}
        self.dcount = {}
        self.dnext = {e: 0 for e in self.ENG}
        self.res = {}
        self.waited = {e: {} for e in self.ENG}
        self._sems = []
        self._cms = []
        import os
        self.dump = [] if os.environ.get("REC_DUMP") else None

    def _newsem(self, name):
        cm = self.nc.semaphore(name)
        s = cm.__enter__()
        self._cms.append(cm)
        return s

    def _csem(self, eng, epoch):
        d = self.csem[eng]
        if epoch not in d:
            d[epoch] = self._newsem(f"c_{eng}_{epoch}")
        return d[epoch]

    def _need(self, eng, tok, waits):
        if tok is None:
            return
        if tok[0] == "c":
            _, e2, n = tok
            if e2.startswith("pe") and eng == "pe":
                return
            if self.waited[eng].get(("c", e2), 0) >= n:
                return
            self.waited[eng][("c", e2)] = n
            ep, loc = divmod(n - 1, EPOCH)
            waits.append((self._csem(e2, ep), loc + 1))
        else:
            _, q, idx, cnt = tok
            key = ("d", q, idx)
            if self.waited[eng].get(key, 0) >= cnt:
                return
            self.waited[eng][key] = cnt
            waits.append((self.dsem[(q, idx)], cnt))

    @staticmethod
    def _flat(ks):
        out = []
        for k in ks:
            if isinstance(k, list):
                out.extend(k)
            else:
                out.append(k)
        return out

    def op(self, eng, fn, reads=(), writes=(), dma=False, lane=None):
        veng = eng + lane if lane else eng
        reads = self._flat(reads)
        writes = self._flat(writes)
        psr = [k for k in reads if k[0] == "ps"]
        if psr:
            reads = [k for k in reads if k[0] != "ps"]
            writes = list(writes) + psr
        waits = []
        for r in reads:
            st = self.res.get(r)
            if st is not None:
                self._need(eng, st["w"], waits)
        for w in writes:
            st = self.res.get(w)
            if st is not None:
                self._need(eng, st["w"], waits)
                for tk in st["r"]:
                    self._need(eng, tk, waits)
        if dma:
            idx = self.dnext[eng] % NDS
            self.dnext[eng] += 1
            key = (eng, idx)
            if key not in self.dsem:
                self.dsem[key] = self._newsem(f"d_{eng}_{idx}")
                self.dcount[key] = 0
            prev = self.dcount[key]
            if prev > 0:
                self._need(eng, ("d", eng, idx, prev), waits)
            self.dcount[key] = prev + 16
            tok = ("d", eng, idx, prev + 16)
            inc = (self.dsem[key], 16)
        else:
            self.seq[veng] += 1
            n = self.seq[veng]
            ep, loc = divmod(n - 1, EPOCH)
            tok = ("c", veng, n)
            inc = (self._csem(veng, ep), 1)
        for r in reads:
            st = self.res.setdefault(r, {"w": None, "r": []})
            st["r"] = [t for t in st["r"] if not (t[0] == "c" and tok[0] == "c" and t[1] == tok[1])]
            st["r"].append(tok)
        for w in writes:
            self.res[w] = {"w": tok, "r": []}
        self.ops[eng].append((waits, fn, inc))
        if self.dump is not None:
            self.dump.append((eng, tok, [(getattr(sm, "name", str(sm)), v) for sm, v in waits], writes[:2], reads[:3]))
        return tok

    def barrier(self):
        toks = []
        for e in self.seq:
            if self.seq[e] > 0:
                toks.append(("c", e, self.seq[e]))
        for (q, idx), cnt in self.dcount.items():
            if cnt > 0:
                toks.append(("d", q, idx, cnt))
        for e in self.ENG:
            waits = []
            for t in toks:
                if t[0] == "c" and (t[1] == e or (e == "pe" and t[1].startswith("pe"))):
                    continue
                self._need(e, t, waits)
            if waits:
                self.ops[e].append((waits, None, None))
        self.res = {}

    def final_wait(self, eng, toks):
        waits = []
        for t in toks:
            self._need(eng, t, waits)
        self.ops[eng].append((waits, None, None))

    def emit(self):
        if self.dump is not None:
            for d in self.dump[-int(os.environ["REC_DUMP"]):]:
                print("OP", d)
        nc = self.nc
        ops = self.ops

        def run(engh, lst, embed=False):
            for waits, fn, inc in lst:
                is_dma = inc is not None and inc[1] == 16
                if embed and fn is not None and waits and not is_dma:
                    for s, v in waits[:-1]:
                        engh.wait_ge(s, v)
                    ins = fn(engh)
                    ins._wait_ge(waits[-1][0], waits[-1][1])
                    ins.then_inc(inc[0], inc[1])
                    continue
                for s, v in waits:
                    engh.wait_ge(s, v)
                if fn is not None:
                    ins = fn(engh)
                    ins.then_inc(inc[0], inc[1])

        with nc.Block() as block:
            @block.tensor
            def _(e):
                run(e, ops["pe"])

            @block.scalar
            def _(e):
                run(e, ops["act"], True)

            @block.vector
            def _(e):
                run(e, ops["dve"], True)

            @block.gpsimd
            def _(e):
                run(e, ops["pool"], True)

            @block.sync
            def _(e):
                run(e, ops["sp"])
        for cm in reversed(self._cms):
            cm.__exit__(None, None, None)


class Prog:
    def __init__(self, cfg, only=None):
        self.cfg = cfg
        self.nc = bass.Bass("TRN2", target_bir_lowering=False)
        self.R = Rec(self.nc)
        self._cms = []
        self.psn = 0
        self.uid = 0
        self.pcol_off = {}
        self.dq = 0

    def sb(self, name, shape, dt=F32):
        self.nalloc = getattr(self, "nalloc", 0) + 1
        cm = self.nc.sbuf_tensor(f"{name}_{self.nalloc}", shape, dt)
        t = cm.__enter__()
        self._cms.append(cm)
        return t

    def ps(self, name, shape, dt=F32):
        cm = self.nc.psum_tensor(name, shape, dt)
        t = cm.__enter__()
        self._cms.append(cm)
        return t

    def free_to(self, n):
        self.R.barrier()
        while len(self._cms) > n:
            self._cms.pop().__exit__(None, None, None)

    def dram(self, name, shape, dt, kind=None):
        if kind is None:
            return self.nc.dram_tensor(name, list(shape), dt).ap()
        return self.nc.dram_tensor(name, list(shape), dt, kind=kind).ap()

    def dmaq(self):
        self.dq += 1
        return "sp"

    def dma(self, out, in_, reads, writes, q="sp"):
        return self.R.op(q, lambda e, o=out, i=in_: e.dma_start(out=o, in_=i), reads, writes, dma=True)

    def act(self, out, in_, func, reads, writes, bias=None, scale=None):
        kw = {}
        if bias is not None:
            kw["bias"] = bias
        if scale is not None:
            kw["scale"] = scale
        return self.R.op("act", lambda e, o=out, i=in_, f=func, k=kw: e.activation(out=o, in_=i, func=f, **k),
                         reads, writes)

    def tt(self, eng, out, in0, in1, op, reads, writes):
        return self.R.op(eng, lambda e, o=out, a=in0, b=in1, p=op: e.tensor_tensor(out=o, in0=a, in1=b, op=p),
                         reads, writes)

    def ts(self, eng, out, in0, s1, s2, op0, op1, reads, writes):
        if s2 is None:
            s2, op1 = 0.0, ALU.add
        return self.R.op(eng, lambda e, o=out, a=in0, x=s1, y=s2, p=op0, q=op1: e.tensor_scalar(o, a, x, y, p, q),
                         reads, writes)

    def stt(self, eng, out, in0, scalar, in1, op0, op1, reads, writes):
        eng = "dve"
        return self.R.op(eng, lambda e, o=out, a=in0, s=scalar, b=in1, p=op0, q=op1:
                         e.scalar_tensor_tensor(out=o, in0=a, scalar=s, in1=b, op0=p, op1=q), reads, writes)

    def copy(self, eng, out, in_, reads, writes):
        if eng == "act":
            return self.act(out, in_, AF.Copy, reads, writes)
        return self.R.op(eng, lambda e, o=out, i=in_: e.tensor_copy(out=o, in_=i), reads, writes)

    def memset(self, eng, ap, val, writes):
        return self.R.op(eng, lambda e, a=ap, v=val: e.memset(a, v), (), writes)

    def mm(self, out, lhsT, rhs, start, stop, reads, writes, lane=None):
        return self.R.op("pe", lambda e, o=out, l=lhsT, r=rhs, s=start, t=stop:
                         e.matmul(o, l, r, start=s, stop=t), reads, writes, lane=lane)

    def transpose(self, out, in_, ident, reads, writes):
        return self.R.op("pe", lambda e, o=out, i=in_, d=ident: e.transpose(o, i, d), reads, writes)


def pcol_pack(vecs):
    cols = []
    off = {}
    c = 0
    for name, v in vecs:
        v = np.asarray(v, np.float32).reshape(-1)
        n = v.size // 128
        assert n * 128 == v.size, name
        off[name] = c
        cols.append(v.reshape(n, 128).T)
        c += n
    return np.ascontiguousarray(np.concatenate(cols, axis=1)), off


def keys(name, n):
    return [(name, i) for i in range(n)]


def phase_gemm(P, jobs, xT, KC, tblocks=None):
    cfg, R = P.cfg, P.R
    T = cfg.T
    mark = len(P._cms)
    xname, xd = xT
    if tblocks is None:
        tblocks = [cfg.tiles]
    maxtb = max(sum(n for _, n in blk) for blk in tblocks)
    X = P.sb("gX", [128, KC, maxtb], BF16)
    PCH = min(KC, 8)
    npieces = (KC + PCH - 1) // PCH
    NW = 4
    wst = [P.sb(f"gwst{i}", [128, PCH * 128], F32) for i in range(NW)]
    wb = [P.sb(f"gwb{i}", [128, KC * 128], BF16) for i in range(2)]
    P.uid += 1
    u = P.uid
    wcnt = 0
    pcnt = 0
    ocnt = 0
    orows = {}
    for bi, blk in enumerate(tblocks):
        b0 = blk[0][0]
        bl = sum(n for _, n in blk)
        xv = xd.rearrange("(kc p) t -> p kc t", p=128)
        for kc in range(KC):
            P.dma(X[:, kc, 0:bl], xv[:, kc, b0:b0 + bl], [(xname, kc)], [("gX", u)])
        for job in jobs:
            odt = job["odt"]
            key = ("orow", odt)
            if key not in orows:
                orows[key] = [P.sb(f"gorow{len(orows)}_{i}", [128, maxtb], odt) for i in range(2)]
            orow = orows[key]
            wd = job["w"]
            for m in range(job["MC"]):
                wbi = wcnt % 2
                wcnt += 1
                for pc in range(npieces):
                    k0 = pc * PCH
                    kn = min(PCH, KC - k0)
                    si = pcnt % NW
                    pcnt += 1
                    P.dma(wst[si][:, 0:kn * 128], wd[m, :, k0 * 128:(k0 + kn) * 128], [(job["wname"], m)],
                          [("gwst", u, si)])
                    ceng = "dve" if (pcnt % 2 == 0) else "pool"
                    P.copy(ceng, wb[wbi][:, k0 * 128:(k0 + kn) * 128], wst[si][:, 0:kn * 128],
                           [("gwst", u, si)], [("gwb", u, wbi, pc)])
                oi = ocnt % 2
                ocnt += 1
                off = 0
                for (t0, n) in blk:
                    bank = P.psn % 8
                    P.psn += 1
                    pst = P.psum[bank]
                    for kc in range(KC):
                        P.mm(pst[:, 0:n], wb[wbi][:, kc * 128:(kc + 1) * 128], X[:, kc, off:off + n],
                             kc == 0, kc == KC - 1,
                             [("gwb", u, wbi, kc // PCH), ("gX", u)], [("ps", bank)])
                    bias = None
                    if job.get("bias") is not None:
                        c = P.pcol_off[job["bias"]] + m
                        bias = P.pcols[:, c:c + 1]
                    P.act(orow[oi][:, off:off + n], pst[:, 0:n], job["func"], [("ps", bank)],
                          [("gorow", u, odt, oi)], bias=bias, scale=job.get("scale"))
                    off += n
                P.dma(job["out"][m * 128:(m + 1) * 128, b0:b0 + bl], orow[oi][:, 0:bl],
                      [("gorow", u, odt, oi)], [(job["oname"], m)], q="act")
    P.free_to(mark)


def norm_stats(P, src, n, DC, u, srckey, rstd, rkey):
    bank = P.psn % 8
    P.psn += 1
    pst = P.psum[bank]
    for kc in range(DC):
        si = P.sqn % 3
        P.sqn += 1
        P.act(P.sq[si][:, 0:n], src[:, kc, 0:n], AF.Square, [srckey[kc]], [("sq", si)])
        P.mm(pst[:, 0:n], P.ones[:, :], P.sq[si][:, 0:n], kc == 0, kc == DC - 1, [("sq", si)], [("ps", bank)])
    P.ts("dve", rstd[:, 0:n], pst[:, 0:n], 1.0 / (DC * 128), 1e-6, ALU.mult, ALU.add, [("ps", bank)], [rkey])
    P.act(rstd[:, 0:n], rstd[:, 0:n], AF.Sqrt, [rkey], [rkey])
    P.R.op("dve", lambda e, o=rstd[:, 0:n]: e.reciprocal(out=o, in_=o), [rkey], [rkey])


def phase_norm(P, hname, hd, src, g_add, g_out, mode, outs, mu=None, store_h=True):
    cfg = P.cfg
    DC = cfg.DC
    mark = len(P._cms)
    P.uid += 1
    u = P.uid
    halo = 1 if mode == "mix6" else 0
    W = 448 + halo
    Hh = P.sb("nH", [128, DC, W], F32)
    S = P.sb("nS", [128, DC, W], F32)
    O = [P.sb(f"nO{i}", [128, DC, 448], BF16) for i in range(2)] if mode != "none" else []
    rstd = [P.sb(f"nr{i}", [128, W], F32) for i in range(2)]
    ocnt = 0
    for ti, (t0, n) in enumerate(cfg.tiles):
        h0 = halo if ti > 0 else 0
        nn = n + halo
        c0 = halo - h0
        hv = hd.rearrange("(kc p) t -> p kc t", p=128)
        kH = [("nH", u, k) for k in range(DC)]
        kS = [("nS", u, k) for k in range(DC)]
        P.dma(Hh[:, :, c0:nn], hv[:, :, t0 - h0:t0 + n], keys(hname, DC), kH)
        if halo and ti == 0:
            P.memset("pool", Hh[:, :, 0:1], 0.0, kH)
        if src is not None:
            sname, sd = src
            sv = sd.rearrange("(kc p) t -> p kc t", p=128)
            P.dma(S[:, :, c0:nn], sv[:, :, t0 - h0:t0 + n], keys(sname, DC), kS)
            if halo and ti == 0:
                P.memset("pool", S[:, :, 0:1], 0.0, kS)
            norm_stats(P, S, nn, DC, u, kS, rstd[0], ("nr", u, 0))
            ga = P.pcol_off[g_add]
            for kc in range(DC):
                P.stt("dve", S[:, kc, 0:nn], S[:, kc, 0:nn], P.pcols[:, ga + kc:ga + kc + 1], rstd[0][:, 0:nn],
                      ALU.mult, ALU.mult, [kS[kc], ("nr", u, 0)], [kS[kc]])
                P.tt("pool", Hh[:, kc, 0:nn], Hh[:, kc, 0:nn], S[:, kc, 0:nn], ALU.add, [kS[kc], kH[kc]], [kH[kc]])
        if store_h:
            P.dma(hv[:, :, t0:t0 + n], Hh[:, :, halo:nn], kH, keys(hname, DC))
        if mode == "none":
            continue
        norm_stats(P, Hh, nn, DC, u, kH, rstd[1], ("nr", u, 1))
        go = P.pcol_off[g_out]
        if mode == "plain":
            oi = ocnt % 2
            ocnt += 1
            for kc in range(DC):
                eng = "dve" if kc % 2 == 0 else "pool"
                P.stt(eng, O[oi][:, kc, 0:n], Hh[:, kc, 0:n], P.pcols[:, go + kc:go + kc + 1], rstd[1][:, 0:n],
                      ALU.mult, ALU.mult, [kH[kc], ("nr", u, 1)], [("nO", u, oi, kc)])
            oname, od = outs[0]
            ov = od.rearrange("(kc p) t -> p kc t", p=128)
            P.dma(ov[:, :, t0:t0 + n], O[oi][:, :, 0:n], [("nO", u, oi, k) for k in range(DC)], keys(oname, DC))
        else:
            for kc in range(DC):
                P.stt("dve", S[:, kc, 0:nn], Hh[:, kc, 0:nn], P.pcols[:, go + kc:go + kc + 1], rstd[1][:, 0:nn],
                      ALU.mult, ALU.mult, [kH[kc], ("nr", u, 1)], [kS[kc]])
            for kc in range(DC):
                P.tt("pool", Hh[:, kc, 0:n], S[:, kc, 0:n], S[:, kc, 1:nn], ALU.subtract, [kS[kc], kH[kc]], [kH[kc]])
            mo = P.pcol_off[mu]
            for i in range(6):
                oi = ocnt % 2
                ocnt += 1
                for kc in range(DC):
                    eng = "dve" if kc % 2 == 0 else "pool"
                    c = mo + i * DC + kc
                    P.stt(eng, O[oi][:, kc, 0:n], Hh[:, kc, 0:n], P.pcols[:, c:c + 1], S[:, kc, 1:nn],
                          ALU.mult, ALU.add, [kH[kc], kS[kc]], [("nO", u, oi, kc)])
                oname, od = outs[i]
                ov = od.rearrange("(kc p) t -> p kc t", p=128)
                P.dma(ov[:, :, t0:t0 + n], O[oi][:, :, 0:n], [("nO", u, oi, k) for k in range(DC)], keys(oname, DC))
    P.free_to(mark)


def setup_consts(P):
    cfg = P.cfg
    n = cfg.DK // 128
    a = P.pcol_off["gab"]
    b = P.pcol_off["ngab"]
    P.ts("dve", P.pcols[:, b:b + n], P.pcols[:, a:a + n], -1.0, None, ALU.mult, None, [("pcols",)], [("pcols2",)])
    a = P.pcol_off["k_a"]
    b = P.pcol_off["omka"]
    n = cfg.DC
    P.ts("dve", P.pcols[:, b:b + n], P.pcols[:, a:a + n], -1.0, 1.0, ALU.mult, ALU.add, [("pcols",)], [("pcols3",)])


def phase_ffnact(P, l, uT, zT, Xd):
    cfg = P.cfg
    T, FC = cfg.T, cfg.FC
    mark = len(P._cms)
    P.uid += 1
    u = P.uid
    zrow = [P.sb(f"fz{i}", [128, T + 2], F32) for i in range(2)]
    urow = [P.sb(f"fu{i}", [128, T], F32) for i in range(2)]
    acc = [P.sb(f"fa{i}", [128, T], F32) for i in range(2)]
    orow = [P.sb(f"fo{i}", [128, T], BF16) for i in range(2)]
    for i in range(2):
        P.memset("pool", zrow[i][:, 0:2], 0.0, [("fz", u, i)])
    c0, c1, c2, cb = (P.pcol_off[f"cw{l}_0"], P.pcol_off[f"cw{l}_1"], P.pcol_off[f"cw{l}_2"], P.pcol_off[f"cb{l}"])
    pc = P.pcols
    for fc in range(FC):
        i = fc % 2
        e1 = "dve" if i == 0 else "pool"
        e2 = "pool" if i == 0 else "dve"
        P.dma(zrow[i][:, 2:T + 2], zT[1][fc * 128:(fc + 1) * 128, :], [(zT[0], fc)], [("fz", u, i)])
        P.dma(urow[i][:, :], uT[1][fc * 128:(fc + 1) * 128, :], [(uT[0], fc)], [("fu", u, i)])
        P.ts(e1, acc[i][:, :], zrow[i][:, 2:T + 2], pc[:, c2 + fc:c2 + fc + 1], pc[:, cb + fc:cb + fc + 1],
             ALU.mult, ALU.add, [("fz", u, i)], [("fa", u, i)])
        P.stt(e1, acc[i][:, :], zrow[i][:, 1:T + 1], pc[:, c1 + fc:c1 + fc + 1], acc[i][:, :], ALU.mult, ALU.add,
              [("fz", u, i), ("fa", u, i)], [("fa", u, i)])
        P.stt(e1, acc[i][:, :], zrow[i][:, 0:T], pc[:, c0 + fc:c0 + fc + 1], acc[i][:, :], ALU.mult, ALU.add,
              [("fz", u, i), ("fa", u, i)], [("fa", u, i)])
        P.act(acc[i][:, :], acc[i][:, :], AF.Silu, [("fa", u, i)], [("fa", u, i)])
        P.tt(e2, orow[i][:, :], acc[i][:, :], urow[i][:, :], ALU.mult, [("fa", u, i), ("fu", u, i)], [("fo", u, i)])
        P.dma(Xd[1][fc * 128:(fc + 1) * 128, :], orow[i][:, :], [("fo", u, i)], [(Xd[0], fc)], q="act")
    P.free_to(mark)


def interleave(lists):
    if not lists:
        return
    n = max(len(l) for l in lists)
    for i in range(n):
        for l in lists:
            if i < len(l):
                l[i]()


class PsReg:
    def __init__(self, P):
        self.P = P
        self.n = 0

    def get(self):
        i = self.n % 4
        self.n += 1
        return [self.P.psum[i], self.P.psum[i + 4]], [("ps", i), ("ps", i + 4)]


C0 = float(np.exp(-0.5))


def phase_rwkv(P, rT, kT, vT, swT, aT, gT, Xo):
    cfg = P.cfg
    T = cfg.T
    NCH = len(cfg.chunks)
    TP = 64 + NCH * 64
    TW = NCH * 64
    mark = len(P._cms)
    P.uid += 1
    u = P.uid
    pr = PsReg(P)
    pc = P.pcols
    names = ["r", "k", "v", "a", "sw", "kk", "kh", "bon", "cum", "Ep", "Em", "Bh", "Kh", "y"]
    tl = {n: P.sb("rw_" + n, [128, TP], BF16 if n in ("Kh", "Bh") else F32) for n in names}
    tl["vb"] = P.sb("rw_vb", [128, TP], BF16)
    names = names + ["vb"]
    AR = P.sb("rw_AR", [128, NCH, 2, 64], BF16)
    onesc = P.sb("rw_ones", [128, 64], F32)
    obf = [P.sb(f"rw_obf{i}", [128, T], BF16) for i in range(2)]
    NS = 8
    SC1 = [P.sb(f"rw_sc1_{i}", [128, 128], BF16) for i in range(NS)]
    SC2 = [P.sb(f"rw_sc2_{i}", [128, 128], BF16) for i in range(NS)]
    TTs = [P.sb(f"rw_tt_{i}", [128, 64], BF16) for i in range(NS)]
    Vst = [P.sb(f"rw_vst_{i}", [128, 64], BF16) for i in range(NS)]
    Bst = [P.sb(f"rw_bst_{i}", [128, 64], BF16) for i in range(NS)]
    Kst = [P.sb(f"rw_kst_{i}", [128, 64], BF16) for i in range(NS)]
    Pbf = [P.sb(f"rw_pbf_{i}", [128, 64], BF16) for i in range(2)]
    Nb = [[P.sb(f"rw_n_{i}_{j}", [128, 64], BF16) for j in range(2)] for i in range(NS)]
    NTb = [[P.sb(f"rw_nt_{i}_{j}", [128, 64], BF16) for j in range(2)] for i in range(NS)]
    TTb = [[P.sb(f"rw_ttb_{i}_{j}", [128, 64], BF16) for j in range(2)] for i in range(NS)]
    Wsb = [P.sb(f"rw_w_{i}", [128, 64], BF16) for i in range(2)]
    Usb = [P.sb(f"rw_u_{i}", [128, 64], BF16) for i in range(2)]
    Pst = [P.sb(f"rw_p_{i}", [128, 64], F32) for i in range(2)]
    Ptmp = P.sb("rw_ptmp", [128, 64], F32)
    P.memset("dve", onesc[:, :], 1.0, [("rw_ones", u)])
    for n in names:
        P.memset("pool", tl[n][:, :], 0.0, [("rw", u, n)])
    K = lambda n: ("rw", u, n)
    HP = [slice(0, 64), slice(64, 128)]
    LN = ["A", "B"]
    pieces = [(c0, min(512, TW - c0)) for c0 in range(0, TW, 512)]

    def full(n):
        return tl[n][:, 64:64 + TW]

    def b64(src, fn):
        for (c0, cn) in pieces:
            bank = P.psn % 8
            P.psn += 1
            pst = P.psum[bank]
            P.mm(pst[:, 0:cn], P.b64[:, :], tl[src][:, 64 + c0:64 + c0 + cn], True, True, [K(src)], [("ps", bank)])
            fn(pst[:, 0:cn], c0, cn, ("ps", bank))

    for fc in range(cfg.DC):
        col = lambda nm: pc[:, P.pcol_off[nm] + fc:P.pcol_off[nm] + fc + 1]
        rows = slice(fc * 128, (fc + 1) * 128)
        for nm, src in (("r", rT), ("k", kT), ("v", vT), ("a", aT), ("sw", swT)):
            P.dma(tl[nm][:, 64:64 + T], src[1][rows, :], [(src[0], fc)], [K(nm)])
        P.ts("pool", full("kk"), full("k"), col("k_k"), None, ALU.mult, None, [K("k")], [K("kk")])
        P.act(full("y"), full("kk"), AF.Square, [K("kk")], [K("y")])

        def f1(ps, c0, cn, key):
            sl = slice(64 + c0, 64 + c0 + cn)
            P.act(tl["cum"][:, sl], ps, AF.Sqrt, [key], [K("cum")])
        b64("y", f1)
        P.ts("dve", full("cum"), full("cum"), 1e-12, None, ALU.max, None, [K("cum")], [K("cum")])
        P.R.op("dve", lambda e, o=full("cum"): e.reciprocal(out=o, in_=o), [K("cum")], [K("cum")])
        P.tt("pool", full("kk"), full("kk"), full("cum"), ALU.mult, [K("kk"), K("cum")], [K("kk")])
        P.ts("dve", full("kh"), full("a"), col("k_a"), col("omka"), ALU.mult, ALU.add, [K("a")], [K("kh")])
        P.tt("pool", full("kh"), full("kh"), full("k"), ALU.mult, [K("kh"), K("k")], [K("kh")])
        P.stt("dve", full("y"), full("r"), col("r_k"), full("kh"), ALU.mult, ALU.mult, [K("r"), K("kh")], [K("y")])

        def f2(ps, c0, cn, key):
            sl = slice(64 + c0, 64 + c0 + cn)
            P.tt("dve", tl["bon"][:, sl], ps, tl["v"][:, sl], ALU.mult, [key, K("v")], [K("bon")])
        b64("y", f2)
        P.copy("pool", full("vb"), full("v"), [K("v")], [K("vb")])
        for (t0, n) in cfg.chunks:
            sl = slice(64 + t0, 64 + t0 + 64)
            P.R.op("dve", lambda e, o=tl["cum"][:, sl], d1=tl["sw"][:, sl]:
                   e.tensor_tensor_scan(out=o, data0=onesc[:, :], data1=d1, initial=0.0, op0=ALU.mult, op1=ALU.add),
                   [K("sw"), ("rw_ones", u)], [K("cum")])
        P.act(full("Ep"), full("cum"), AF.Exp, [K("cum")], [K("Ep")], scale=-C0)
        P.act(full("Em"), full("cum"), AF.Exp, [K("cum")], [K("Em")], scale=C0)
        P.tt("pool", full("cum"), full("cum"), full("sw"), ALU.subtract, [K("cum"), K("sw")], [K("cum")])
        P.act(full("cum"), full("cum"), AF.Exp, [K("cum")], [K("cum")], scale=-C0)
        arv = lambda j: AR[:, :, j, :]
        v3 = lambda n: tl[n][:, 64:64 + TW].rearrange("p (c t) -> p c t", t=64)
        P.stt("dve", arv(0), v3("kk"), -1.0, v3("cum"), ALU.mult, ALU.mult, [K("kk"), K("cum")], [K("AR")])
        P.tt("pool", arv(1), v3("r"), v3("Ep"), ALU.mult, [K("r"), K("Ep")], [K("AR")])
        P.tt("pool", full("Bh"), full("kk"), full("a"), ALU.mult, [K("kk"), K("a")], [K("Bh")])
        P.tt("dve", full("Bh"), full("Bh"), full("Em"), ALU.mult, [K("Bh"), K("Em")], [K("Bh")])
        P.tt("pool", full("Kh"), full("kh"), full("Em"), ALU.mult, [K("kh"), K("Em")], [K("Kh")])
        P.memset("pool", Pst[0][:, :], 0.0, [("rw_p", u, 0, 0), ("rw_p", u, 0, 1)])
        P.memset("pool", Pbf[0][:, :], 0.0, [("rw_pb", u, 0, 0), ("rw_pb", u, 0, 1)])

        def pre_stages(ci):
            s = ci % NS
            t0 = ci * 64
            ch = slice(64 + t0, 128 + t0)
            st = []
            EV = ["dve", "act"]

            def a():
                r, kr = pr.get()
                for h, hp in enumerate(HP):
                    P.mm(r[h][hp, 0:128], tl["Kh"][hp, ch], AR[hp, ci, :, :], True, True, [K("Kh"), K("AR")], [kr[h]], lane=LN[h])
                    P.mm(r[h][hp, 128:256], tl["Bh"][hp, ch], AR[hp, ci, :, :], True, True, [K("Bh"), K("AR")], [kr[h]], lane=LN[h])
                    P.mm(r[h][hp, 256:320], AR[hp, ci, 0, :], tl["Bh"][hp, ch], True, True, [K("Bh"), K("AR")], [kr[h]], lane=LN[h])
                for h, hp in enumerate(HP):
                    P.tt("dve", SC1[s][hp, :], r[h][hp, 0:128], P.mask12[hp, :], ALU.mult, [kr[h]], [("sc1", u, s, h)])
                    P.tt("dve", SC2[s][hp, :], r[h][hp, 128:256], P.mask12[hp, :], ALU.mult, [kr[h]], [("sc2", u, s, h)])
                    P.tt("dve", Nb[s][0][hp, :], r[h][hp, 256:320], P.maskL[hp, :], ALU.mult, [kr[h]], [("nb", u, s, 0, h)])
                    P.copy("pool", NTb[s][0][hp, :], SC2[s][hp, 0:64], [("sc2", u, s, h)], [("ntb", u, s, 0, h)])
                    P.tt("pool", TTb[s][0][hp, :], SC2[s][hp, 0:64], P.identst[hp, :], ALU.add, [("sc2", u, s, h)],
                         [("ttb", u, s, 0, h)])
            st.append(a)
            def lvl_stage(lvl):
                def f():
                    q, kq = pr.get()
                    do_b = lvl <= 5
                    do_c = lvl >= 2
                    if do_b:
                        i0, i1 = (lvl - 1) % 2, lvl % 2
                    if do_c:
                        k = lvl - 1
                        j0, j1 = (k - 1) % 2, k % 2
                    for h, hp in enumerate(HP):
                        if do_c:
                            P.mm(q[h][hp, 0:64], P.identbf[hp, :], TTb[s][j0][hp, :], True, False,
                                 [("ttb", u, s, j0, h)], [kq[h]], lane=LN[h])
                            P.mm(q[h][hp, 0:64], Nb[s][j1][hp, :], TTb[s][j0][hp, :], False, True,
                                 [("nb", u, s, j1, h), ("ttb", u, s, j0, h)], [kq[h]], lane=LN[h])
                        if do_b:
                            P.mm(q[h][hp, 64:128], NTb[s][i0][hp, :], Nb[s][i0][hp, :], True, True,
                                 [("ntb", u, s, i0, h), ("nb", u, s, i0, h)], [kq[h]], lane=LN[h])
                            if lvl < 5:
                                P.mm(q[h][hp, 128:192], Nb[s][i0][hp, :], NTb[s][i0][hp, :], True, True,
                                     [("ntb", u, s, i0, h), ("nb", u, s, i0, h)], [kq[h]], lane=LN[h])
                    for h, hp in enumerate(HP):
                        if do_c:
                            dst = TTs[s] if k == 5 else TTb[s][j1]
                            dk = ("tts", u, s, h) if k == 5 else ("ttb", u, s, j1, h)
                            P.copy(EV[h], dst[hp, :], q[h][hp, 0:64], [kq[h]], [dk])
                        if do_b:
                            P.copy(EV[h], Nb[s][i1][hp, :], q[h][hp, 64:128], [kq[h]], [("nb", u, s, i1, h)])
                            if lvl < 5:
                                P.copy(EV[h], NTb[s][i1][hp, :], q[h][hp, 128:192], [kq[h]], [("ntb", u, s, i1, h)])
                return f
            for lvl in range(1, 7):
                st.append(lvl_stage(lvl))

            def d():
                q, kq = pr.get()
                lst = (("vb", Vst, "vst"), ("Bh", Bst, "bst"), ("Kh", Kst, "kst"))
                for j, (nm, dstl, dkey) in enumerate(lst):
                    cs = slice(j * 64, j * 64 + 64)
                    P.mm(q[1][:, cs], tl[nm][64:128, 64 + t0 - 64:64 + t0 + 64], P.identbf2[64:128, 64:128], True, True,
                         [K(nm)], [kq[1]], lane="B")
                    P.mm(q[0][0:64, cs], tl[nm][0:64, ch], P.identbf2[0:64, 0:64], True, True, [K(nm)], [kq[0]], lane="A")
                for j, (nm, dstl, dkey) in enumerate(lst):
                    cs = slice(j * 64, j * 64 + 64)
                    P.copy("act", dstl[s][0:64, :], q[0][0:64, cs], [kq[0]], [(dkey, u, s, 0)])
                    P.copy("act", dstl[s][64:128, :], q[1][64:128, cs], [kq[1]], [(dkey, u, s, 1)])
            st.insert(1, d)
            return st

        def scan_stages(ci):
            s = ci % NS
            t0 = ci * 64
            ch = slice(64 + t0, 128 + t0)
            pi, po = ci % 2, (ci + 1) % 2
            wi = ci % 2
            st = []
            EV = ["act", "dve"]

            def a():
                q, kq = pr.get()
                for h, hp in enumerate(HP):
                    P.mm(q[h][hp, 0:64], AR[hp, ci, 0, :], Pbf[pi][hp, :], True, False, [K("AR"), ("rw_pb", u, pi, h)], [kq[h]], lane=LN[h])
                    P.mm(q[h][hp, 0:64], SC1[s][hp, 0:64], Vst[s][hp, :], False, True,
                         [("sc1", u, s, h), ("vst", u, s, h)], [kq[h]], lane=LN[h])
                for h, hp in enumerate(HP):
                    P.copy(EV[h], Wsb[wi][hp, :], q[h][hp, 0:64], [kq[h]], [("rw_w", u, wi, h)])
            st.append(a)

            def b():
                q, kq = pr.get()
                for h, hp in enumerate(HP):
                    P.mm(q[h][hp, 0:64], TTs[s][hp, :], Wsb[wi][hp, :], True, True,
                         [("tts", u, s, h), ("rw_w", u, wi, h)], [kq[h]], lane=LN[h])
                for h, hp in enumerate(HP):
                    P.copy(EV[h], Usb[wi][hp, :], q[h][hp, 0:64], [kq[h]], [("rw_u", u, wi, h)])
            st.append(b)

            def c():
                q, kq = pr.get()
                for h, hp in enumerate(HP):
                    P.mm(q[h][hp, 0:64], Bst[s][hp, :], Usb[wi][hp, :], True, False,
                         [("bst", u, s, h), ("rw_u", u, wi, h)], [kq[h]], lane=LN[h])
                    P.mm(q[h][hp, 0:64], Kst[s][hp, :], Vst[s][hp, :], False, True,
                         [("kst", u, s, h), ("vst", u, s, h)], [kq[h]], lane=LN[h])
                for h, hp in enumerate(HP):
                    P.mm(q[h][hp, 64:128], Pbf[pi][hp, :], AR[hp, ci, 1, :], True, False, [("rw_pb", u, pi, h), K("AR")], [kq[h]], lane=LN[h])
                    P.mm(q[h][hp, 64:128], Usb[wi][hp, :], SC2[s][hp, 64:128], False, False,
                         [("rw_u", u, wi, h), ("sc2", u, s, h)], [kq[h]], lane=LN[h])
                    P.mm(q[h][hp, 64:128], Vst[s][hp, :], SC1[s][hp, 64:128], False, True,
                         [("vst", u, s, h), ("sc1", u, s, h)], [kq[h]], lane=LN[h])
                for h, hp in enumerate(HP):
                    P.tt("dve", Ptmp[hp, :], q[h][hp, 0:64], Pst[pi][hp, :], ALU.add, [kq[h], ("rw_p", u, pi, h)],
                         [("rw_ptmp", u, h)])
                    P.act(Pst[po][hp, :], Ptmp[hp, :], AF.Copy, [("rw_ptmp", u, h), K("Ep")], [("rw_p", u, po, h)],
                          scale=tl["Ep"][hp, 64 + t0 + 63:64 + t0 + 64])
                    P.ts("pool", Pbf[po][hp, :], Ptmp[hp, :], tl["Ep"][hp, 64 + t0 + 63:64 + t0 + 64], None, ALU.mult, None,
                         [("rw_ptmp", u, h), K("Ep")], [("rw_pb", u, po, h)])
                for h, hp in enumerate(HP):
                    P.copy(EV[h], tl["y"][hp, ch], q[h][hp, 64:128], [kq[h]], [K("y")])
            st.append(c)
            return st

        import os
        RWDBG = int(os.environ.get("RWDBG", "9"))
        if RWDBG < 1:
            continue
        LA = 4
        NPRE = int(os.environ.get("RW_NPRE", "99"))
        NSCAN = int(os.environ.get("RW_NSCAN", "99"))
        for c2 in range(0, NCH + LA + 1, 2):
            lists = []
            sc = []
            for ci in (c2, c2 + 1):
                if ci < NCH:
                    lists.append(pre_stages(ci)[:NPRE])
                if 0 <= ci - LA < NCH:
                    sc += scan_stages(ci - LA)[:NSCAN]
            if sc:
                lists.append(sc)
            interleave(lists)
        if RWDBG < 3:
            continue
        P.dma(tl["sw"][:, 64:64 + T], gT[1][rows, :], [(gT[0], fc)], [K("sw")])

        def f3(ps, c0, cn, key):
            sl = slice(64 + c0, 64 + c0 + cn)
            P.stt("dve", tl["y"][:, sl], ps, -1.0 / 64, tl["y"][:, sl], ALU.mult, ALU.add, [key, K("y")], [K("y")])
        b64("y", f3)
        P.act(full("cum"), full("y"), AF.Square, [K("y")], [K("cum")])

        def f4(ps, c0, cn, key):
            sl = slice(64 + c0, 64 + c0 + cn)
            P.ts("dve", tl["Ep"][:, sl], ps, 1.0 / 64, 64e-5, ALU.mult, ALU.add, [key], [K("Ep")])
        b64("cum", f4)
        P.act(full("Ep"), full("Ep"), AF.Sqrt, [K("Ep")], [K("Ep")])
        P.R.op("dve", lambda e, o=full("Ep"): e.reciprocal(out=o, in_=o), [K("Ep")], [K("Ep")])
        P.tt("pool", full("y"), full("y"), full("Ep"), ALU.mult, [K("y"), K("Ep")], [K("y")])
        P.ts("dve", full("y"), full("y"), col("lnx_g"), col("lnx_b"), ALU.mult, ALU.add, [K("y")], [K("y")])
        P.tt("pool", full("y"), full("y"), full("bon"), ALU.add, [K("y"), K("bon")], [K("y")])
        oi = fc % 2
        P.tt("dve", obf[oi][:, :], tl["y"][:, 64:64 + T], tl["sw"][:, 64:64 + T], ALU.mult, [K("y"), K("sw")],
             [("rw_obf", u, oi)])
        P.dma(Xo[1][rows, :], obf[oi][:, :], [("rw_obf", u, oi)], [(Xo[0], fc)])
    P.free_to(mark)


def phase_gla(P, projT, gaT, Xgo):
    cfg = P.cfg
    T = cfg.T
    NCH = len(cfg.chunks)
    TW = NCH * 64
    KCH, VCH = cfg.HK // 128, cfg.HV // 128
    DK, DV, HV = cfg.DK, cfg.DV, cfg.HV
    mark = len(P._cms)
    P.uid += 1
    u = P.uid
    pc = P.pcols
    mk = lambda nm, n: [P.sb(f"gl_{nm}{i}", [128, TW], F32) for i in range(n)]
    q, k, ga, cum = mk("q", KCH), mk("k", KCH), mk("ga", KCH), mk("cum", KCH)
    v, o = mk("v", VCH), mk("o", VCH)
    onesc = P.sb("gl_ones", [128, 64], F32)
    obf = [P.sb(f"gl_obf{i}", [128, T], BF16) for i in range(2)]
    rsd = P.sb("gl_rstd", [128, TW], F32)
    NS = 4
    AT = [P.sb(f"gl_at{i}", [128, 64], F32) for i in range(NS)]
    ktm = [P.sb(f"gl_ktm{i}", [128, KCH * 128], F32) for i in range(NS)]
    vtm = [P.sb(f"gl_vtm{i}", [128, VCH * 128], F32) for i in range(NS)]
    Sst = P.sb("gl_S", [128, KCH, HV], F32)
    Stmp = P.sb("gl_Stmp", [128, HV], F32)
    P.memset("dve", onesc[:, :], 1.0, [("gl_ones", u)])
    K = lambda nm, i: ("gl", u, nm, i)
    for nm, lst in (("q", q), ("k", k), ("ga", ga), ("cum", cum), ("v", v), ("o", o)):
        for i, t_ in enumerate(lst):
            P.memset("pool", t_[:, :], 0.0, [K(nm, i)])
    for i in range(NS):
        P.memset("pool", AT[i][:, :], 0.0, [("gl_at", u, i)])
        P.memset("pool", ktm[i][:, :], 0.0, [("gl_ktm", u, i)])
        P.memset("pool", vtm[i][:, :], 0.0, [("gl_vtm", u, i)])
    pieces = [(c0, min(512, TW - c0)) for c0 in range(0, TW, 512)]
    bankn = [0]

    def bank():
        b = bankn[0] % 8
        bankn[0] += 1
        return P.psum[b], ("ps", b)

    scale = float(cfg.HK) ** -0.5
    for h in range(cfg.GH):
        for kc in range(KCH):
            r0 = h * cfg.HK + kc * 128
            P.dma(q[kc][:, 0:T], projT[1][r0:r0 + 128, :], [(projT[0], r0 // 128)], [K("q", kc)])
            P.dma(k[kc][:, 0:T], projT[1][DK + r0:DK + r0 + 128, :], [(projT[0], (DK + r0) // 128)], [K("k", kc)])
            P.dma(ga[kc][:, 0:T], gaT[1][r0:r0 + 128, :], [(gaT[0], r0 // 128)], [K("ga", kc)])
        for vc in range(VCH):
            r0 = 2 * DK + h * HV + vc * 128
            P.dma(v[vc][:, 0:T], projT[1][r0:r0 + 128, :], [(projT[0], r0 // 128)], [K("v", vc)])
        for kc in range(KCH):
            P.act(ga[kc][:, 0:T], ga[kc][:, 0:T], AF.Ln, [K("ga", kc)], [K("ga", kc)], bias=1.0)
            for ci in range(NCH):
                sl = slice(ci * 64, ci * 64 + 64)
                P.R.op("dve", lambda e, o_=cum[kc][:, sl], d1=ga[kc][:, sl]:
                       e.tensor_tensor_scan(out=o_, data0=onesc[:, :], data1=d1, initial=0.0, op0=ALU.mult, op1=ALU.add),
                       [K("ga", kc), ("gl_ones", u)], [K("cum", kc)])
            P.act(ga[kc][:, :], cum[kc][:, :], AF.Exp, [K("cum", kc)], [K("ga", kc)], scale=-1.0 / 16)
            P.act(cum[kc][:, :], cum[kc][:, :], AF.Exp, [K("cum", kc)], [K("cum", kc)], scale=1.0 / 16)
            P.stt("dve", q[kc][:, :], q[kc][:, :], scale, ga[kc][:, :], ALU.mult, ALU.mult,
                  [K("q", kc), K("ga", kc)], [K("q", kc)])
            P.tt("pool", k[kc][:, :], k[kc][:, :], cum[kc][:, :], ALU.mult, [K("k", kc), K("cum", kc)], [K("k", kc)])
        P.memset("pool", Sst[:, :, :], 0.0, [("gl_S", u, i) for i in range(KCH)])

        def pre_stages(ci):
            s = ci % NS
            ch = slice(ci * 64, ci * 64 + 64)
            st = []

            def a():
                pa, ka = bank()
                for kc in range(KCH):
                    P.mm(pa[0:64, 0:64], k[kc][:, ch], q[kc][:, ch], kc == 0, kc == KCH - 1,
                         [K("k", kc), K("q", kc)], [ka])
                P.tt("dve", AT[s][0:64, :], pa[0:64, 0:64], P.maskG[0:64, :], ALU.mult, [ka], [("gl_at", u, s)])
            st.append(a)

            def b():
                pb, kb = bank()
                for kc in range(KCH):
                    P.mm(pb[0:64, kc * 128:(kc + 1) * 128], k[kc][:, ch], P.ident[:, :], True, True, [K("k", kc)], [kb])
                P.copy("act", ktm[s][0:64, :], pb[0:64, 0:KCH * 128], [kb], [("gl_ktm", u, s)])
                pv, kv = bank()
                for vc in range(VCH):
                    P.mm(pv[0:64, vc * 128:(vc + 1) * 128], v[vc][:, ch], P.ident[:, :], True, True, [K("v", vc)], [kv])
                P.copy("act", vtm[s][0:64, :], pv[0:64, 0:VCH * 128], [kv], [("gl_vtm", u, s)])
            st.append(b)
            return st

        def scan_stages(ci):
            s = ci % NS
            ch = slice(ci * 64, ci * 64 + 64)
            st = []

            def a():
                for vc in range(VCH):
                    po, ko = bank()
                    P.mm(po[:, 0:64], vtm[s][:, vc * 128:(vc + 1) * 128], AT[s][:, :], True, False,
                         [("gl_vtm", u, s), ("gl_at", u, s)], [ko])
                    for kc in range(KCH):
                        P.mm(po[:, 0:64], Sst[:, kc, vc * 128:(vc + 1) * 128], q[kc][:, ch], False, kc == KCH - 1,
                             [("gl_S", u, kc), K("q", kc)], [ko])
                    P.copy("act" if vc % 2 == 0 else "dve", o[vc][:, ch], po[:, 0:64], [ko], [K("o", vc)])
            st.append(a)

            def b():
                for kc in range(KCH):
                    pss, kss = bank()
                    P.mm(pss[:, 0:HV], ktm[s][:, kc * 128:(kc + 1) * 128], vtm[s][:, :], True, True,
                         [("gl_ktm", u, s), ("gl_vtm", u, s)], [kss])
                    P.tt("dve", Stmp[:, :], pss[:, 0:HV], Sst[:, kc, :], ALU.add, [kss, ("gl_S", u, kc)], [("gl_Stmp", u)])
                    P.act(Sst[:, kc, :], Stmp[:, :], AF.Copy, [("gl_Stmp", u), K("ga", kc)], [("gl_S", u, kc)],
                          scale=ga[kc][:, ci * 64 + 63:ci * 64 + 64])
            st.append(b)
            return st

        LA = 2
        for ci in range(NCH + LA):
            lists = []
            if ci < NCH:
                lists.append(pre_stages(ci))
            if ci - LA >= 0:
                lists.append(scan_stages(ci - LA))
            interleave(lists)
        for vc in range(VCH):
            P.act(v[vc][:, :], o[vc][:, :], AF.Square, [K("o", vc)], [K("v", vc)])
        for (c0, cn) in pieces:
            pn, kn = bank()
            for vc in range(VCH):
                P.mm(pn[:, 0:cn], P.ones[:, :], v[vc][:, c0:c0 + cn], vc == 0, vc == VCH - 1, [K("v", vc)], [kn])
            P.ts("dve", rsd[:, c0:c0 + cn], pn[:, 0:cn], 1.0 / HV, 1e-6, ALU.mult, ALU.add, [kn], [("gl_rsd", u)])
        P.act(rsd[:, :], rsd[:, :], AF.Sqrt, [("gl_rsd", u)], [("gl_rsd", u)])
        P.R.op("dve", lambda e, o_=rsd[:, :]: e.reciprocal(out=o_, in_=o_), [("gl_rsd", u)], [("gl_rsd", u)])
        for vc in range(VCH):
            f0 = h * HV + vc * 128
            fc = f0 // 128
            r0 = 2 * DK + DV + f0
            P.dma(v[vc][:, 0:T], projT[1][r0:r0 + 128, :], [(projT[0], r0 // 128)], [K("v", vc)])
            cg = P.pcol_off["gng"] + fc
            cb = P.pcol_off["grb"] + fc
            P.stt("dve", o[vc][:, :], o[vc][:, :], pc[:, cg:cg + 1], rsd[:, :], ALU.mult, ALU.mult,
                  [K("o", vc), ("gl_rsd", u)], [K("o", vc)])
            P.act(v[vc][:, 0:T], v[vc][:, 0:T], AF.Silu, [K("v", vc)], [K("v", vc)], bias=pc[:, cb:cb + 1])
            oi = (h * VCH + vc) % 2
            P.tt("pool", obf[oi][:, :], o[vc][:, 0:T], v[vc][:, 0:T], ALU.mult, [K("o", vc), K("v", vc)],
                 [("gl_obf", u, oi)])
            P.dma(Xgo[1][f0:f0 + 128, :], obf[oi][:, :], [("gl_obf", u, oi)], [(Xgo[0], fc)])
    P.free_to(mark)


def wrelayout(W):
    K, N = W.shape
    KC, MC = (K + 127) // 128, (N + 127) // 128
    if K != KC * 128 or N != MC * 128:
        Wp = np.zeros((KC * 128, MC * 128), np.float32)
        Wp[:K, :N] = W
    else:
        Wp = np.asarray(W, np.float32)
    return np.ascontiguousarray(Wp.reshape(KC, 128, MC, 128).transpose(2, 1, 0, 3).reshape(MC, 128, KC * 128))


def pad128(v):
    v = np.asarray(v, np.float32).reshape(-1)
    n = ((v.size + 127) // 128) * 128
    if n != v.size:
        v = np.concatenate([v, np.zeros(n - v.size, np.float32)])
    return v


def host_prep(cfg, inp):
    W = {}
    W["wr"] = wrelayout(inp["rw_wr"][0]); W["wk"] = wrelayout(inp["rw_wk"][0]); W["wv"] = wrelayout(inp["rw_wv"][0])
    W["wo"] = wrelayout(inp["rw_wo"][0])
    W["w1"] = wrelayout(inp["rw_w1"][0]); W["w2"] = wrelayout(inp["rw_w2"][0])
    W["a1"] = wrelayout(inp["rw_a1"][0]); W["a2"] = wrelayout(inp["rw_a2"][0])
    W["g1"] = wrelayout(inp["rw_g1"][0]); W["g2"] = wrelayout(inp["rw_g2"][0])
    W["gin"] = wrelayout(inp["gla_w_in"][0]); W["ga1"] = wrelayout(inp["gla_a1"][0])
    W["ga2"] = wrelayout(inp["gla_a2"][0]); W["gwo"] = wrelayout(inp["gla_wo"][0])
    for l in range(2):
        W[f"up{l}"] = wrelayout(inp["ffn_up"][l]); W[f"gate{l}"] = wrelayout(inp["ffn_gate"][l])
        W[f"down{l}"] = wrelayout(inp["ffn_down"][l])
    vecs = []
    for l in range(2):
        for j in range(4):
            vecs.append((f"ng{l}_{j}", inp["norm_g"][l, j]))
        for j in range(3):
            vecs.append((f"cw{l}_{j}", inp["ffn_conv"][l, j]))
        vecs.append((f"cb{l}", inp["ffn_conv_b"][l]))
    vecs.append(("mu", inp["rw_mu"][0].reshape(-1)))
    for nm in ("w0", "a0", "k_k", "k_a", "lnx_g", "lnx_b", "r_k"):
        vecs.append((nm, inp["rw_" + nm][0].reshape(-1)))
    vecs.append(("gab", inp["gla_a_b"][0]))
    vecs.append(("grb", inp["gla_r_b"][0]))
    vecs.append(("gng", np.tile(np.asarray(inp["gla_norm_g"][0]), cfg.GH)))
    pc, off = pcol_pack([(n, pad128(v)) for n, v in vecs])
    W["pcols"] = pc
    return W, off


def make_consts():
    c = np.zeros((128, 640), np.float32)
    c[:, 0:128] = np.eye(128)
    for p in range(128):
        for q in range(128):
            if p // 64 == q // 64:
                c[p, 128 + q] = 1.0
    s_ = np.arange(128) % 64
    t_ = np.arange(64)
    strictU = (t_[None, :] > s_[:, None]).astype(np.float32)
    inclU = (t_[None, :] >= s_[:, None]).astype(np.float32)
    c[:, 256:320] = strictU
    c[:, 320:384] = inclU
    c[:, 384:448] = (s_[:, None] > t_[None, :]).astype(np.float32)
    c[:, 448:512] = (s_[:, None] == t_[None, :]).astype(np.float32)
    c[:, 512:576] = inclU
    return c


def make_xT(cfg, inp, b):
    h = np.concatenate([np.asarray(inp["meta"], np.float32), np.asarray(inp["x"][b], np.float32)], axis=0)
    return np.ascontiguousarray(h.T)


def build(cfg, off, ncols, wshapes, upto=99, dbg=None):
    P = Prog(cfg)
    P.pcol_off = off
    D, T, F, DC, FC = cfg.D, cfg.T, cfg.F, cfg.DC, cfg.FC
    xT = P.dram("xT", [D, T], F32, "ExternalInput")
    pcd = P.dram("pcols", [128, ncols], F32, "ExternalInput")
    wd = {n: P.dram(n, list(s), F32, "ExternalInput") for n, s in wshapes.items() if n != "pcols"}
    outT = P.dram("outT", [D, T], F32, "ExternalOutput")
    P.pcols = P.sb("pcols_sb", [128, ncols + 64], F32)
    cst = P.dram("consts", [128, 640], F32, "ExternalInput")
    P.cst = P.sb("consts_sb", [128, 640], F32)
    P.ident = P.cst[:, 0:128]
    P.b64 = P.cst[:, 128:256]
    P.mask12 = P.cst[:, 256:384]
    P.maskL = P.cst[:, 384:448]
    P.identst = P.cst[:, 448:512]
    P.maskG = P.cst[:, 512:576]
    off = dict(off)
    off["ngab"] = ncols
    off["omka"] = ncols + 16
    P.pcol_off = off
    P.ones = P.sb("ones_sb", [128, 128], F32)
    P.sq = [P.sb(f"sq{i}", [128, 449], F32) for i in range(3)]
    P.sqn = 0
    P.psum = [P.ps(f"bank{i}", [128, 512], F32) for i in range(8)]
    P.dma(P.pcols[:, 0:ncols], pcd, [], [("pcols",)])
    P.memset("dve", P.ones[:, :], 1.0, [("ones",)])
    P.dma(P.cst[:, :], cst, [], [("cst",)])
    setup_consts(P)
    P.identbf = P.sb("identbf", [128, 64], BF16)
    P.copy("dve", P.identbf[:, :], P.identst[:, :], [("cst",)], [("identbf",)])
    P.identbf2 = P.sb("identbf2", [128, 128], BF16)
    P.copy("dve", P.identbf2[:, :], P.ident[:, :], [("cst",)], [("identbf2",)])
    P.R.barrier()
    S = {}

    def scratch(name, rows, dt):
        S[name] = P.dram("s_" + name, [rows, T], dt)
        return (name, S[name])

    def job(wname, out, odt=F32, func=AF.Copy, bias=None, scale=None):
        return dict(w=wd[wname], wname=wname, MC=wshapes[wname][0], out=out[1], oname=out[0], odt=odt, func=func,
                    bias=bias, scale=scale)

    hT = scratch("hT", D, F32)
    stage = [0]

    def done():
        stage[0] += 1
        return stage[0] >= upto

    def finish():
        toks = []
        if dbg is not None:
            name, rows, dt = dbg
            dd = P.dram("dbg", [rows, T], dt, "ExternalOutput")
            nch = (rows + 127) // 128
            for c in range(nch):
                r0, r1 = c * 128, min(rows, (c + 1) * 128)
                toks.append(P.dma(dd[r0:r1, :], S[name][r0:r1, :], [(name, c)], [("dbg", c)]))
        hv = S["hT"]
        for c in range(DC):
            toks.append(P.dma(outT[c * 128:(c + 1) * 128, :], hv[c * 128:(c + 1) * 128, :], [("hT", c)], [("outT", c)]))
        P.R.final_wait("sp", toks)
        P.R.emit()
        return P.nc

    Xs = [scratch(f"X{i}", D, BF16) for i in range(6)]
    phase_norm(P, "xT", xT, None, None, "ng0_0", "mix6", Xs, mu="mu", store_h=False)
    for c in range(DC):
        P.dma(S["hT"][c * 128:(c + 1) * 128, :], xT[c * 128:(c + 1) * 128, :], [], [("hT", c)])
    if done():
        return finish()
    rT, kT, vT = scratch("rT", D, F32), scratch("kT", D, F32), scratch("vT", D, F32)
    phase_gemm(P, [job("wr", rT)], Xs[0], DC)
    if done():
        return finish()
    phase_gemm(P, [job("wk", kT)], Xs[2], DC)
    phase_gemm(P, [job("wv", vT)], Xs[3], DC)
    h1w = scratch("h1w", 128 * wshapes["w1"][0], BF16)
    h1a = scratch("h1a", 128 * wshapes["a1"][0], BF16)
    h1g = scratch("h1g", 128 * wshapes["g1"][0], BF16)
    phase_gemm(P, [job("w1", h1w, BF16, AF.Tanh)], Xs[1], DC)
    phase_gemm(P, [job("a1", h1a, BF16)], Xs[4], DC)
    phase_gemm(P, [job("g1", h1g, BF16, AF.Sigmoid)], Xs[5], DC)
    swT, aT, gT = scratch("swT", D, F32), scratch("aT", D, F32), scratch("gT", D, F32)
    phase_gemm(P, [job("w2", swT, F32, AF.Sigmoid, bias="w0")], h1w, wshapes["w1"][0])
    phase_gemm(P, [job("a2", aT, F32, AF.Sigmoid, bias="a0")], h1a, wshapes["a1"][0])
    phase_gemm(P, [job("g2", gT)], h1g, wshapes["g1"][0])
    if done():
        return finish()
    Xo = scratch("Xo", D, BF16)
    phase_rwkv(P, rT, kT, vT, swT, aT, gT, Xo)
    if done():
        return finish()
    mixT = scratch("mixT", D, F32)
    phase_gemm(P, [job("wo", mixT)], Xo, DC)
    if done():
        return finish()
    for l in range(2):
        if l == 1:
            Xg = scratch("Xg", D, BF16)
            phase_norm(P, "hT", S["hT"], ("fT", S["fT"]), "ng0_3", "ng1_0", "plain", [Xg])
            if done():
                return finish()
            projT = scratch("projT", cfg.GIN, F32)
            phase_gemm(P, [job("gin", projT)], Xg, DC)
            h1ga = scratch("h1ga", 128 * wshapes["ga1"][0], BF16)
            phase_gemm(P, [job("ga1", h1ga, BF16)], Xg, DC)
            gaT = scratch("gaT", cfg.DK, F32)
            phase_gemm(P, [job("ga2", gaT, F32, AF.Exp, bias="ngab", scale=-1.0)], h1ga, wshapes["ga1"][0])
            if done():
                return finish()
            Xgo = scratch("Xgo", D, BF16)
            phase_gla(P, projT, gaT, Xgo)
            if done():
                return finish()
            phase_gemm(P, [job("gwo", mixT)], Xgo, DC)
            if done():
                return finish()
        Xf = scratch(f"Xf{l}", D, BF16)
        phase_norm(P, "hT", S["hT"], mixT, f"ng{l}_1", f"ng{l}_2", "plain", [Xf])
        if done():
            return finish()
        uT, zT = scratch(f"uT{l}", F, F32), scratch(f"zT{l}", F, F32)
        phase_gemm(P, [job(f"up{l}", uT), job(f"gate{l}", zT)], Xf, DC)
        if done():
            return finish()
        Xd = scratch(f"Xd{l}", F, BF16)
        phase_ffnact(P, l, uT, zT, Xd)
        if done():
            return finish()
        if l == 0:
            fT = scratch("fT", D, F32)
        blocks = cfg.down_blocks
        phase_gemm(P, [job(f"down{l}", fT)], Xd, FC, tblocks=blocks)
        if done():
            return finish()
    phase_norm(P, "hT", S["hT"], ("fT", S["fT"]), "ng1_3", None, "none", [])
    return finish()


NCORES = 4


def kernel(**inputs):
    import os
    inp = {k: np.asarray(v) for k, v in inputs.items()}
    cfg = Cfg()
    B = inp["x"].shape[0]
    W, off = host_prep(cfg, inp)
    wshapes = {k: v.shape for k, v in W.items()}
    upto = int(os.environ.get("K_UPTO", "99"))
    nc = build(cfg, off, W["pcols"].shape[1], wshapes, upto=upto)
    W["consts"] = make_consts()
    in_maps = []
    for c in range(NCORES):
        m = dict(W)
        m["xT"] = make_xT(cfg, inp, c % B)
        in_maps.append(m)
    res = run_bass_kernel_spmd(nc, in_maps, core_ids=list(range(NCORES)))
    out = np.stack([np.ascontiguousarray(res.results[b]["outT"].T[cfg.NMETA:]) for b in range(B)], axis=0)
    return out.astype(np.float32)
```

```python
import os
import numpy as np
import concourse.bass as bass
import concourse.mybir as mybir
from concourse.bass_utils import run_bass_kernel_spmd

F32 = mybir.dt.float32
BF16 = mybir.dt.bfloat16
AF = mybir.ActivationFunctionType
ALU = mybir.AluOpType
EPOCH = 30000
NDS = 24


class Cfg:
    def __init__(self, D=4096, SEQ=2048, NMETA=16, F=11008, LW=128, LA=128, LG=480,
                 GH=8, GLORA=16):
        self.D, self.SEQ, self.NMETA, self.F = D, SEQ, NMETA, F
        self.T = SEQ + NMETA
        self.DC = D // 128
        self.FC = F // 128
        self.LW, self.LA, self.LG = LW, LA, LG
        self.LGP = ((LG + 127) // 128) * 128
        self.GH = GH
        self.DK = D // 2
        self.DV = D
        self.HK = self.DK // GH
        self.HV = self.DV // GH
        self.GLORA = GLORA
        self.GIN = 2 * self.DK + 2 * self.DV
        self.tiles = []
        t = 0
        while t < self.T:
            n = min(448, self.T - t)
            self.tiles.append((t, n))
            t += n
        nt = 6 if self.T % 6 == 0 and self.T // 6 <= 448 else 0
        if nt:
            w = self.T // 6
            tl6 = [(i * w, w) for i in range(6)]
            self.down_blocks = [tl6[0:2], tl6[2:4], tl6[4:6]]
        else:
            self.down_blocks = [[t_] for t_ in self.tiles]
        self.chunks = []
        t = 0
        while t < self.T:
            n = min(64, self.T - t)
            self.chunks.append((t, n))
            t += n


class Rec:
    ENG = ("pe", "act", "dve", "pool", "sp")

    def __init__(self, nc):
        self.nc = nc
        self.ops = {e: [] for e in self.ENG}
        self.seq = {e: 0 for e in self.ENG + ("peA", "peB")}
        self.csem = {e: {} for e in self.ENG + ("peA", "peB")}
        self.dsem = {}
        self.dcount = {}
        self.dnext = {e: 0 for e in self.ENG}
        self.res = {}
        self.waited = {e: {} for e in self.ENG}
        self._sems = []
        self._cms = []
        import os
        self.dump = [] if os.environ.get("REC_DUMP") else None

    def _newsem(self, name):
        cm = self.nc.semaphore(name)
        s = cm.__enter__()
        self._cms.append(cm)
        return s

    def _csem(self, eng, epoch):
        d = self.csem[eng]
        if epoch not in d:
            d[epoch] = self._newsem(f"c_{eng}_{epoch}")
        return d[epoch]

    def _need(self, eng, tok, waits):
        if tok is None:
            return
        if tok[0] == "c":
            _, e2, n = tok
            if e2.startswith("pe") and eng == "pe":
                return
            if self.waited[eng].get(("c", e2), 0) >= n:
                return
            self.waited[eng][("c", e2)] = n
            ep, loc = divmod(n - 1, EPOCH)
            waits.append((self._csem(e2, ep), loc + 1))
        else:
            _, q, idx, cnt = tok
            key = ("d", q, idx)
            if self.waited[eng].get(key, 0) >= cnt:
                return
            self.waited[eng][key] = cnt
            waits.append((self.dsem[(q, idx)], cnt))

    @staticmethod
    def _flat(ks):
        out = []
        for k in ks:
            if isinstance(k, list):
                out.extend(k)
            else:
                out.append(k)
        return out

    def op(self, eng, fn, reads=(), writes=(), dma=False, lane=None):
        veng = eng + lane if lane else eng
        reads = self._flat(reads)
        writes = self._flat(writes)
        psr = [k for k in reads if k[0] == "ps"]
        if psr:
            reads = [k for k in reads if k[0] != "ps"]
            writes = list(writes) + psr
        waits = []
        for r in reads:
            st = self.res.get(r)
            if st is not None:
                self._need(eng, st["w"], waits)
        for w in writes:
            st = self.res.get(w)
            if st is not None:
                self._need(eng, st["w"], waits)
                for tk in st["r"]:
                    self._need(eng, tk, waits)
        if dma:
            idx = self.dnext[eng] % NDS
            self.dnext[eng] += 1
            key = (eng, idx)
            if key not in self.dsem:
                self.dsem[key] = self._newsem(f"d_{eng}_{idx}")
                self.dcount[key] = 0
            prev = self.dcount[key]
            if prev > 0:
                self._need(eng, ("d", eng, idx, prev), waits)
            self.dcount[key] = prev + 16
            tok = ("d", eng, idx, prev + 16)
            inc = (self.dsem[key], 16)
        else:
            self.seq[veng] += 1
            n = self.seq[veng]
            ep, loc = divmod(n - 1, EPOCH)
            tok = ("c", veng, n)
            inc = (self._csem(veng, ep), 1)
        for r in reads:
            st = self.res.setdefault(r, {"w": None, "r": []})
            st["r"] = [t for t in st["r"] if not (t[0] == "c" and tok[0] == "c" and t[1] == tok[1])]
            st["r"].append(tok)
        for w in writes:
            self.res[w] = {"w": tok, "r": []}
        self.ops[eng].append((waits, fn, inc))
        if self.dump is not None:
            self.dump.append((eng, tok, [(getattr(sm, "name", str(sm)), v) for sm, v in waits], writes[:2], reads[:3]))
        return tok

    def barrier(self):
        toks = []
        for e in self.seq:
            if self.seq[e] > 0:
                toks.append(("c", e, self.seq[e]))
        for (q, idx), cnt in self.dcount.items():
            if cnt > 0:
                toks.append(("d", q, idx, cnt))
        for e in self.ENG:
            waits = []
            for t in toks:
                if t[0] == "c" and (t[1] == e or (e == "pe" and t[1].startswith("pe"))):
                    continue
                self._need(e, t, waits)
            if waits:
                self.ops[e].append((waits, None, None))
        self.res = {}

    def final_wait(self, eng, toks):
        waits = []
        for t in toks:
            self._need(eng, t, waits)
        self.ops[eng].append((waits, None, None))

    def emit(self):
        if self.dump is not None:
            for d in self.dump[-int(os.environ["REC_DUMP"]):]:
                print("OP", d)
        nc = self.nc
        ops = self.ops

        def run(engh, lst):
            for waits, fn, inc in lst:
                for s, v in waits:
                    engh.wait_ge(s, v)
                if fn is not None:
                    ins = fn(engh)
                    ins.then_inc(inc[0], inc[1])

        with nc.Block() as block:
            @block.tensor
            def _(e):
                run(e, ops["pe"])

            @block.scalar
            def _(e):
                run(e, ops["act"])

            @block.vector
            def _(e):
                run(e, ops["dve"])

            @block.gpsimd
            def _(e):
                run(e, ops["pool"])

            @block.sync
            def _(e):
                run(e, ops["sp"])
        for cm in reversed(self._cms):
            cm.__exit__(None, None, None)


class Prog:
    def __init__(self, cfg, only=None):
        self.cfg = cfg
        self.nc = bass.Bass("TRN2", target_bir_lowering=False)
        self.R = Rec(self.nc)
        self._cms = []
        self.psn = 0
        self.uid = 0
        self.pcol_off = {}
        self.dq = 0

    def sb(self, name, shape, dt=F32):
        self.nalloc = getattr(self, "nalloc", 0) + 1
        cm = self.nc.sbuf_tensor(f"{name}_{self.nalloc}", shape, dt)
        t = cm.__enter__()
        self._cms.append(cm)
        return t

    def ps(self, name, shape, dt=F32):
        cm = self.nc.psum_tensor(name, shape, dt)
        t = cm.__enter__()
        self._cms.append(cm)
        return t

    def free_to(self, n):
        self.R.barrier()
        while len(self._cms) > n:
            self._cms.pop().__exit__(None, None, None)

    def dram(self, name, shape, dt, kind=None):
        if kind is None:
            return self.nc.dram_tensor(name, list(shape), dt).ap()
        return self.nc.dram_tensor(name, list(shape), dt, kind=kind).ap()

    def dmaq(self):
        self.dq += 1
        return "sp"

    def dma(self, out, in_, reads, writes, q="sp"):
        return self.R.op(q, lambda e, o=out, i=in_: e.dma_start(out=o, in_=i), reads, writes, dma=True)

    def act(self, out, in_, func, reads, writes, bias=None, scale=None):
        kw = {}
        if bias is not None:
            kw["bias"] = bias
        if scale is not None:
            kw["scale"] = scale
        return self.R.op("act", lambda e, o=out, i=in_, f=func, k=kw: e.activation(out=o, in_=i, func=f, **k),
                         reads, writes)

    def tt(self, eng, out, in0, in1, op, reads, writes):
        return self.R.op(eng, lambda e, o=out, a=in0, b=in1, p=op: e.tensor_tensor(out=o, in0=a, in1=b, op=p),
                         reads, writes)

    def ts(self, eng, out, in0, s1, s2, op0, op1, reads, writes):
        if s2 is None:
            s2, op1 = 0.0, ALU.add
        return self.R.op(eng, lambda e, o=out, a=in0, x=s1, y=s2, p=op0, q=op1: e.tensor_scalar(o, a, x, y, p, q),
                         reads, writes)

    def stt(self, eng, out, in0, scalar, in1, op0, op1, reads, writes):
        eng = "dve"
        return self.R.op(eng, lambda e, o=out, a=in0, s=scalar, b=in1, p=op0, q=op1:
                         e.scalar_tensor_tensor(out=o, in0=a, scalar=s, in1=b, op0=p, op1=q), reads, writes)

    def copy(self, eng, out, in_, reads, writes):
        if eng == "act":
            return self.act(out, in_, AF.Copy, reads, writes)
        return self.R.op(eng, lambda e, o=out, i=in_: e.tensor_copy(out=o, in_=i), reads, writes)

    def memset(self, eng, ap, val, writes):
        return self.R.op(eng, lambda e, a=ap, v=val: e.memset(a, v), (), writes)

    def mm(self, out, lhsT, rhs, start, stop, reads, writes, lane=None):
        return self.R.op("pe", lambda e, o=out, l=lhsT, r=rhs, s=start, t=stop:
                         e.matmul(o, l, r, start=s, stop=t), reads, writes, lane=lane)

    def transpose(self, out, in_, ident, reads, writes):
        return self.R.op("pe", lambda e, o=out, i=in_, d=ident: e.transpose(o, i, d), reads, writes)


def pcol_pack(vecs):
    cols = []
    off = {}
    c = 0
    for name, v in vecs:
        v = np.asarray(v, np.float32).reshape(-1)
        n = v.size // 128
        assert n * 128 == v.size, name
        off[name] = c
        cols.append(v.reshape(n, 128).T)
        c += n
    return np.ascontiguousarray(np.concatenate(cols, axis=1)), off


def keys(name, n):
    return [(name, i) for i in range(n)]


def phase_gemm(P, jobs, xT, KC, tblocks=None):
    cfg, R = P.cfg, P.R
    T = cfg.T
    mark = len(P._cms)
    xname, xd = xT
    if tblocks is None:
        tblocks = [cfg.tiles]
    maxtb = max(sum(n for _, n in blk) for blk in tblocks)
    X = P.sb("gX", [128, KC, maxtb], BF16)
    PCH = min(KC, 8)
    npieces = (KC + PCH - 1) // PCH
    NW = 4
    wst = [P.sb(f"gwst{i}", [128, PCH * 128], F32) for i in range(NW)]
    wb = [P.sb(f"gwb{i}", [128, KC * 128], BF16) for i in range(2)]
    P.uid += 1
    u = P.uid
    wcnt = 0
    pcnt = 0
    ocnt = 0
    orows = {}
    for bi, blk in enumerate(tblocks):
        b0 = blk[0][0]
        bl = sum(n for _, n in blk)
        xv = xd.rearrange("(kc p) t -> p kc t", p=128)
        for kc in range(KC):
            P.dma(X[:, kc, 0:bl], xv[:, kc, b0:b0 + bl], [(xname, kc)], [("gX", u)])
        for job in jobs:
            odt = job["odt"]
            key = ("orow", odt)
            if key not in orows:
                orows[key] = [P.sb(f"gorow{len(orows)}_{i}", [128, maxtb], odt) for i in range(2)]
            orow = orows[key]
            wd = job["w"]
            for m in range(job["MC"]):
                wbi = wcnt % 2
                wcnt += 1
                for pc in range(npieces):
                    k0 = pc * PCH
                    kn = min(PCH, KC - k0)
                    si = pcnt % NW
                    pcnt += 1
                    P.dma(wst[si][:, 0:kn * 128], wd[m, :, k0 * 128:(k0 + kn) * 128], [(job["wname"], m)],
                          [("gwst", u, si)])
                    ceng = "dve" if (pcnt % 2 == 0) else "pool"
                    P.copy(ceng, wb[wbi][:, k0 * 128:(k0 + kn) * 128], wst[si][:, 0:kn * 128],
                           [("gwst", u, si)], [("gwb", u, wbi, pc)])
                oi = ocnt % 2
                ocnt += 1
                off = 0
                for (t0, n) in blk:
                    bank = P.psn % 8
                    P.psn += 1
                    pst = P.psum[bank]
                    for kc in range(KC):
                        P.mm(pst[:, 0:n], wb[wbi][:, kc * 128:(kc + 1) * 128], X[:, kc, off:off + n],
                             kc == 0, kc == KC - 1,
                             [("gwb", u, wbi, kc // PCH), ("gX", u)], [("ps", bank)])
                    bias = None
                    if job.get("bias") is not None:
                        c = P.pcol_off[job["bias"]] + m
                        bias = P.pcols[:, c:c + 1]
                    P.act(orow[oi][:, off:off + n], pst[:, 0:n], job["func"], [("ps", bank)],
                          [("gorow", u, odt, oi)], bias=bias, scale=job.get("scale"))
                    off += n
                P.dma(job["out"][m * 128:(m + 1) * 128, b0:b0 + bl], orow[oi][:, 0:bl],
                      [("gorow", u, odt, oi)], [(job["oname"], m)], q="act")
    P.free_to(mark)


def norm_stats(P, src, n, DC, u, srckey, rstd, rkey):
    bank = P.psn % 8
    P.psn += 1
    pst = P.psum[bank]
    for kc in range(DC):
        si = P.sqn % 3
        P.sqn += 1
        P.act(P.sq[si][:, 0:n], src[:, kc, 0:n], AF.Square, [srckey[kc]], [("sq", si)])
        P.mm(pst[:, 0:n], P.ones[:, :], P.sq[si][:, 0:n], kc == 0, kc == DC - 1, [("sq", si)], [("ps", bank)])
    P.ts("dve", rstd[:, 0:n], pst[:, 0:n], 1.0 / (DC * 128), 1e-6, ALU.mult, ALU.add, [("ps", bank)], [rkey])
    P.act(rstd[:, 0:n], rstd[:, 0:n], AF.Sqrt, [rkey], [rkey])
    P.R.op("dve", lambda e, o=rstd[:, 0:n]: e.reciprocal(out=o, in_=o), [rkey], [rkey])


def phase_norm(P, hname, hd, src, g_add, g_out, mode, outs, mu=None, store_h=True):
    cfg = P.cfg
    DC = cfg.DC
    mark = len(P._cms)
    P.uid += 1
    u = P.uid
    halo = 1 if mode == "mix6" else 0
    W = 448 + halo
    Hh = P.sb("nH", [128, DC, W], F32)
    S = P.sb("nS", [128, DC, W], F32)
    O = [P.sb(f"nO{i}", [128, DC, 448], BF16) for i in range(2)] if mode != "none" else []
    rstd = [P.sb(f"nr{i}", [128, W], F32) for i in range(2)]
    ocnt = 0
    for ti, (t0, n) in enumerate(cfg.tiles):
        h0 = halo if ti > 0 else 0
        nn = n + halo
        c0 = halo - h0
        hv = hd.rearrange("(kc p) t -> p kc t", p=128)
        kH = [("nH", u, k) for k in range(DC)]
        kS = [("nS", u, k) for k in range(DC)]
        P.dma(Hh[:, :, c0:nn], hv[:, :, t0 - h0:t0 + n], keys(hname, DC), kH)
        if halo and ti == 0:
            P.memset("pool", Hh[:, :, 0:1], 0.0, kH)
        if src is not None:
            sname, sd = src
            sv = sd.rearrange("(kc p) t -> p kc t", p=128)
            P.dma(S[:, :, c0:nn], sv[:, :, t0 - h0:t0 + n], keys(sname, DC), kS)
            if halo and ti == 0:
                P.memset("pool", S[:, :, 0:1], 0.0, kS)
            norm_stats(P, S, nn, DC, u, kS, rstd[0], ("nr", u, 0))
            ga = P.pcol_off[g_add]
            for kc in range(DC):
                P.stt("dve", S[:, kc, 0:nn], S[:, kc, 0:nn], P.pcols[:, ga + kc:ga + kc + 1], rstd[0][:, 0:nn],
                      ALU.mult, ALU.mult, [kS[kc], ("nr", u, 0)], [kS[kc]])
                P.tt("pool", Hh[:, kc, 0:nn], Hh[:, kc, 0:nn], S[:, kc, 0:nn], ALU.add, [kS[kc], kH[kc]], [kH[kc]])
        if store_h:
            P.dma(hv[:, :, t0:t0 + n], Hh[:, :, halo:nn], kH, keys(hname, DC))
        if mode == "none":
            continue
        norm_stats(P, Hh, nn, DC, u, kH, rstd[1], ("nr", u, 1))
        go = P.pcol_off[g_out]
        if mode == "plain":
            oi = ocnt % 2
            ocnt += 1
            for kc in range(DC):
                eng = "dve" if kc % 2 == 0 else "pool"
                P.stt(eng, O[oi][:, kc, 0:n], Hh[:, kc, 0:n], P.pcols[:, go + kc:go + kc + 1], rstd[1][:, 0:n],
                      ALU.mult, ALU.mult, [kH[kc], ("nr", u, 1)], [("nO", u, oi, kc)])
            oname, od = outs[0]
            ov = od.rearrange("(kc p) t -> p kc t", p=128)
            P.dma(ov[:, :, t0:t0 + n], O[oi][:, :, 0:n], [("nO", u, oi, k) for k in range(DC)], keys(oname, DC))
        else:
            for kc in range(DC):
                P.stt("dve", S[:, kc, 0:nn], Hh[:, kc, 0:nn], P.pcols[:, go + kc:go + kc + 1], rstd[1][:, 0:nn],
                      ALU.mult, ALU.mult, [kH[kc], ("nr", u, 1)], [kS[kc]])
            for kc in range(DC):
                P.tt("pool", Hh[:, kc, 0:n], S[:, kc, 0:n], S[:, kc, 1:nn], ALU.subtract, [kS[kc], kH[kc]], [kH[kc]])
            mo = P.pcol_off[mu]
            for i in range(6):
                oi = ocnt % 2
                ocnt += 1
                for kc in range(DC):
                    eng = "dve" if kc % 2 == 0 else "pool"
                    c = mo + i * DC + kc
                    P.stt(eng, O[oi][:, kc, 0:n], Hh[:, kc, 0:n], P.pcols[:, c:c + 1], S[:, kc, 1:nn],
                          ALU.mult, ALU.add, [kH[kc], kS[kc]], [("nO", u, oi, kc)])
                oname, od = outs[i]
                ov = od.rearrange("(kc p) t -> p kc t", p=128)
                P.dma(ov[:, :, t0:t0 + n], O[oi][:, :, 0:n], [("nO", u, oi, k) for k in range(DC)], keys(oname, DC))
    P.free_to(mark)


def setup_consts(P):
    cfg = P.cfg
    n = cfg.DK // 128
    a = P.pcol_off["gab"]
    b = P.pcol_off["ngab"]
    P.ts("dve", P.pcols[:, b:b + n], P.pcols[:, a:a + n], -1.0, None, ALU.mult, None, [("pcols",)], [("pcols2",)])
    a = P.pcol_off["k_a"]
    b = P.pcol_off["omka"]
    n = cfg.DC
    P.ts("dve", P.pcols[:, b:b + n], P.pcols[:, a:a + n], -1.0, 1.0, ALU.mult, ALU.add, [("pcols",)], [("pcols3",)])


def phase_ffnact(P, l, uT, zT, Xd):
    cfg = P.cfg
    T, FC = cfg.T, cfg.FC
    mark = len(P._cms)
    P.uid += 1
    u = P.uid
    zrow = [P.sb(f"fz{i}", [128, T + 2], F32) for i in range(2)]
    urow = [P.sb(f"fu{i}", [128, T], F32) for i in range(2)]
    acc = [P.sb(f"fa{i}", [128, T], F32) for i in range(2)]
    orow = [P.sb(f"fo{i}", [128, T], BF16) for i in range(2)]
    for i in range(2):
        P.memset("pool", zrow[i][:, 0:2], 0.0, [("fz", u, i)])
    c0, c1, c2, cb = (P.pcol_off[f"cw{l}_0"], P.pcol_off[f"cw{l}_1"], P.pcol_off[f"cw{l}_2"], P.pcol_off[f"cb{l}"])
    pc = P.pcols
    for fc in range(FC):
        i = fc % 2
        e1 = "dve" if i == 0 else "pool"
        e2 = "pool" if i == 0 else "dve"
        P.dma(zrow[i][:, 2:T + 2], zT[1][fc * 128:(fc + 1) * 128, :], [(zT[0], fc)], [("fz", u, i)])
        P.dma(urow[i][:, :], uT[1][fc * 128:(fc + 1) * 128, :], [(uT[0], fc)], [("fu", u, i)])
        P.ts(e1, acc[i][:, :], zrow[i][:, 2:T + 2], pc[:, c2 + fc:c2 + fc + 1], pc[:, cb + fc:cb + fc + 1],
             ALU.mult, ALU.add, [("fz", u, i)], [("fa", u, i)])
        P.stt(e1, acc[i][:, :], zrow[i][:, 1:T + 1], pc[:, c1 + fc:c1 + fc + 1], acc[i][:, :], ALU.mult, ALU.add,
              [("fz", u, i), ("fa", u, i)], [("fa", u, i)])
        P.stt(e1, acc[i][:, :], zrow[i][:, 0:T], pc[:, c0 + fc:c0 + fc + 1], acc[i][:, :], ALU.mult, ALU.add,
              [("fz", u, i), ("fa", u, i)], [("fa", u, i)])
        P.act(acc[i][:, :], acc[i][:, :], AF.Silu, [("fa", u, i)], [("fa", u, i)])
        P.tt(e2, orow[i][:, :], acc[i][:, :], urow[i][:, :], ALU.mult, [("fa", u, i), ("fu", u, i)], [("fo", u, i)])
        P.dma(Xd[1][fc * 128:(fc + 1) * 128, :], orow[i][:, :], [("fo", u, i)], [(Xd[0], fc)], q="act")
    P.free_to(mark)


def interleave(lists):
    if not lists:
        return
    n = max(len(l) for l in lists)
    for i in range(n):
        for l in lists:
            if i < len(l):
                l[i]()


class PsReg:
    def __init__(self, P):
        self.P = P
        self.n = 0

    def get(self):
        i = self.n % 4
        self.n += 1
        return [self.P.psum[i], self.P.psum[i + 4]], [("ps", i), ("ps", i + 4)]


C0 = float(np.exp(-0.5))


def phase_rwkv(P, rT, kT, vT, swT, aT, gT, Xo):
    cfg = P.cfg
    T = cfg.T
    NCH = len(cfg.chunks)
    TP = 64 + NCH * 64
    TW = NCH * 64
    mark = len(P._cms)
    P.uid += 1
    u = P.uid
    pr = PsReg(P)
    pc = P.pcols
    names = ["r", "k", "v", "a", "sw", "kk", "kh", "bon", "cum", "Ep", "Em", "Bh", "Kh", "y"]
    tl = {n: P.sb("rw_" + n, [128, TP], BF16 if n in ("Kh", "Bh") else F32) for n in names}
    tl["vb"] = P.sb("rw_vb", [128, TP], BF16)
    names = names + ["vb"]
    AR = P.sb("rw_AR", [128, NCH, 2, 64], BF16)
    onesc = P.sb("rw_ones", [128, 64], F32)
    obf = [P.sb(f"rw_obf{i}", [128, T], BF16) for i in range(2)]
    NS = 8
    SC1 = [P.sb(f"rw_sc1_{i}", [128, 128], BF16) for i in range(NS)]
    SC2 = [P.sb(f"rw_sc2_{i}", [128, 128], BF16) for i in range(NS)]
    TTs = [P.sb(f"rw_tt_{i}", [128, 128], BF16) for i in range(NS)]
    Vst = [P.sb(f"rw_vst_{i}", [128, 64], BF16) for i in range(NS)]
    Bst = [P.sb(f"rw_bst_{i}", [128, 64], BF16) for i in range(NS)]
    Kst = [P.sb(f"rw_kst_{i}", [128, 64], BF16) for i in range(NS)]
    Pbf = [P.sb(f"rw_pbf_{i}", [128, 64], BF16) for i in range(2)]
    Nb = [[P.sb(f"rw_n_{i}_{j}", [128, 128], BF16) for j in range(2)] for i in range(NS)]
    NTb = [[P.sb(f"rw_nt_{i}_{j}", [128, 128], BF16) for j in range(2)] for i in range(NS)]
    TTb = [[P.sb(f"rw_ttb_{i}_{j}", [128, 128], BF16) for j in range(2)] for i in range(NS)]
    for i in range(NS):
        for j in range(2):
            P.memset("pool", Nb[i][j][:, :], 0.0, [("nb", u, i, j)])
            P.memset("pool", NTb[i][j][:, :], 0.0, [("ntb", u, i, j)])
            P.memset("pool", TTb[i][j][:, :], 0.0, [("ttb", u, i, j)])
    Wsb = [P.sb(f"rw_w_{i}", [128, 64], BF16) for i in range(2)]
    Usb = [P.sb(f"rw_u_{i}", [128, 64], BF16) for i in range(2)]
    Pst = [P.sb(f"rw_p_{i}", [128, 64], F32) for i in range(2)]
    Ptmp = P.sb("rw_ptmp", [128, 64], F32)
    P.memset("dve", onesc[:, :], 1.0, [("rw_ones", u)])
    for n in names:
        P.memset("pool", tl[n][:, :], 0.0, [("rw", u, n)])
    K = lambda n: ("rw", u, n)
    HP = [slice(0, 64), slice(64, 128)]
    LN = ["A", "B"]
    pieces = [(c0, min(512, TW - c0)) for c0 in range(0, TW, 512)]

    def full(n):
        return tl[n][:, 64:64 + TW]

    def b64(src, fn):
        for (c0, cn) in pieces:
            bank = P.psn % 8
            P.psn += 1
            pst = P.psum[bank]
            P.mm(pst[:, 0:cn], P.b64[:, :], tl[src][:, 64 + c0:64 + c0 + cn], True, True, [K(src)], [("ps", bank)])
            fn(pst[:, 0:cn], c0, cn, ("ps", bank))

    for fc in range(cfg.DC):
        col = lambda nm: pc[:, P.pcol_off[nm] + fc:P.pcol_off[nm] + fc + 1]
        rows = slice(fc * 128, (fc + 1) * 128)
        for nm, src in (("r", rT), ("k", kT), ("v", vT), ("a", aT), ("sw", swT)):
            P.dma(tl[nm][:, 64:64 + T], src[1][rows, :], [(src[0], fc)], [K(nm)])
        P.ts("pool", full("kk"), full("k"), col("k_k"), None, ALU.mult, None, [K("k")], [K("kk")])
        P.act(full("y"), full("kk"), AF.Square, [K("kk")], [K("y")])

        def f1(ps, c0, cn, key):
            sl = slice(64 + c0, 64 + c0 + cn)
            P.act(tl["cum"][:, sl], ps, AF.Sqrt, [key], [K("cum")])
        b64("y", f1)
        P.ts("dve", full("cum"), full("cum"), 1e-12, None, ALU.max, None, [K("cum")], [K("cum")])
        P.R.op("dve", lambda e, o=full("cum"): e.reciprocal(out=o, in_=o), [K("cum")], [K("cum")])
        P.tt("pool", full("kk"), full("kk"), full("cum"), ALU.mult, [K("kk"), K("cum")], [K("kk")])
        P.ts("dve", full("kh"), full("a"), col("k_a"), col("omka"), ALU.mult, ALU.add, [K("a")], [K("kh")])
        P.tt("pool", full("kh"), full("kh"), full("k"), ALU.mult, [K("kh"), K("k")], [K("kh")])
        P.stt("dve", full("y"), full("r"), col("r_k"), full("kh"), ALU.mult, ALU.mult, [K("r"), K("kh")], [K("y")])

        def f2(ps, c0, cn, key):
            sl = slice(64 + c0, 64 + c0 + cn)
            P.tt("dve", tl["bon"][:, sl], ps, tl["v"][:, sl], ALU.mult, [key, K("v")], [K("bon")])
        b64("y", f2)
        P.copy("pool", full("vb"), full("v"), [K("v")], [K("vb")])
        for (t0, n) in cfg.chunks:
            sl = slice(64 + t0, 64 + t0 + 64)
            P.R.op("dve", lambda e, o=tl["cum"][:, sl], d1=tl["sw"][:, sl]:
                   e.tensor_tensor_scan(out=o, data0=onesc[:, :], data1=d1, initial=0.0, op0=ALU.mult, op1=ALU.add),
                   [K("sw"), ("rw_ones", u)], [K("cum")])
        P.act(full("Ep"), full("cum"), AF.Exp, [K("cum")], [K("Ep")], scale=-C0)
        P.act(full("Em"), full("cum"), AF.Exp, [K("cum")], [K("Em")], scale=C0)
        P.tt("pool", full("cum"), full("cum"), full("sw"), ALU.subtract, [K("cum"), K("sw")], [K("cum")])
        P.act(full("cum"), full("cum"), AF.Exp, [K("cum")], [K("cum")], scale=-C0)
        arv = lambda j: AR[:, :, j, :]
        v3 = lambda n: tl[n][:, 64:64 + TW].rearrange("p (c t) -> p c t", t=64)
        P.stt("dve", arv(0), v3("kk"), -1.0, v3("cum"), ALU.mult, ALU.mult, [K("kk"), K("cum")], [K("AR")])
        P.tt("pool", arv(1), v3("r"), v3("Ep"), ALU.mult, [K("r"), K("Ep")], [K("AR")])
        P.tt("pool", full("Bh"), full("kk"), full("a"), ALU.mult, [K("kk"), K("a")], [K("Bh")])
        P.tt("dve", full("Bh"), full("Bh"), full("Em"), ALU.mult, [K("Bh"), K("Em")], [K("Bh")])
        P.tt("pool", full("Kh"), full("kh"), full("Em"), ALU.mult, [K("kh"), K("Em")], [K("Kh")])
        P.memset("pool", Pst[0][:, :], 0.0, [("rw_p", u, 0, 0), ("rw_p", u, 0, 1)])
        P.memset("pool", Pbf[0][:, :], 0.0, [("rw_pb", u, 0, 0), ("rw_pb", u, 0, 1)])

        def pre_stages(ci):
            s = ci % NS
            t0 = ci * 64
            ch = slice(64 + t0, 128 + t0)
            st = []
            EV = ["dve", "act"]

            def a():
                r, kr = pr.get()
                for h, hp in enumerate(HP):
                    P.mm(r[h][hp, 0:128], tl["Kh"][hp, ch], AR[hp, ci, :, :], True, True, [K("Kh"), K("AR")], [kr[h]], lane=LN[h])
                    P.mm(r[h][hp, 128:256], tl["Bh"][hp, ch], AR[hp, ci, :, :], True, True, [K("Bh"), K("AR")], [kr[h]], lane=LN[h])
                    P.mm(r[h][hp, 256:320], AR[hp, ci, 0, :], tl["Bh"][hp, ch], True, True, [K("Bh"), K("AR")], [kr[h]], lane=LN[h])
                for h, hp in enumerate(HP):
                    P.tt("dve", SC1[s][hp, :], r[h][hp, 0:128], P.mask12[hp, :], ALU.mult, [kr[h]], [("sc1", u, s, h)])
                    P.tt("dve", SC2[s][hp, :], r[h][hp, 128:256], P.mask12[hp, :], ALU.mult, [kr[h]], [("sc2", u, s, h)])
                    bc = slice(h * 64, h * 64 + 64)
                    P.tt("dve", Nb[s][0][hp, bc], r[h][hp, 256:320], P.maskL[hp, :], ALU.mult, [kr[h]], [("nb", u, s, 0)])
                    P.copy("pool", NTb[s][0][hp, bc], SC2[s][hp, 0:64], [("sc2", u, s, h)], [("ntb", u, s, 0)])
                    P.tt("pool", TTb[s][0][hp, bc], SC2[s][hp, 0:64], P.identst[hp, :], ALU.add, [("sc2", u, s, h)],
                         [("ttb", u, s, 0)])
            st.append(a)
            def lvl_stage(lvl):
                def f():
                    q, kq = pr.get()
                    do_b = lvl <= 5
                    do_c = lvl >= 2
                    if do_b:
                        i0, i1 = (lvl - 1) % 2, lvl % 2
                    if do_c:
                        k = lvl - 1
                        j0, j1 = (k - 1) % 2, k % 2
                        P.mm(q[0][:, 0:128], P.identbf2[:, :], TTb[s][j0][:, :], True, False, [("ttb", u, s, j0)], [kq[0]])
                        P.mm(q[0][:, 0:128], Nb[s][j1][:, :], TTb[s][j0][:, :], False, True,
                             [("nb", u, s, j1), ("ttb", u, s, j0)], [kq[0]])
                    if do_b:
                        P.mm(q[1][:, 0:128], NTb[s][i0][:, :], Nb[s][i0][:, :], True, True,
                             [("ntb", u, s, i0), ("nb", u, s, i0)], [kq[1]])
                        if lvl < 5:
                            P.mm(q[1][:, 128:256], Nb[s][i0][:, :], NTb[s][i0][:, :], True, True,
                                 [("ntb", u, s, i0), ("nb", u, s, i0)], [kq[1]])
                    if do_c:
                        if k == 5:
                            P.copy("dve", TTs[s][:, :], q[0][:, 0:128], [kq[0]], [("tts", u, s, 0), ("tts", u, s, 1)])
                        else:
                            P.copy("dve", TTb[s][j1][:, :], q[0][:, 0:128], [kq[0]], [("ttb", u, s, j1)])
                    if do_b:
                        P.copy("act", Nb[s][i1][:, :], q[1][:, 0:128], [kq[1]], [("nb", u, s, i1)])
                        if lvl < 5:
                            P.copy("act", NTb[s][i1][:, :], q[1][:, 128:256], [kq[1]], [("ntb", u, s, i1)])
                return f
            for lvl in range(1, 7):
                st.append(lvl_stage(lvl))

            def d():
                q, kq = pr.get()
                lst = (("vb", Vst, "vst"), ("Bh", Bst, "bst"), ("Kh", Kst, "kst"))
                for j, (nm, dstl, dkey) in enumerate(lst):
                    cs = slice(j * 64, j * 64 + 64)
                    P.mm(q[1][:, cs], tl[nm][64:128, 64 + t0 - 64:64 + t0 + 64], P.identbf2[64:128, 64:128], True, True,
                         [K(nm)], [kq[1]], lane="B")
                    P.mm(q[0][0:64, cs], tl[nm][0:64, ch], P.identbf2[0:64, 0:64], True, True, [K(nm)], [kq[0]], lane="A")
                for j, (nm, dstl, dkey) in enumerate(lst):
                    cs = slice(j * 64, j * 64 + 64)
                    P.copy("act", dstl[s][0:64, :], q[0][0:64, cs], [kq[0]], [(dkey, u, s, 0)])
                    P.copy("act", dstl[s][64:128, :], q[1][64:128, cs], [kq[1]], [(dkey, u, s, 1)])
            st.insert(1, d)
            return st

        def scan_stages(ci):
            s = ci % NS
            t0 = ci * 64
            ch = slice(64 + t0, 128 + t0)
            pi, po = ci % 2, (ci + 1) % 2
            wi = ci % 2
            st = []
            EV = ["act", "dve"]

            def a():
                q, kq = pr.get()
                for h, hp in enumerate(HP):
                    P.mm(q[h][hp, 0:64], AR[hp, ci, 0, :], Pbf[pi][hp, :], True, False, [K("AR"), ("rw_pb", u, pi, h)], [kq[h]], lane=LN[h])
                    P.mm(q[h][hp, 0:64], SC1[s][hp, 0:64], Vst[s][hp, :], False, True,
                         [("sc1", u, s, h), ("vst", u, s, h)], [kq[h]], lane=LN[h])
                for h, hp in enumerate(HP):
                    P.copy(EV[h], Wsb[wi][hp, :], q[h][hp, 0:64], [kq[h]], [("rw_w", u, wi, h)])
            st.append(a)

            def b():
                q, kq = pr.get()
                for h, hp in enumerate(HP):
                    P.mm(q[h][hp, 0:64], TTs[s][hp, h * 64:h * 64 + 64], Wsb[wi][hp, :], True, True,
                         [("tts", u, s, h), ("rw_w", u, wi, h)], [kq[h]], lane=LN[h])
                for h, hp in enumerate(HP):
                    P.copy(EV[h], Usb[wi][hp, :], q[h][hp, 0:64], [kq[h]], [("rw_u", u, wi, h)])
            st.append(b)

            def c():
                q, kq = pr.get()
                for h, hp in enumerate(HP):
                    P.mm(q[h][hp, 0:64], Bst[s][hp, :], Usb[wi][hp, :], True, False,
                         [("bst", u, s, h), ("rw_u", u, wi, h)], [kq[h]], lane=LN[h])
                    P.mm(q[h][hp, 0:64], Kst[s][hp, :], Vst[s][hp, :], False, True,
                         [("kst", u, s, h), ("vst", u, s, h)], [kq[h]], lane=LN[h])
                for h, hp in enumerate(HP):
                    P.mm(q[h][hp, 64:128], Pbf[pi][hp, :], AR[hp, ci, 1, :], True, False, [("rw_pb", u, pi, h), K("AR")], [kq[h]], lane=LN[h])
                    P.mm(q[h][hp, 64:128], Usb[wi][hp, :], SC2[s][hp, 64:128], False, False,
                         [("rw_u", u, wi, h), ("sc2", u, s, h)], [kq[h]], lane=LN[h])
                    P.mm(q[h][hp, 64:128], Vst[s][hp, :], SC1[s][hp, 64:128], False, True,
                         [("vst", u, s, h), ("sc1", u, s, h)], [kq[h]], lane=LN[h])
                for h, hp in enumerate(HP):
                    P.tt("dve", Ptmp[hp, :], q[h][hp, 0:64], Pst[pi][hp, :], ALU.add, [kq[h], ("rw_p", u, pi, h)],
                         [("rw_ptmp", u, h)])
                    P.act(Pst[po][hp, :], Ptmp[hp, :], AF.Copy, [("rw_ptmp", u, h), K("Ep")], [("rw_p", u, po, h)],
                          scale=tl["Ep"][hp, 64 + t0 + 63:64 + t0 + 64])
                    P.ts("pool", Pbf[po][hp, :], Ptmp[hp, :], tl["Ep"][hp, 64 + t0 + 63:64 + t0 + 64], None, ALU.mult, None,
                         [("rw_ptmp", u, h), K("Ep")], [("rw_pb", u, po, h)])
                for h, hp in enumerate(HP):
                    P.copy(EV[h], tl["y"][hp, ch], q[h][hp, 64:128], [kq[h]], [K("y")])
            st.append(c)
            return st

        import os
        RWDBG = int(os.environ.get("RWDBG", "9"))
        if RWDBG < 1:
            continue
        LA = 4
        NPRE = int(os.environ.get("RW_NPRE", "99"))
        NSCAN = int(os.environ.get("RW_NSCAN", "99"))
        for c2 in range(0, NCH + LA + 1, 2):
            lists = []
            sc = []
            for ci in (c2, c2 + 1):
                if ci < NCH:
                    lists.append(pre_stages(ci)[:NPRE])
                if 0 <= ci - LA < NCH:
                    sc += scan_stages(ci - LA)[:NSCAN]
            if sc:
                lists.append(sc)
            interleave(lists)
        if RWDBG < 3:
            continue
        P.dma(tl["sw"][:, 64:64 + T], gT[1][rows, :], [(gT[0], fc)], [K("sw")])

        def f3(ps, c0, cn, key):
            sl = slice(64 + c0, 64 + c0 + cn)
            P.stt("dve", tl["y"][:, sl], ps, -1.0 / 64, tl["y"][:, sl], ALU.mult, ALU.add, [key, K("y")], [K("y")])
        b64("y", f3)
        P.act(full("cum"), full("y"), AF.Square, [K("y")], [K("cum")])

        def f4(ps, c0, cn, key):
            sl = slice(64 + c0, 64 + c0 + cn)
            P.ts("dve", tl["Ep"][:, sl], ps, 1.0 / 64, 64e-5, ALU.mult, ALU.add, [key], [K("Ep")])
        b64("cum", f4)
        P.act(full("Ep"), full("Ep"), AF.Sqrt, [K("Ep")], [K("Ep")])
        P.R.op("dve", lambda e, o=full("Ep"): e.reciprocal(out=o, in_=o), [K("Ep")], [K("Ep")])
        P.tt("pool", full("y"), full("y"), full("Ep"), ALU.mult, [K("y"), K("Ep")], [K("y")])
        P.ts("dve", full("y"), full("y"), col("lnx_g"), col("lnx_b"), ALU.mult, ALU.add, [K("y")], [K("y")])
        P.tt("pool", full("y"), full("y"), full("bon"), ALU.add, [K("y"), K("bon")], [K("y")])
        oi = fc % 2
        P.tt("dve", obf[oi][:, :], tl["y"][:, 64:64 + T], tl["sw"][:, 64:64 + T], ALU.mult, [K("y"), K("sw")],
             [("rw_obf", u, oi)])
        P.dma(Xo[1][rows, :], obf[oi][:, :], [("rw_obf", u, oi)], [(Xo[0], fc)])
    P.free_to(mark)


def phase_gla(P, projT, gaT, Xgo):
    cfg = P.cfg
    T = cfg.T
    NCH = len(cfg.chunks)
    TW = NCH * 64
    KCH, VCH = cfg.HK // 128, cfg.HV // 128
    DK, DV, HV = cfg.DK, cfg.DV, cfg.HV
    mark = len(P._cms)
    P.uid += 1
    u = P.uid
    pc = P.pcols
    mk = lambda nm, n: [P.sb(f"gl_{nm}{i}", [128, TW], F32) for i in range(n)]
    q, k, ga, cum = mk("q", KCH), mk("k", KCH), mk("ga", KCH), mk("cum", KCH)
    v, o = mk("v", VCH), mk("o", VCH)
    onesc = P.sb("gl_ones", [128, 64], F32)
    obf = [P.sb(f"gl_obf{i}", [128, T], BF16) for i in range(2)]
    rsd = P.sb("gl_rstd", [128, TW], F32)
    NS = 4
    AT = [P.sb(f"gl_at{i}", [128, 64], F32) for i in range(NS)]
    ktm = [P.sb(f"gl_ktm{i}", [128, KCH * 128], F32) for i in range(NS)]
    vtm = [P.sb(f"gl_vtm{i}", [128, VCH * 128], F32) for i in range(NS)]
    Sst = P.sb("gl_S", [128, KCH, HV], F32)
    Stmp = P.sb("gl_Stmp", [128, HV], F32)
    P.memset("dve", onesc[:, :], 1.0, [("gl_ones", u)])
    K = lambda nm, i: ("gl", u, nm, i)
    for nm, lst in (("q", q), ("k", k), ("ga", ga), ("cum", cum), ("v", v), ("o", o)):
        for i, t_ in enumerate(lst):
            P.memset("pool", t_[:, :], 0.0, [K(nm, i)])
    for i in range(NS):
        P.memset("pool", AT[i][:, :], 0.0, [("gl_at", u, i)])
        P.memset("pool", ktm[i][:, :], 0.0, [("gl_ktm", u, i)])
        P.memset("pool", vtm[i][:, :], 0.0, [("gl_vtm", u, i)])
    pieces = [(c0, min(512, TW - c0)) for c0 in range(0, TW, 512)]
    bankn = [0]

    def bank():
        b = bankn[0] % 8
        bankn[0] += 1
        return P.psum[b], ("ps", b)

    scale = float(cfg.HK) ** -0.5
    for h in range(cfg.GH):
        for kc in range(KCH):
            r0 = h * cfg.HK + kc * 128
            P.dma(q[kc][:, 0:T], projT[1][r0:r0 + 128, :], [(projT[0], r0 // 128)], [K("q", kc)])
            P.dma(k[kc][:, 0:T], projT[1][DK + r0:DK + r0 + 128, :], [(projT[0], (DK + r0) // 128)], [K("k", kc)])
            P.dma(ga[kc][:, 0:T], gaT[1][r0:r0 + 128, :], [(gaT[0], r0 // 128)], [K("ga", kc)])
        for vc in range(VCH):
            r0 = 2 * DK + h * HV + vc * 128
            P.dma(v[vc][:, 0:T], projT[1][r0:r0 + 128, :], [(projT[0], r0 // 128)], [K("v", vc)])
        for kc in range(KCH):
            P.act(ga[kc][:, 0:T], ga[kc][:, 0:T], AF.Ln, [K("ga", kc)], [K("ga", kc)], bias=1.0)
            for ci in range(NCH):
                sl = slice(ci * 64, ci * 64 + 64)
                P.R.op("dve", lambda e, o_=cum[kc][:, sl], d1=ga[kc][:, sl]:
                       e.tensor_tensor_scan(out=o_, data0=onesc[:, :], data1=d1, initial=0.0, op0=ALU.mult, op1=ALU.add),
                       [K("ga", kc), ("gl_ones", u)], [K("cum", kc)])
            P.act(ga[kc][:, :], cum[kc][:, :], AF.Exp, [K("cum", kc)], [K("ga", kc)], scale=-1.0 / 16)
            P.act(cum[kc][:, :], cum[kc][:, :], AF.Exp, [K("cum", kc)], [K("cum", kc)], scale=1.0 / 16)
            P.stt("dve", q[kc][:, :], q[kc][:, :], scale, ga[kc][:, :], ALU.mult, ALU.mult,
                  [K("q", kc), K("ga", kc)], [K("q", kc)])
            P.tt("pool", k[kc][:, :], k[kc][:, :], cum[kc][:, :], ALU.mult, [K("k", kc), K("cum", kc)], [K("k", kc)])
        P.memset("pool", Sst[:, :, :], 0.0, [("gl_S", u, i) for i in range(KCH)])

        def pre_stages(ci):
            s = ci % NS
            ch = slice(ci * 64, ci * 64 + 64)
            st = []

            def a():
                pa, ka = bank()
                for kc in range(KCH):
                    P.mm(pa[0:64, 0:64], k[kc][:, ch], q[kc][:, ch], kc == 0, kc == KCH - 1,
                         [K("k", kc), K("q", kc)], [ka])
                P.tt("dve", AT[s][0:64, :], pa[0:64, 0:64], P.maskG[0:64, :], ALU.mult, [ka], [("gl_at", u, s)])
            st.append(a)

            def b():
                pb, kb = bank()
                for kc in range(KCH):
                    P.mm(pb[0:64, kc * 128:(kc + 1) * 128], k[kc][:, ch], P.ident[:, :], True, True, [K("k", kc)], [kb])
                P.copy("act", ktm[s][0:64, :], pb[0:64, 0:KCH * 128], [kb], [("gl_ktm", u, s)])
                pv, kv = bank()
                for vc in range(VCH):
                    P.mm(pv[0:64, vc * 128:(vc + 1) * 128], v[vc][:, ch], P.ident[:, :], True, True, [K("v", vc)], [kv])
                P.copy("act", vtm[s][0:64, :], pv[0:64, 0:VCH * 128], [kv], [("gl_vtm", u, s)])
            st.append(b)
            return st

        def scan_stages(ci):
            s = ci % NS
            ch = slice(ci * 64, ci * 64 + 64)
            st = []

            def a():
                for vc in range(VCH):
                    po, ko = bank()
                    P.mm(po[:, 0:64], vtm[s][:, vc * 128:(vc + 1) * 128], AT[s][:, :], True, False,
                         [("gl_vtm", u, s), ("gl_at", u, s)], [ko])
                    for kc in range(KCH):
                        P.mm(po[:, 0:64], Sst[:, kc, vc * 128:(vc + 1) * 128], q[kc][:, ch], False, kc == KCH - 1,
                             [("gl_S", u, kc), K("q", kc)], [ko])
                    P.copy("act" if vc % 2 == 0 else "dve", o[vc][:, ch], po[:, 0:64], [ko], [K("o", vc)])
            st.append(a)

            def b():
                for kc in range(KCH):
                    pss, kss = bank()
                    P.mm(pss[:, 0:HV], ktm[s][:, kc * 128:(kc + 1) * 128], vtm[s][:, :], True, True,
                         [("gl_ktm", u, s), ("gl_vtm", u, s)], [kss])
                    P.tt("dve", Stmp[:, :], pss[:, 0:HV], Sst[:, kc, :], ALU.add, [kss, ("gl_S", u, kc)], [("gl_Stmp", u)])
                    P.act(Sst[:, kc, :], Stmp[:, :], AF.Copy, [("gl_Stmp", u), K("ga", kc)], [("gl_S", u, kc)],
                          scale=ga[kc][:, ci * 64 + 63:ci * 64 + 64])
            st.append(b)
            return st

        LA = 2
        for ci in range(NCH + LA):
            lists = []
            if ci < NCH:
                lists.append(pre_stages(ci))
            if ci - LA >= 0:
                lists.append(scan_stages(ci - LA))
            interleave(lists)
        for vc in range(VCH):
            P.act(v[vc][:, :], o[vc][:, :], AF.Square, [K("o", vc)], [K("v", vc)])
        for (c0, cn) in pieces:
            pn, kn = bank()
            for vc in range(VCH):
                P.mm(pn[:, 0:cn], P.ones[:, :], v[vc][:, c0:c0 + cn], vc == 0, vc == VCH - 1, [K("v", vc)], [kn])
            P.ts("dve", rsd[:, c0:c0 + cn], pn[:, 0:cn], 1.0 / HV, 1e-6, ALU.mult, ALU.add, [kn], [("gl_rsd", u)])
        P.act(rsd[:, :], rsd[:, :], AF.Sqrt, [("gl_rsd", u)], [("gl_rsd", u)])
        P.R.op("dve", lambda e, o_=rsd[:, :]: e.reciprocal(out=o_, in_=o_), [("gl_rsd", u)], [("gl_rsd", u)])
        for vc in range(VCH):
            f0 = h * HV + vc * 128
            fc = f0 // 128
            r0 = 2 * DK + DV + f0
            P.dma(v[vc][:, 0:T], projT[1][r0:r0 + 128, :], [(projT[0], r0 // 128)], [K("v", vc)])
            cg = P.pcol_off["gng"] + fc
            cb = P.pcol_off["grb"] + fc
            P.stt("dve", o[vc][:, :], o[vc][:, :], pc[:, cg:cg + 1], rsd[:, :], ALU.mult, ALU.mult,
                  [K("o", vc), ("gl_rsd", u)], [K("o", vc)])
            P.act(v[vc][:, 0:T], v[vc][:, 0:T], AF.Silu, [K("v", vc)], [K("v", vc)], bias=pc[:, cb:cb + 1])
            oi = (h * VCH + vc) % 2
            P.tt("pool", obf[oi][:, :], o[vc][:, 0:T], v[vc][:, 0:T], ALU.mult, [K("o", vc), K("v", vc)],
                 [("gl_obf", u, oi)])
            P.dma(Xgo[1][f0:f0 + 128, :], obf[oi][:, :], [("gl_obf", u, oi)], [(Xgo[0], fc)])
    P.free_to(mark)


def wrelayout(W):
    K, N = W.shape
    KC, MC = (K + 127) // 128, (N + 127) // 128
    if K != KC * 128 or N != MC * 128:
        Wp = np.zeros((KC * 128, MC * 128), np.float32)
        Wp[:K, :N] = W
    else:
        Wp = np.asarray(W, np.float32)
    return np.ascontiguousarray(Wp.reshape(KC, 128, MC, 128).transpose(2, 1, 0, 3).reshape(MC, 128, KC * 128))


def pad128(v):
    v = np.asarray(v, np.float32).reshape(-1)
    n = ((v.size + 127) // 128) * 128
    if n != v.size:
        v = np.concatenate([v, np.zeros(n - v.size, np.float32)])
    return v


def host_prep(cfg, inp):
    W = {}
    W["wr"] = wrelayout(inp["rw_wr"][0]); W["wk"] = wrelayout(inp["rw_wk"][0]); W["wv"] = wrelayout(inp["rw_wv"][0])
    W["wo"] = wrelayout(inp["rw_wo"][0])
    W["w1"] = wrelayout(inp["rw_w1"][0]); W["w2"] = wrelayout(inp["rw_w2"][0])
    W["a1"] = wrelayout(inp["rw_a1"][0]); W["a2"] = wrelayout(inp["rw_a2"][0])
    W["g1"] = wrelayout(inp["rw_g1"][0]); W["g2"] = wrelayout(inp["rw_g2"][0])
    W["gin"] = wrelayout(inp["gla_w_in"][0]); W["ga1"] = wrelayout(inp["gla_a1"][0])
    W["ga2"] = wrelayout(inp["gla_a2"][0]); W["gwo"] = wrelayout(inp["gla_wo"][0])
    for l in range(2):
        W[f"up{l}"] = wrelayout(inp["ffn_up"][l]); W[f"gate{l}"] = wrelayout(inp["ffn_gate"][l])
        W[f"down{l}"] = wrelayout(inp["ffn_down"][l])
    vecs = []
    for l in range(2):
        for j in range(4):
            vecs.append((f"ng{l}_{j}", inp["norm_g"][l, j]))
        for j in range(3):
            vecs.append((f"cw{l}_{j}", inp["ffn_conv"][l, j]))
        vecs.append((f"cb{l}", inp["ffn_conv_b"][l]))
    vecs.append(("mu", inp["rw_mu"][0].reshape(-1)))
    for nm in ("w0", "a0", "k_k", "k_a", "lnx_g", "lnx_b", "r_k"):
        vecs.append((nm, inp["rw_" + nm][0].reshape(-1)))
    vecs.append(("gab", inp["gla_a_b"][0]))
    vecs.append(("grb", inp["gla_r_b"][0]))
    vecs.append(("gng", np.tile(np.asarray(inp["gla_norm_g"][0]), cfg.GH)))
    pc, off = pcol_pack([(n, pad128(v)) for n, v in vecs])
    W["pcols"] = pc
    return W, off


def make_consts():
    c = np.zeros((128, 640), np.float32)
    c[:, 0:128] = np.eye(128)
    for p in range(128):
        for q in range(128):
            if p // 64 == q // 64:
                c[p, 128 + q] = 1.0
    s_ = np.arange(128) % 64
    t_ = np.arange(64)
    strictU = (t_[None, :] > s_[:, None]).astype(np.float32)
    inclU = (t_[None, :] >= s_[:, None]).astype(np.float32)
    c[:, 256:320] = strictU
    c[:, 320:384] = inclU
    c[:, 384:448] = (s_[:, None] > t_[None, :]).astype(np.float32)
    c[:, 448:512] = (s_[:, None] == t_[None, :]).astype(np.float32)
    c[:, 512:576] = inclU
    return c


def make_xT(cfg, inp, b):
    h = np.concatenate([np.asarray(inp["meta"], np.float32), np.asarray(inp["x"][b], np.float32)], axis=0)
    return np.ascontiguousarray(h.T)


def build(cfg, off, ncols, wshapes, upto=99, dbg=None):
    P = Prog(cfg)
    P.pcol_off = off
    D, T, F, DC, FC = cfg.D, cfg.T, cfg.F, cfg.DC, cfg.FC
    xT = P.dram("xT", [D, T], F32, "ExternalInput")
    pcd = P.dram("pcols", [128, ncols], F32, "ExternalInput")
    wd = {n: P.dram(n, list(s), F32, "ExternalInput") for n, s in wshapes.items() if n != "pcols"}
    outT = P.dram("outT", [D, T], F32, "ExternalOutput")
    P.pcols = P.sb("pcols_sb", [128, ncols + 64], F32)
    cst = P.dram("consts", [128, 640], F32, "ExternalInput")
    P.cst = P.sb("consts_sb", [128, 640], F32)
    P.ident = P.cst[:, 0:128]
    P.b64 = P.cst[:, 128:256]
    P.mask12 = P.cst[:, 256:384]
    P.maskL = P.cst[:, 384:448]
    P.identst = P.cst[:, 448:512]
    P.maskG = P.cst[:, 512:576]
    off = dict(off)
    off["ngab"] = ncols
    off["omka"] = ncols + 16
    P.pcol_off = off
    P.ones = P.sb("ones_sb", [128, 128], F32)
    P.sq = [P.sb(f"sq{i}", [128, 449], F32) for i in range(3)]
    P.sqn = 0
    P.psum = [P.ps(f"bank{i}", [128, 512], F32) for i in range(8)]
    P.dma(P.pcols[:, 0:ncols], pcd, [], [("pcols",)])
    P.memset("dve", P.ones[:, :], 1.0, [("ones",)])
    P.dma(P.cst[:, :], cst, [], [("cst",)])
    setup_consts(P)
    P.identbf = P.sb("identbf", [128, 64], BF16)
    P.copy("dve", P.identbf[:, :], P.identst[:, :], [("cst",)], [("identbf",)])
    P.identbf2 = P.sb("identbf2", [128, 128], BF16)
    P.copy("dve", P.identbf2[:, :], P.ident[:, :], [("cst",)], [("identbf2",)])
    P.R.barrier()
    S = {}

    def scratch(name, rows, dt):
        S[name] = P.dram("s_" + name, [rows, T], dt)
        return (name, S[name])

    def job(wname, out, odt=F32, func=AF.Copy, bias=None, scale=None):
        return dict(w=wd[wname], wname=wname, MC=wshapes[wname][0], out=out[1], oname=out[0], odt=odt, func=func,
                    bias=bias, scale=scale)

    hT = scratch("hT", D, F32)
    stage = [0]

    def done():
        stage[0] += 1
        return stage[0] >= upto

    def finish():
        toks = []
        if dbg is not None:
            name, rows, dt = dbg
            dd = P.dram("dbg", [rows, T], dt, "ExternalOutput")
            nch = (rows + 127) // 128
            for c in range(nch):
                r0, r1 = c * 128, min(rows, (c + 1) * 128)
                toks.append(P.dma(dd[r0:r1, :], S[name][r0:r1, :], [(name, c)], [("dbg", c)]))
        hv = S["hT"]
        for c in range(DC):
            toks.append(P.dma(outT[c * 128:(c + 1) * 128, :], hv[c * 128:(c + 1) * 128, :], [("hT", c)], [("outT", c)]))
        P.R.final_wait("sp", toks)
        P.R.emit()
        return P.nc

    Xs = [scratch(f"X{i}", D, BF16) for i in range(6)]
    phase_norm(P, "xT", xT, None, None, "ng0_0", "mix6", Xs, mu="mu", store_h=False)
    for c in range(DC):
        P.dma(S["hT"][c * 128:(c + 1) * 128, :], xT[c * 128:(c + 1) * 128, :], [], [("hT", c)])
    if done():
        return finish()
    rT, kT, vT = scratch("rT", D, F32), scratch("kT", D, F32), scratch("vT", D, F32)
    phase_gemm(P, [job("wr", rT)], Xs[0], DC)
    if done():
        return finish()
    phase_gemm(P, [job("wk", kT)], Xs[2], DC)
    phase_gemm(P, [job("wv", vT)], Xs[3], DC)
    h1w = scratch("h1w", 128 * wshapes["w1"][0], BF16)
    h1a = scratch("h1a", 128 * wshapes["a1"][0], BF16)
    h1g = scratch("h1g", 128 * wshapes["g1"][0], BF16)
    phase_gemm(P, [job("w1", h1w, BF16, AF.Tanh)], Xs[1], DC)
    phase_gemm(P, [job("a1", h1a, BF16)], Xs[4], DC)
    phase_gemm(P, [job("g1", h1g, BF16, AF.Sigmoid)], Xs[5], DC)
    swT, aT, gT = scratch("swT", D, F32), scratch("aT", D, F32), scratch("gT", D, F32)
    phase_gemm(P, [job("w2", swT, F32, AF.Sigmoid, bias="w0")], h1w, wshapes["w1"][0])
    phase_gemm(P, [job("a2", aT, F32, AF.Sigmoid, bias="a0")], h1a, wshapes["a1"][0])
    phase_gemm(P, [job("g2", gT)], h1g, wshapes["g1"][0])
    if done():
        return finish()
    Xo = scratch("Xo", D, BF16)
    phase_rwkv(P, rT, kT, vT, swT, aT, gT, Xo)
    if done():
        return finish()
    mixT = scratch("mixT", D, F32)
    phase_gemm(P, [job("wo", mixT)], Xo, DC)
    if done():
        return finish()
    for l in range(2):
        if l == 1:
            Xg = scratch("Xg", D, BF16)
            phase_norm(P, "hT", S["hT"], ("fT", S["fT"]), "ng0_3", "ng1_0", "plain", [Xg])
            if done():
                return finish()
            projT = scratch("projT", cfg.GIN, F32)
            phase_gemm(P, [job("gin", projT)], Xg, DC)
            h1ga = scratch("h1ga", 128 * wshapes["ga1"][0], BF16)
            phase_gemm(P, [job("ga1", h1ga, BF16)], Xg, DC)
            gaT = scratch("gaT", cfg.DK, F32)
            phase_gemm(P, [job("ga2", gaT, F32, AF.Exp, bias="ngab", scale=-1.0)], h1ga, wshapes["ga1"][0])
            if done():
                return finish()
            Xgo = scratch("Xgo", D, BF16)
            phase_gla(P, projT, gaT, Xgo)
            if done():
                return finish()
            phase_gemm(P, [job("gwo", mixT)], Xgo, DC)
            if done():
                return finish()
        Xf = scratch(f"Xf{l}", D, BF16)
        phase_norm(P, "hT", S["hT"], mixT, f"ng{l}_1", f"ng{l}_2", "plain", [Xf])
        if done():
            return finish()
        uT, zT = scratch(f"uT{l}", F, F32), scratch(f"zT{l}", F, F32)
        phase_gemm(P, [job(f"up{l}", uT), job(f"gate{l}", zT)], Xf, DC)
        if done():
            return finish()
        Xd = scratch(f"Xd{l}", F, BF16)
        phase_ffnact(P, l, uT, zT, Xd)
        if done():
            return finish()
        if l == 0:
            fT = scratch("fT", D, F32)
        blocks = cfg.down_blocks
        phase_gemm(P, [job(f"down{l}", fT)], Xd, FC, tblocks=blocks)
        if done():
            return finish()
    phase_norm(P, "hT", S["hT"], ("fT", S["fT"]), "ng1_3", None, "none", [])
    return finish()


NCORES = 4


def kernel(**inputs):
    import os
    inp = {k: np.asarray(v) for k, v in inputs.items()}
    cfg = Cfg()
    B = inp["x"].shape[0]
    W, off = host_prep(cfg, inp)
    wshapes = {k: v.shape for k, v in W.items()}
    upto = int(os.environ.get("K_UPTO", "99"))
    nc = build(cfg, off, W["pcols"].shape[1], wshapes, upto=upto)
    W["consts"] = make_consts()
    in_maps = []
    for c in range(NCORES):
        m = dict(W)
        m["xT"] = make_xT(cfg, inp, c % B)
        in_maps.append(m)
    res = run_bass_kernel_spmd(nc, in_maps, core_ids=list(range(NCORES)))
    out = np.stack([np.ascontiguousarray(res.results[b]["outT"].T[cfg.NMETA:]) for b in range(B)], axis=0)
    return out.astype(np.float32)
```

```python
import os
import numpy as np
import concourse.bass as bass
import concourse.mybir as mybir
from concourse.bass_utils import run_bass_kernel_spmd

F32 = mybir.dt.float32
BF16 = mybir.dt.bfloat16
AF = mybir.ActivationFunctionType
ALU = mybir.AluOpType
EPOCH = 30000
NDS = 24


class Cfg:
    def __init__(self, D=4096, SEQ=2048, NMETA=16, F=11008, LW=128, LA=128, LG=480,
                 GH=8, GLORA=16):
        self.D, self.SEQ, self.NMETA, self.F = D, SEQ, NMETA, F
        self.T = SEQ + NMETA
        self.DC = D // 128
        self.FC = F // 128
        self.LW, self.LA, self.LG = LW, LA, LG
        self.LGP = ((LG + 127) // 128) * 128
        self.GH = GH
        self.DK = D // 2
        self.DV = D
        self.HK = self.DK // GH
        self.HV = self.DV // GH
        self.GLORA = GLORA
        self.GIN = 2 * self.DK + 2 * self.DV
        self.tiles = []
        t = 0
        while t < self.T:
            n = min(448, self.T - t)
            self.tiles.append((t, n))
            t += n
        nt = 6 if self.T % 6 == 0 and self.T // 6 <= 448 else 0
        if nt:
            w = self.T // 6
            tl6 = [(i * w, w) for i in range(6)]
            self.down_blocks = [tl6[0:2], tl6[2:4], tl6[4:6]]
        else:
            self.down_blocks = [[t_] for t_ in self.tiles]
        self.chunks = []
        t = 0
        while t < self.T:
            n = min(64, self.T - t)
            self.chunks.append((t, n))
            t += n


class Rec:
    ENG = ("pe", "act", "dve", "pool", "sp")

    def __init__(self, nc):
        self.nc = nc
        self.ops = {e: [] for e in self.ENG}
        self.seq = {e: 0 for e in self.ENG + ("peA", "peB")}
        self.csem = {e: {} for e in self.ENG + ("peA", "peB")}
        self.dsem = {}
        self.dcount = {}
        self.dnext = {e: 0 for e in self.ENG}
        self.res = {}
        self.waited = {e: {} for e in self.ENG}
        self._sems = []
        self._cms = []
        import os
        self.dump = [] if os.environ.get("REC_DUMP") else None

    def _newsem(self, name):
        cm = self.nc.semaphore(name)
        s = cm.__enter__()
        self._cms.append(cm)
        return s

    def _csem(self, eng, epoch):
        d = self.csem[eng]
        if epoch not in d:
            d[epoch] = self._newsem(f"c_{eng}_{epoch}")
        return d[epoch]

    def _need(self, eng, tok, waits):
        if tok is None:
            return
        if tok[0] == "c":
            _, e2, n = tok
            if e2.startswith("pe") and eng == "pe":
                return
            if self.waited[eng].get(("c", e2), 0) >= n:
                return
            self.waited[eng][("c", e2)] = n
            ep, loc = divmod(n - 1, EPOCH)
            waits.append((self._csem(e2, ep), loc + 1))
        else:
            _, q, idx, cnt = tok
            key = ("d", q, idx)
            if self.waited[eng].get(key, 0) >= cnt:
                return
            self.waited[eng][key] = cnt
            waits.append((self.dsem[(q, idx)], cnt))

    @staticmethod
    def _flat(ks):
        out = []
        for k in ks:
            if isinstance(k, list):
                out.extend(k)
            else:
                out.append(k)
        return out

    def op(self, eng, fn, reads=(), writes=(), dma=False, lane=None):
        veng = eng + lane if lane else eng
        reads = self._flat(reads)
        writes = self._flat(writes)
        psr = [k for k in reads if k[0] == "ps"]
        if psr:
            reads = [k for k in reads if k[0] != "ps"]
            writes = list(writes) + psr
        waits = []
        for r in reads:
            st = self.res.get(r)
            if st is not None:
                self._need(eng, st["w"], waits)
        for w in writes:
            st = self.res.get(w)
            if st is not None:
                self._need(eng, st["w"], waits)
                for tk in st["r"]:
                    self._need(eng, tk, waits)
        if dma:
            idx = self.dnext[eng] % NDS
            self.dnext[eng] += 1
            key = (eng, idx)
            if key not in self.dsem:
                self.dsem[key] = self._newsem(f"d_{eng}_{idx}")
                self.dcount[key] = 0
            prev = self.dcount[key]
            if prev > 0:
                self._need(eng, ("d", eng, idx, prev), waits)
            self.dcount[key] = prev + 16
            tok = ("d", eng, idx, prev + 16)
            inc = (self.dsem[key], 16)
        else:
            self.seq[veng] += 1
            n = self.seq[veng]
            ep, loc = divmod(n - 1, EPOCH)
            tok = ("c", veng, n)
            inc = (self._csem(veng, ep), 1)
        for r in reads:
            st = self.res.setdefault(r, {"w": None, "r": []})
            st["r"] = [t for t in st["r"] if not (t[0] == "c" and tok[0] == "c" and t[1] == tok[1])]
            st["r"].append(tok)
        for w in writes:
            self.res[w] = {"w": tok, "r": []}
        self.ops[eng].append((waits, fn, inc))
        if self.dump is not None:
            self.dump.append((eng, tok, [(getattr(sm, "name", str(sm)), v) for sm, v in waits], writes[:2], reads[:3]))
        return tok

    def barrier(self):
        toks = []
        for e in self.seq:
            if self.seq[e] > 0:
                toks.append(("c", e, self.seq[e]))
        for (q, idx), cnt in self.dcount.items():
            if cnt > 0:
                toks.append(("d", q, idx, cnt))
        for e in self.ENG:
            waits = []
            for t in toks:
                if t[0] == "c" and (t[1] == e or (e == "pe" and t[1].startswith("pe"))):
                    continue
                self._need(e, t, waits)
            if waits:
                self.ops[e].append((waits, None, None))
        self.res = {}

    def final_wait(self, eng, toks):
        waits = []
        for t in toks:
            self._need(eng, t, waits)
        self.ops[eng].append((waits, None, None))

    def emit(self):
        if self.dump is not None:
            for d in self.dump[-int(os.environ["REC_DUMP"]):]:
                print("OP", d)
        nc = self.nc
        ops = self.ops

        def run(engh, lst):
            for waits, fn, inc in lst:
                for s, v in waits:
                    engh.wait_ge(s, v)
                if fn is not None:
                    ins = fn(engh)
                    ins.then_inc(inc[0], inc[1])

        with nc.Block() as block:
            @block.tensor
            def _(e):
                run(e, ops["pe"])

            @block.scalar
            def _(e):
                run(e, ops["act"])

            @block.vector
            def _(e):
                run(e, ops["dve"])

            @block.gpsimd
            def _(e):
                run(e, ops["pool"])

            @block.sync
            def _(e):
                run(e, ops["sp"])
        for cm in reversed(self._cms):
            cm.__exit__(None, None, None)


class Prog:
    def __init__(self, cfg, only=None):
        self.cfg = cfg
        self.nc = bass.Bass("TRN2", target_bir_lowering=False)
        self.R = Rec(self.nc)
        self._cms = []
        self.psn = 0
        self.uid = 0
        self.pcol_off = {}
        self.dq = 0

    def sb(self, name, shape, dt=F32):
        self.nalloc = getattr(self, "nalloc", 0) + 1
        cm = self.nc.sbuf_tensor(f"{name}_{self.nalloc}", shape, dt)
        t = cm.__enter__()
        self._cms.append(cm)
        return t

    def ps(self, name, shape, dt=F32):
        cm = self.nc.psum_tensor(name, shape, dt)
        t = cm.__enter__()
        self._cms.append(cm)
        return t

    def free_to(self, n):
        self.R.barrier()
        while len(self._cms) > n:
            self._cms.pop().__exit__(None, None, None)

    def dram(self, name, shape, dt, kind=None):
        if kind is None:
            return self.nc.dram_tensor(name, list(shape), dt).ap()
        return self.nc.dram_tensor(name, list(shape), dt, kind=kind).ap()

    def dmaq(self):
        self.dq += 1
        return "sp"

    def dma(self, out, in_, reads, writes, q="sp"):
        return self.R.op(q, lambda e, o=out, i=in_: e.dma_start(out=o, in_=i), reads, writes, dma=True)

    def act(self, out, in_, func, reads, writes, bias=None, scale=None):
        kw = {}
        if bias is not None:
            kw["bias"] = bias
        if scale is not None:
            kw["scale"] = scale
        return self.R.op("act", lambda e, o=out, i=in_, f=func, k=kw: e.activation(out=o, in_=i, func=f, **k),
                         reads, writes)

    def tt(self, eng, out, in0, in1, op, reads, writes):
        return self.R.op(eng, lambda e, o=out, a=in0, b=in1, p=op: e.tensor_tensor(out=o, in0=a, in1=b, op=p),
                         reads, writes)

    def ts(self, eng, out, in0, s1, s2, op0, op1, reads, writes):
        if s2 is None:
            s2, op1 = 0.0, ALU.add
        return self.R.op(eng, lambda e, o=out, a=in0, x=s1, y=s2, p=op0, q=op1: e.tensor_scalar(o, a, x, y, p, q),
                         reads, writes)

    def stt(self, eng, out, in0, scalar, in1, op0, op1, reads, writes):
        eng = "dve"
        return self.R.op(eng, lambda e, o=out, a=in0, s=scalar, b=in1, p=op0, q=op1:
                         e.scalar_tensor_tensor(out=o, in0=a, scalar=s, in1=b, op0=p, op1=q), reads, writes)

    def copy(self, eng, out, in_, reads, writes):
        if eng == "act":
            return self.act(out, in_, AF.Copy, reads, writes)
        return self.R.op(eng, lambda e, o=out, i=in_: e.tensor_copy(out=o, in_=i), reads, writes)

    def memset(self, eng, ap, val, writes):
        return self.R.op(eng, lambda e, a=ap, v=val: e.memset(a, v), (), writes)

    def mm(self, out, lhsT, rhs, start, stop, reads, writes, lane=None):
        return self.R.op("pe", lambda e, o=out, l=lhsT, r=rhs, s=start, t=stop:
                         e.matmul(o, l, r, start=s, stop=t), reads, writes, lane=lane)

    def transpose(self, out, in_, ident, reads, writes):
        return self.R.op("pe", lambda e, o=out, i=in_, d=ident: e.transpose(o, i, d), reads, writes)


def pcol_pack(vecs):
    cols = []
    off = {}
    c = 0
    for name, v in vecs:
        v = np.asarray(v, np.float32).reshape(-1)
        n = v.size // 128
        assert n * 128 == v.size, name
        off[name] = c
        cols.append(v.reshape(n, 128).T)
        c += n
    return np.ascontiguousarray(np.concatenate(cols, axis=1)), off


def keys(name, n):
    return [(name, i) for i in range(n)]


def phase_gemm(P, jobs, xT, KC, tblocks=None):
    cfg, R = P.cfg, P.R
    T = cfg.T
    mark = len(P._cms)
    xname, xd = xT
    if tblocks is None:
        tblocks = [cfg.tiles]
    maxtb = max(sum(n for _, n in blk) for blk in tblocks)
    X = P.sb("gX", [128, KC, maxtb], BF16)
    PCH = min(KC, 8)
    npieces = (KC + PCH - 1) // PCH
    NW = 4
    wst = [P.sb(f"gwst{i}", [128, PCH * 128], F32) for i in range(NW)]
    wb = [P.sb(f"gwb{i}", [128, KC * 128], BF16) for i in range(2)]
    P.uid += 1
    u = P.uid
    wcnt = 0
    pcnt = 0
    ocnt = 0
    orows = {}
    for bi, blk in enumerate(tblocks):
        b0 = blk[0][0]
        bl = sum(n for _, n in blk)
        xv = xd.rearrange("(kc p) t -> p kc t", p=128)
        for kc in range(KC):
            P.dma(X[:, kc, 0:bl], xv[:, kc, b0:b0 + bl], [(xname, kc)], [("gX", u)])
        for job in jobs:
            odt = job["odt"]
            key = ("orow", odt)
            if key not in orows:
                orows[key] = [P.sb(f"gorow{len(orows)}_{i}", [128, maxtb], odt) for i in range(2)]
            orow = orows[key]
            wd = job["w"]
            for m in range(job["MC"]):
                wbi = wcnt % 2
                wcnt += 1
                for pc in range(npieces):
                    k0 = pc * PCH
                    kn = min(PCH, KC - k0)
                    si = pcnt % NW
                    pcnt += 1
                    P.dma(wst[si][:, 0:kn * 128], wd[m, :, k0 * 128:(k0 + kn) * 128], [(job["wname"], m)],
                          [("gwst", u, si)])
                    ceng = "dve" if (pcnt % 2 == 0) else "pool"
                    P.copy(ceng, wb[wbi][:, k0 * 128:(k0 + kn) * 128], wst[si][:, 0:kn * 128],
                           [("gwst", u, si)], [("gwb", u, wbi, pc)])
                oi = ocnt % 2
                ocnt += 1
                off = 0
                for (t0, n) in blk:
                    bank = P.psn % 8
                    P.psn += 1
                    pst = P.psum[bank]
                    for kc in range(KC):
                        P.mm(pst[:, 0:n], wb[wbi][:, kc * 128:(kc + 1) * 128], X[:, kc, off:off + n],
                             kc == 0, kc == KC - 1,
                             [("gwb", u, wbi, kc // PCH), ("gX", u)], [("ps", bank)])
                    bias = None
                    if job.get("bias") is not None:
                        c = P.pcol_off[job["bias"]] + m
                        bias = P.pcols[:, c:c + 1]
                    P.act(orow[oi][:, off:off + n], pst[:, 0:n], job["func"], [("ps", bank)],
                          [("gorow", u, odt, oi)], bias=bias, scale=job.get("scale"))
                    off += n
                P.dma(job["out"][m * 128:(m + 1) * 128, b0:b0 + bl], orow[oi][:, 0:bl],
                      [("gorow", u, odt, oi)], [(job["oname"], m)], q="act")
    P.free_to(mark)


def norm_stats(P, src, n, DC, u, srckey, rstd, rkey):
    bank = P.psn % 8
    P.psn += 1
    pst = P.psum[bank]
    for kc in range(DC):
        si = P.sqn % 3
        P.sqn += 1
        P.act(P.sq[si][:, 0:n], src[:, kc, 0:n], AF.Square, [srckey[kc]], [("sq", si)])
        P.mm(pst[:, 0:n], P.ones[:, :], P.sq[si][:, 0:n], kc == 0, kc == DC - 1, [("sq", si)], [("ps", bank)])
    P.ts("dve", rstd[:, 0:n], pst[:, 0:n], 1.0 / (DC * 128), 1e-6, ALU.mult, ALU.add, [("ps", bank)], [rkey])
    P.act(rstd[:, 0:n], rstd[:, 0:n], AF.Sqrt, [rkey], [rkey])
    P.R.op("dve", lambda e, o=rstd[:, 0:n]: e.reciprocal(out=o, in_=o), [rkey], [rkey])


def phase_norm(P, hname, hd, src, g_add, g_out, mode, outs, mu=None, store_h=True):
    cfg = P.cfg
    DC = cfg.DC
    mark = len(P._cms)
    P.uid += 1
    u = P.uid
    halo = 1 if mode == "mix6" else 0
    W = 448 + halo
    Hh = P.sb("nH", [128, DC, W], F32)
    S = P.sb("nS", [128, DC, W], F32)
    O = [P.sb(f"nO{i}", [128, DC, 448], BF16) for i in range(2)] if mode != "none" else []
    rstd = [P.sb(f"nr{i}", [128, W], F32) for i in range(2)]
    ocnt = 0
    for ti, (t0, n) in enumerate(cfg.tiles):
        h0 = halo if ti > 0 else 0
        nn = n + halo
        c0 = halo - h0
        hv = hd.rearrange("(kc p) t -> p kc t", p=128)
        kH = [("nH", u, k) for k in range(DC)]
        kS = [("nS", u, k) for k in range(DC)]
        P.dma(Hh[:, :, c0:nn], hv[:, :, t0 - h0:t0 + n], keys(hname, DC), kH)
        if halo and ti == 0:
            P.memset("pool", Hh[:, :, 0:1], 0.0, kH)
        if src is not None:
            sname, sd = src
            sv = sd.rearrange("(kc p) t -> p kc t", p=128)
            P.dma(S[:, :, c0:nn], sv[:, :, t0 - h0:t0 + n], keys(sname, DC), kS)
            if halo and ti == 0:
                P.memset("pool", S[:, :, 0:1], 0.0, kS)
            norm_stats(P, S, nn, DC, u, kS, rstd[0], ("nr", u, 0))
            ga = P.pcol_off[g_add]
            for kc in range(DC):
                P.stt("dve", S[:, kc, 0:nn], S[:, kc, 0:nn], P.pcols[:, ga + kc:ga + kc + 1], rstd[0][:, 0:nn],
                      ALU.mult, ALU.mult, [kS[kc], ("nr", u, 0)], [kS[kc]])
                P.tt("pool", Hh[:, kc, 0:nn], Hh[:, kc, 0:nn], S[:, kc, 0:nn], ALU.add, [kS[kc], kH[kc]], [kH[kc]])
        if store_h:
            P.dma(hv[:, :, t0:t0 + n], Hh[:, :, halo:nn], kH, keys(hname, DC))
        if mode == "none":
            continue
        norm_stats(P, Hh, nn, DC, u, kH, rstd[1], ("nr", u, 1))
        go = P.pcol_off[g_out]
        if mode == "plain":
            oi = ocnt % 2
            ocnt += 1
            for kc in range(DC):
                eng = "dve" if kc % 2 == 0 else "pool"
                P.stt(eng, O[oi][:, kc, 0:n], Hh[:, kc, 0:n], P.pcols[:, go + kc:go + kc + 1], rstd[1][:, 0:n],
                      ALU.mult, ALU.mult, [kH[kc], ("nr", u, 1)], [("nO", u, oi, kc)])
            oname, od = outs[0]
            ov = od.rearrange("(kc p) t -> p kc t", p=128)
            P.dma(ov[:, :, t0:t0 + n], O[oi][:, :, 0:n], [("nO", u, oi, k) for k in range(DC)], keys(oname, DC))
        else:
            for kc in range(DC):
                P.stt("dve", S[:, kc, 0:nn], Hh[:, kc, 0:nn], P.pcols[:, go + kc:go + kc + 1], rstd[1][:, 0:nn],
                      ALU.mult, ALU.mult, [kH[kc], ("nr", u, 1)], [kS[kc]])
            for kc in range(DC):
                P.tt("pool", Hh[:, kc, 0:n], S[:, kc, 0:n], S[:, kc, 1:nn], ALU.subtract, [kS[kc], kH[kc]], [kH[kc]])
            mo = P.pcol_off[mu]
            for i in range(6):
                oi = ocnt % 2
                ocnt += 1
                for kc in range(DC):
                    eng = "dve" if kc % 2 == 0 else "pool"
                    c = mo + i * DC + kc
                    P.stt(eng, O[oi][:, kc, 0:n], Hh[:, kc, 0:n], P.pcols[:, c:c + 1], S[:, kc, 1:nn],
                          ALU.mult, ALU.add, [kH[kc], kS[kc]], [("nO", u, oi, kc)])
                oname, od = outs[i]
                ov = od.rearrange("(kc p) t -> p kc t", p=128)
                P.dma(ov[:, :, t0:t0 + n], O[oi][:, :, 0:n], [("nO", u, oi, k) for k in range(DC)], keys(oname, DC))
    P.free_to(mark)


def setup_consts(P):
    cfg = P.cfg
    n = cfg.DK // 128
    a = P.pcol_off["gab"]
    b = P.pcol_off["ngab"]
    P.ts("dve", P.pcols[:, b:b + n], P.pcols[:, a:a + n], -1.0, None, ALU.mult, None, [("pcols",)], [("pcols2",)])
    a = P.pcol_off["k_a"]
    b = P.pcol_off["omka"]
    n = cfg.DC
    P.ts("dve", P.pcols[:, b:b + n], P.pcols[:, a:a + n], -1.0, 1.0, ALU.mult, ALU.add, [("pcols",)], [("pcols3",)])


def phase_ffnact(P, l, uT, zT, Xd):
    cfg = P.cfg
    T, FC = cfg.T, cfg.FC
    mark = len(P._cms)
    P.uid += 1
    u = P.uid
    zrow = [P.sb(f"fz{i}", [128, T + 2], F32) for i in range(2)]
    urow = [P.sb(f"fu{i}", [128, T], F32) for i in range(2)]
    acc = [P.sb(f"fa{i}", [128, T], F32) for i in range(2)]
    orow = [P.sb(f"fo{i}", [128, T], BF16) for i in range(2)]
    for i in range(2):
        P.memset("pool", zrow[i][:, 0:2], 0.0, [("fz", u, i)])
    c0, c1, c2, cb = (P.pcol_off[f"cw{l}_0"], P.pcol_off[f"cw{l}_1"], P.pcol_off[f"cw{l}_2"], P.pcol_off[f"cb{l}"])
    pc = P.pcols
    for fc in range(FC):
        i = fc % 2
        e1 = "dve" if i == 0 else "pool"
        e2 = "pool" if i == 0 else "dve"
        P.dma(zrow[i][:, 2:T + 2], zT[1][fc * 128:(fc + 1) * 128, :], [(zT[0], fc)], [("fz", u, i)])
        P.dma(urow[i][:, :], uT[1][fc * 128:(fc + 1) * 128, :], [(uT[0], fc)], [("fu", u, i)])
        P.ts(e1, acc[i][:, :], zrow[i][:, 2:T + 2], pc[:, c2 + fc:c2 + fc + 1], pc[:, cb + fc:cb + fc + 1],
             ALU.mult, ALU.add, [("fz", u, i)], [("fa", u, i)])
        P.stt(e1, acc[i][:, :], zrow[i][:, 1:T + 1], pc[:, c1 + fc:c1 + fc + 1], acc[i][:, :], ALU.mult, ALU.add,
              [("fz", u, i), ("fa", u, i)], [("fa", u, i)])
        P.stt(e1, acc[i][:, :], zrow[i][:, 0:T], pc[:, c0 + fc:c0 + fc + 1], acc[i][:, :], ALU.mult, ALU.add,
              [("fz", u, i), ("fa", u, i)], [("fa", u, i)])
        P.act(acc[i][:, :], acc[i][:, :], AF.Silu, [("fa", u, i)], [("fa", u, i)])
        P.tt(e2, orow[i][:, :], acc[i][:, :], urow[i][:, :], ALU.mult, [("fa", u, i), ("fu", u, i)], [("fo", u, i)])
        P.dma(Xd[1][fc * 128:(fc + 1) * 128, :], orow[i][:, :], [("fo", u, i)], [(Xd[0], fc)], q="act")
    P.free_to(mark)


def interleave(lists):
    if not lists:
        return
    n = max(len(l) for l in lists)
    for i in range(n):
        for l in lists:
            if i < len(l):
                l[i]()


class PsReg:
    def __init__(self, P):
        self.P = P
        self.n = 0

    def get(self):
        i = self.n % 4
        self.n += 1
        return [self.P.psum[i], self.P.psum[i + 4]], [("ps", i), ("ps", i + 4)]


C0 = float(np.exp(-0.5))


def phase_rwkv(P, rT, kT, vT, swT, aT, gT, Xo):
    cfg = P.cfg
    T = cfg.T
    NCH = len(cfg.chunks)
    TP = 64 + NCH * 64
    TW = NCH * 64
    mark = len(P._cms)
    P.uid += 1
    u = P.uid
    pr = PsReg(P)
    pc = P.pcols
    names = ["r", "k", "v", "a", "sw", "kk", "kh", "bon", "cum", "Ep", "Em", "Bh", "Kh", "y"]
    tl = {n: P.sb("rw_" + n, [128, TP], BF16 if n in ("Kh", "Bh") else F32) for n in names}
    tl["vb"] = P.sb("rw_vb", [128, TP], BF16)
    names = names + ["vb"]
    AR = P.sb("rw_AR", [128, NCH, 2, 64], BF16)
    onesc = P.sb("rw_ones", [128, 64], F32)
    obf = [P.sb(f"rw_obf{i}", [128, T], BF16) for i in range(2)]
    NS = 8
    SC1 = [P.sb(f"rw_sc1_{i}", [128, 128], BF16) for i in range(NS)]
    SC2 = [P.sb(f"rw_sc2_{i}", [128, 128], BF16) for i in range(NS)]
    TTs = [P.sb(f"rw_tt_{i}", [128, 128], BF16) for i in range(NS)]
    Vst = [P.sb(f"rw_vst_{i}", [128, 64], BF16) for i in range(NS)]
    Bst = [P.sb(f"rw_bst_{i}", [128, 64], BF16) for i in range(NS)]
    Kst = [P.sb(f"rw_kst_{i}", [128, 64], BF16) for i in range(NS)]
    Pbf = [P.sb(f"rw_pbf_{i}", [128, 64], BF16) for i in range(2)]
    Nb = [[P.sb(f"rw_n_{i}_{j}", [128, 128], BF16) for j in range(2)] for i in range(NS)]
    NTb = [[P.sb(f"rw_nt_{i}_{j}", [128, 128], BF16) for j in range(2)] for i in range(NS)]
    TTb = [[P.sb(f"rw_ttb_{i}_{j}", [128, 128], BF16) for j in range(2)] for i in range(NS)]
    for i in range(NS):
        for j in range(2):
            P.memset("pool", Nb[i][j][:, :], 0.0, [("nb", u, i, j)])
            P.memset("pool", NTb[i][j][:, :], 0.0, [("ntb", u, i, j)])
            P.memset("pool", TTb[i][j][:, :], 0.0, [("ttb", u, i, j)])
    Wsb = [P.sb(f"rw_w_{i}", [128, 64], BF16) for i in range(2)]
    Usb = [P.sb(f"rw_u_{i}", [128, 64], BF16) for i in range(2)]
    Pst = [P.sb(f"rw_p_{i}", [128, 64], F32) for i in range(2)]
    Ptmp = P.sb("rw_ptmp", [128, 64], F32)
    P.memset("dve", onesc[:, :], 1.0, [("rw_ones", u)])
    for n in names:
        P.memset("pool", tl[n][:, :], 0.0, [("rw", u, n)])
    K = lambda n: ("rw", u, n)
    HP = [slice(0, 64), slice(64, 128)]
    LN = ["A", "B"]
    pieces = [(c0, min(512, TW - c0)) for c0 in range(0, TW, 512)]

    def full(n):
        return tl[n][:, 64:64 + TW]

    def b64(src, fn):
        for (c0, cn) in pieces:
            bank = P.psn % 8
            P.psn += 1
            pst = P.psum[bank]
            P.mm(pst[:, 0:cn], P.b64[:, :], tl[src][:, 64 + c0:64 + c0 + cn], True, True, [K(src)], [("ps", bank)])
            fn(pst[:, 0:cn], c0, cn, ("ps", bank))

    for fc in range(cfg.DC):
        col = lambda nm: pc[:, P.pcol_off[nm] + fc:P.pcol_off[nm] + fc + 1]
        rows = slice(fc * 128, (fc + 1) * 128)
        for nm, src in (("r", rT), ("k", kT), ("v", vT), ("a", aT), ("sw", swT)):
            P.dma(tl[nm][:, 64:64 + T], src[1][rows, :], [(src[0], fc)], [K(nm)])
        P.ts("pool", full("kk"), full("k"), col("k_k"), None, ALU.mult, None, [K("k")], [K("kk")])
        P.act(full("y"), full("kk"), AF.Square, [K("kk")], [K("y")])

        def f1(ps, c0, cn, key):
            sl = slice(64 + c0, 64 + c0 + cn)
            P.act(tl["cum"][:, sl], ps, AF.Sqrt, [key], [K("cum")])
        b64("y", f1)
        P.ts("dve", full("cum"), full("cum"), 1e-12, None, ALU.max, None, [K("cum")], [K("cum")])
        P.R.op("dve", lambda e, o=full("cum"): e.reciprocal(out=o, in_=o), [K("cum")], [K("cum")])
        P.tt("pool", full("kk"), full("kk"), full("cum"), ALU.mult, [K("kk"), K("cum")], [K("kk")])
        P.ts("dve", full("kh"), full("a"), col("k_a"), col("omka"), ALU.mult, ALU.add, [K("a")], [K("kh")])
        P.tt("pool", full("kh"), full("kh"), full("k"), ALU.mult, [K("kh"), K("k")], [K("kh")])
        P.stt("dve", full("y"), full("r"), col("r_k"), full("kh"), ALU.mult, ALU.mult, [K("r"), K("kh")], [K("y")])

        def f2(ps, c0, cn, key):
            sl = slice(64 + c0, 64 + c0 + cn)
            P.tt("dve", tl["bon"][:, sl], ps, tl["v"][:, sl], ALU.mult, [key, K("v")], [K("bon")])
        b64("y", f2)
        P.copy("pool", full("vb"), full("v"), [K("v")], [K("vb")])
        for (t0, n) in cfg.chunks:
            sl = slice(64 + t0, 64 + t0 + 64)
            P.R.op("dve", lambda e, o=tl["cum"][:, sl], d1=tl["sw"][:, sl]:
                   e.tensor_tensor_scan(out=o, data0=onesc[:, :], data1=d1, initial=0.0, op0=ALU.mult, op1=ALU.add),
                   [K("sw"), ("rw_ones", u)], [K("cum")])
        P.act(full("Ep"), full("cum"), AF.Exp, [K("cum")], [K("Ep")], scale=-C0)
        P.act(full("Em"), full("cum"), AF.Exp, [K("cum")], [K("Em")], scale=C0)
        P.tt("pool", full("cum"), full("cum"), full("sw"), ALU.subtract, [K("cum"), K("sw")], [K("cum")])
        P.act(full("cum"), full("cum"), AF.Exp, [K("cum")], [K("cum")], scale=-C0)
        arv = lambda j: AR[:, :, j, :]
        v3 = lambda n: tl[n][:, 64:64 + TW].rearrange("p (c t) -> p c t", t=64)
        P.stt("dve", arv(0), v3("kk"), -1.0, v3("cum"), ALU.mult, ALU.mult, [K("kk"), K("cum")], [K("AR")])
        P.tt("pool", arv(1), v3("r"), v3("Ep"), ALU.mult, [K("r"), K("Ep")], [K("AR")])
        P.tt("pool", full("Bh"), full("kk"), full("a"), ALU.mult, [K("kk"), K("a")], [K("Bh")])
        P.tt("dve", full("Bh"), full("Bh"), full("Em"), ALU.mult, [K("Bh"), K("Em")], [K("Bh")])
        P.tt("pool", full("Kh"), full("kh"), full("Em"), ALU.mult, [K("kh"), K("Em")], [K("Kh")])
        P.memset("pool", Pst[0][:, :], 0.0, [("rw_p", u, 0, 0), ("rw_p", u, 0, 1)])
        P.memset("pool", Pbf[0][:, :], 0.0, [("rw_pb", u, 0, 0), ("rw_pb", u, 0, 1)])

        def pre_stages(ci):
            s = ci % NS
            t0 = ci * 64
            ch = slice(64 + t0, 128 + t0)
            st = []
            EV = ["dve", "act"]

            def a():
                r, kr = pr.get()
                for h, hp in enumerate(HP):
                    P.mm(r[h][hp, 0:128], tl["Kh"][hp, ch], AR[hp, ci, :, :], True, True, [K("Kh"), K("AR")], [kr[h]], lane=LN[h])
                    P.mm(r[h][hp, 128:256], tl["Bh"][hp, ch], AR[hp, ci, :, :], True, True, [K("Bh"), K("AR")], [kr[h]], lane=LN[h])
                    P.mm(r[h][hp, 256:320], AR[hp, ci, 0, :], tl["Bh"][hp, ch], True, True, [K("Bh"), K("AR")], [kr[h]], lane=LN[h])
                for h, hp in enumerate(HP):
                    P.tt("dve", SC1[s][hp, :], r[h][hp, 0:128], P.mask12[hp, :], ALU.mult, [kr[h]], [("sc1", u, s, h)])
                    P.tt("dve", SC2[s][hp, :], r[h][hp, 128:256], P.mask12[hp, :], ALU.mult, [kr[h]], [("sc2", u, s, h)])
                    bc = slice(h * 64, h * 64 + 64)
                    P.tt("dve", Nb[s][0][hp, bc], r[h][hp, 256:320], P.maskL[hp, :], ALU.mult, [kr[h]], [("nb", u, s, 0)])
                    P.copy("pool", NTb[s][0][hp, bc], SC2[s][hp, 0:64], [("sc2", u, s, h)], [("ntb", u, s, 0)])
                    P.tt("pool", TTb[s][0][hp, bc], SC2[s][hp, 0:64], P.identst[hp, :], ALU.add, [("sc2", u, s, h)],
                         [("ttb", u, s, 0)])
            st.append(a)
            def lvl_stage(lvl):
                def f():
                    q, kq = pr.get()
                    do_b = lvl <= 5
                    do_c = lvl >= 2
                    if do_b:
                        i0, i1 = (lvl - 1) % 2, lvl % 2
                    if do_c:
                        k = lvl - 1
                        j0, j1 = (k - 1) % 2, k % 2
                        P.mm(q[0][:, 0:128], P.identbf2[:, :], TTb[s][j0][:, :], True, False, [("ttb", u, s, j0)], [kq[0]])
                        P.mm(q[0][:, 0:128], Nb[s][j1][:, :], TTb[s][j0][:, :], False, True,
                             [("nb", u, s, j1), ("ttb", u, s, j0)], [kq[0]])
                    if do_b:
                        P.mm(q[1][:, 0:128], NTb[s][i0][:, :], Nb[s][i0][:, :], True, True,
                             [("ntb", u, s, i0), ("nb", u, s, i0)], [kq[1]])
                        if lvl < 5:
                            P.mm(q[1][:, 128:256], Nb[s][i0][:, :], NTb[s][i0][:, :], True, True,
                                 [("ntb", u, s, i0), ("nb", u, s, i0)], [kq[1]])
                    if do_c:
                        if k == 5:
                            P.copy("dve", TTs[s][:, :], q[0][:, 0:128], [kq[0]], [("tts", u, s, 0), ("tts", u, s, 1)])
                        else:
                            P.copy("dve", TTb[s][j1][:, :], q[0][:, 0:128], [kq[0]], [("ttb", u, s, j1)])
                    if do_b:
                        P.copy("act", Nb[s][i1][:, :], q[1][:, 0:128], [kq[1]], [("nb", u, s, i1)])
                        if lvl < 5:
                            P.copy("act", NTb[s][i1][:, :], q[1][:, 128:256], [kq[1]], [("ntb", u, s, i1)])
                return f
            for lvl in range(1, 7):
                st.append(lvl_stage(lvl))

            def d():
                q, kq = pr.get()
                lst = (("vb", Vst, "vst"), ("Bh", Bst, "bst"), ("Kh", Kst, "kst"))
                for j, (nm, dstl, dkey) in enumerate(lst):
                    cs = slice(j * 64, j * 64 + 64)
                    P.mm(q[1][:, cs], tl[nm][64:128, 64 + t0 - 64:64 + t0 + 64], P.identbf2[64:128, 64:128], True, True,
                         [K(nm)], [kq[1]], lane="B")
                    P.mm(q[0][0:64, cs], tl[nm][0:64, ch], P.identbf2[0:64, 0:64], True, True, [K(nm)], [kq[0]], lane="A")
                for j, (nm, dstl, dkey) in enumerate(lst):
                    cs = slice(j * 64, j * 64 + 64)
                    P.copy("act", dstl[s][0:64, :], q[0][0:64, cs], [kq[0]], [(dkey, u, s, 0)])
                    P.copy("dve", dstl[s][64:128, :], q[1][64:128, cs], [kq[1]], [(dkey, u, s, 1)])
            st.insert(1, d)
            return st

        def scan_stages(ci):
            s = ci % NS
            t0 = ci * 64
            ch = slice(64 + t0, 128 + t0)
            pi, po = ci % 2, (ci + 1) % 2
            wi = ci % 2
            st = []
            EV = ["act", "dve"]

            def a():
                q, kq = pr.get()
                for h, hp in enumerate(HP):
                    P.mm(q[h][hp, 0:64], AR[hp, ci, 0, :], Pbf[pi][hp, :], True, False, [K("AR"), ("rw_pb", u, pi, h)], [kq[h]], lane=LN[h])
                    P.mm(q[h][hp, 0:64], SC1[s][hp, 0:64], Vst[s][hp, :], False, True,
                         [("sc1", u, s, h), ("vst", u, s, h)], [kq[h]], lane=LN[h])
                for h, hp in enumerate(HP):
                    P.copy(EV[h], Wsb[wi][hp, :], q[h][hp, 0:64], [kq[h]], [("rw_w", u, wi, h)])
            st.append(a)

            def b():
                q, kq = pr.get()
                for h, hp in enumerate(HP):
                    P.mm(q[h][hp, 0:64], TTs[s][hp, h * 64:h * 64 + 64], Wsb[wi][hp, :], True, True,
                         [("tts", u, s, h), ("rw_w", u, wi, h)], [kq[h]], lane=LN[h])
                for h, hp in enumerate(HP):
                    P.copy(EV[h], Usb[wi][hp, :], q[h][hp, 0:64], [kq[h]], [("rw_u", u, wi, h)])
            st.append(b)

            def c():
                q, kq = pr.get()
                for h, hp in enumerate(HP):
                    P.mm(q[h][hp, 0:64], Bst[s][hp, :], Usb[wi][hp, :], True, False,
                         [("bst", u, s, h), ("rw_u", u, wi, h)], [kq[h]], lane=LN[h])
                    P.mm(q[h][hp, 0:64], Kst[s][hp, :], Vst[s][hp, :], False, True,
                         [("kst", u, s, h), ("vst", u, s, h)], [kq[h]], lane=LN[h])
                for h, hp in enumerate(HP):
                    P.mm(q[h][hp, 64:128], Pbf[pi][hp, :], AR[hp, ci, 1, :], True, False, [("rw_pb", u, pi, h), K("AR")], [kq[h]], lane=LN[h])
                    P.mm(q[h][hp, 64:128], Usb[wi][hp, :], SC2[s][hp, 64:128], False, False,
                         [("rw_u", u, wi, h), ("sc2", u, s, h)], [kq[h]], lane=LN[h])
                    P.mm(q[h][hp, 64:128], Vst[s][hp, :], SC1[s][hp, 64:128], False, True,
                         [("vst", u, s, h), ("sc1", u, s, h)], [kq[h]], lane=LN[h])
                for h, hp in enumerate(HP):
                    P.tt("dve", Ptmp[hp, :], q[h][hp, 0:64], Pst[pi][hp, :], ALU.add, [kq[h], ("rw_p", u, pi, h)],
                         [("rw_ptmp", u, h)])
                    P.act(Pst[po][hp, :], Ptmp[hp, :], AF.Copy, [("rw_ptmp", u, h), K("Ep")], [("rw_p", u, po, h)],
                          scale=tl["Ep"][hp, 64 + t0 + 63:64 + t0 + 64])
                    P.ts("pool", Pbf[po][hp, :], Ptmp[hp, :], tl["Ep"][hp, 64 + t0 + 63:64 + t0 + 64], None, ALU.mult, None,
                         [("rw_ptmp", u, h), K("Ep")], [("rw_pb", u, po, h)])
                for h, hp in enumerate(HP):
                    P.copy(EV[h], tl["y"][hp, ch], q[h][hp, 64:128], [kq[h]], [K("y")])
            st.append(c)
            return st

        import os
        RWDBG = int(os.environ.get("RWDBG", "9"))
        if RWDBG < 1:
            continue
        LA = 4
        NPRE = int(os.environ.get("RW_NPRE", "99"))
        NSCAN = int(os.environ.get("RW_NSCAN", "99"))
        for c2 in range(0, NCH + LA + 1, 2):
            lists = []
            sc = []
            for ci in (c2, c2 + 1):
                if ci < NCH:
                    lists.append(pre_stages(ci)[:NPRE])
                if 0 <= ci - LA < NCH:
                    sc += scan_stages(ci - LA)[:NSCAN]
            if sc:
                lists.append(sc)
            interleave(lists)
        if RWDBG < 3:
            continue
        P.dma(tl["sw"][:, 64:64 + T], gT[1][rows, :], [(gT[0], fc)], [K("sw")])

        def f3(ps, c0, cn, key):
            sl = slice(64 + c0, 64 + c0 + cn)
            P.stt("dve", tl["y"][:, sl], ps, -1.0 / 64, tl["y"][:, sl], ALU.mult, ALU.add, [key, K("y")], [K("y")])
        b64("y", f3)
        P.act(full("cum"), full("y"), AF.Square, [K("y")], [K("cum")])

        def f4(ps, c0, cn, key):
            sl = slice(64 + c0, 64 + c0 + cn)
            P.ts("dve", tl["Ep"][:, sl], ps, 1.0 / 64, 64e-5, ALU.mult, ALU.add, [key], [K("Ep")])
        b64("cum", f4)
        P.act(full("Ep"), full("Ep"), AF.Sqrt, [K("Ep")], [K("Ep")])
        P.R.op("dve", lambda e, o=full("Ep"): e.reciprocal(out=o, in_=o), [K("Ep")], [K("Ep")])
        P.tt("pool", full("y"), full("y"), full("Ep"), ALU.mult, [K("y"), K("Ep")], [K("y")])
        P.ts("dve", full("y"), full("y"), col("lnx_g"), col("lnx_b"), ALU.mult, ALU.add, [K("y")], [K("y")])
        P.tt("pool", full("y"), full("y"), full("bon"), ALU.add, [K("y"), K("bon")], [K("y")])
        oi = fc % 2
        P.tt("dve", obf[oi][:, :], tl["y"][:, 64:64 + T], tl["sw"][:, 64:64 + T], ALU.mult, [K("y"), K("sw")],
             [("rw_obf", u, oi)])
        P.dma(Xo[1][rows, :], obf[oi][:, :], [("rw_obf", u, oi)], [(Xo[0], fc)])
    P.free_to(mark)


def phase_gla(P, projT, gaT, Xgo):
    cfg = P.cfg
    T = cfg.T
    NCH = len(cfg.chunks)
    TW = NCH * 64
    KCH, VCH = cfg.HK // 128, cfg.HV // 128
    DK, DV, HV = cfg.DK, cfg.DV, cfg.HV
    mark = len(P._cms)
    P.uid += 1
    u = P.uid
    pc = P.pcols
    mk = lambda nm, n: [P.sb(f"gl_{nm}{i}", [128, TW], F32) for i in range(n)]
    q, k, ga, cum = mk("q", KCH), mk("k", KCH), mk("ga", KCH), mk("cum", KCH)
    v, o = mk("v", VCH), mk("o", VCH)
    onesc = P.sb("gl_ones", [128, 64], F32)
    obf = [P.sb(f"gl_obf{i}", [128, T], BF16) for i in range(2)]
    rsd = P.sb("gl_rstd", [128, TW], F32)
    NS = 4
    AT = [P.sb(f"gl_at{i}", [128, 64], F32) for i in range(NS)]
    ktm = [P.sb(f"gl_ktm{i}", [128, KCH * 128], F32) for i in range(NS)]
    vtm = [P.sb(f"gl_vtm{i}", [128, VCH * 128], F32) for i in range(NS)]
    Sst = P.sb("gl_S", [128, KCH, HV], F32)
    Stmp = P.sb("gl_Stmp", [128, HV], F32)
    P.memset("dve", onesc[:, :], 1.0, [("gl_ones", u)])
    K = lambda nm, i: ("gl", u, nm, i)
    for nm, lst in (("q", q), ("k", k), ("ga", ga), ("cum", cum), ("v", v), ("o", o)):
        for i, t_ in enumerate(lst):
            P.memset("pool", t_[:, :], 0.0, [K(nm, i)])
    for i in range(NS):
        P.memset("pool", AT[i][:, :], 0.0, [("gl_at", u, i)])
        P.memset("pool", ktm[i][:, :], 0.0, [("gl_ktm", u, i)])
        P.memset("pool", vtm[i][:, :], 0.0, [("gl_vtm", u, i)])
    pieces = [(c0, min(512, TW - c0)) for c0 in range(0, TW, 512)]
    bankn = [0]

    def bank():
        b = bankn[0] % 8
        bankn[0] += 1
        return P.psum[b], ("ps", b)

    scale = float(cfg.HK) ** -0.5
    for h in range(cfg.GH):
        for kc in range(KCH):
            r0 = h * cfg.HK + kc * 128
            P.dma(q[kc][:, 0:T], projT[1][r0:r0 + 128, :], [(projT[0], r0 // 128)], [K("q", kc)])
            P.dma(k[kc][:, 0:T], projT[1][DK + r0:DK + r0 + 128, :], [(projT[0], (DK + r0) // 128)], [K("k", kc)])
            P.dma(ga[kc][:, 0:T], gaT[1][r0:r0 + 128, :], [(gaT[0], r0 // 128)], [K("ga", kc)])
        for vc in range(VCH):
            r0 = 2 * DK + h * HV + vc * 128
            P.dma(v[vc][:, 0:T], projT[1][r0:r0 + 128, :], [(projT[0], r0 // 128)], [K("v", vc)])
        for kc in range(KCH):
            P.act(ga[kc][:, 0:T], ga[kc][:, 0:T], AF.Ln, [K("ga", kc)], [K("ga", kc)], bias=1.0)
            for ci in range(NCH):
                sl = slice(ci * 64, ci * 64 + 64)
                P.R.op("dve", lambda e, o_=cum[kc][:, sl], d1=ga[kc][:, sl]:
                       e.tensor_tensor_scan(out=o_, data0=onesc[:, :], data1=d1, initial=0.0, op0=ALU.mult, op1=ALU.add),
                       [K("ga", kc), ("gl_ones", u)], [K("cum", kc)])
            P.act(ga[kc][:, :], cum[kc][:, :], AF.Exp, [K("cum", kc)], [K("ga", kc)], scale=-1.0 / 16)
            P.act(cum[kc][:, :], cum[kc][:, :], AF.Exp, [K("cum", kc)], [K("cum", kc)], scale=1.0 / 16)
            P.stt("dve", q[kc][:, :], q[kc][:, :], scale, ga[kc][:, :], ALU.mult, ALU.mult,
                  [K("q", kc), K("ga", kc)], [K("q", kc)])
            P.tt("pool", k[kc][:, :], k[kc][:, :], cum[kc][:, :], ALU.mult, [K("k", kc), K("cum", kc)], [K("k", kc)])
        P.memset("pool", Sst[:, :, :], 0.0, [("gl_S", u, i) for i in range(KCH)])

        def pre_stages(ci):
            s = ci % NS
            ch = slice(ci * 64, ci * 64 + 64)
            st = []

            def a():
                pa, ka = bank()
                for kc in range(KCH):
                    P.mm(pa[0:64, 0:64], k[kc][:, ch], q[kc][:, ch], kc == 0, kc == KCH - 1,
                         [K("k", kc), K("q", kc)], [ka])
                P.tt("dve", AT[s][0:64, :], pa[0:64, 0:64], P.maskG[0:64, :], ALU.mult, [ka], [("gl_at", u, s)])
            st.append(a)

            def b():
                pb, kb = bank()
                for kc in range(KCH):
                    P.mm(pb[0:64, kc * 128:(kc + 1) * 128], k[kc][:, ch], P.ident[:, :], True, True, [K("k", kc)], [kb])
                P.copy("act", ktm[s][0:64, :], pb[0:64, 0:KCH * 128], [kb], [("gl_ktm", u, s)])
                pv, kv = bank()
                for vc in range(VCH):
                    P.mm(pv[0:64, vc * 128:(vc + 1) * 128], v[vc][:, ch], P.ident[:, :], True, True, [K("v", vc)], [kv])
                P.copy("act", vtm[s][0:64, :], pv[0:64, 0:VCH * 128], [kv], [("gl_vtm", u, s)])
            st.append(b)
            return st

        def scan_stages(ci):
            s = ci % NS
            ch = slice(ci * 64, ci * 64 + 64)
            st = []

            def a():
                for vc in range(VCH):
                    po, ko = bank()
                    P.mm(po[:, 0:64], vtm[s][:, vc * 128:(vc + 1) * 128], AT[s][:, :], True, False,
                         [("gl_vtm", u, s), ("gl_at", u, s)], [ko])
                    for kc in range(KCH):
                        P.mm(po[:, 0:64], Sst[:, kc, vc * 128:(vc + 1) * 128], q[kc][:, ch], False, kc == KCH - 1,
                             [("gl_S", u, kc), K("q", kc)], [ko])
                    P.copy("act" if vc % 2 == 0 else "dve", o[vc][:, ch], po[:, 0:64], [ko], [K("o", vc)])
            st.append(a)

            def b():
                for kc in range(KCH):
                    pss, kss = bank()
                    P.mm(pss[:, 0:HV], ktm[s][:, kc * 128:(kc + 1) * 128], vtm[s][:, :], True, True,
                         [("gl_ktm", u, s), ("gl_vtm", u, s)], [kss])
                    P.tt("dve", Stmp[:, :], pss[:, 0:HV], Sst[:, kc, :], ALU.add, [kss, ("gl_S", u, kc)], [("gl_Stmp", u)])
                    P.act(Sst[:, kc, :], Stmp[:, :], AF.Copy, [("gl_Stmp", u), K("ga", kc)], [("gl_S", u, kc)],
                          scale=ga[kc][:, ci * 64 + 63:ci * 64 + 64])
            st.append(b)
            return st

        LA = 2
        for ci in range(NCH + LA):
            lists = []
            if ci < NCH:
                lists.append(pre_stages(ci))
            if ci - LA >= 0:
                lists.append(scan_stages(ci - LA))
            interleave(lists)
        for vc in range(VCH):
            P.act(v[vc][:, :], o[vc][:, :], AF.Square, [K("o", vc)], [K("v", vc)])
        for (c0, cn) in pieces:
            pn, kn = bank()
            for vc in range(VCH):
                P.mm(pn[:, 0:cn], P.ones[:, :], v[vc][:, c0:c0 + cn], vc == 0, vc == VCH - 1, [K("v", vc)], [kn])
            P.ts("dve", rsd[:, c0:c0 + cn], pn[:, 0:cn], 1.0 / HV, 1e-6, ALU.mult, ALU.add, [kn], [("gl_rsd", u)])
        P.act(rsd[:, :], rsd[:, :], AF.Sqrt, [("gl_rsd", u)], [("gl_rsd", u)])
        P.R.op("dve", lambda e, o_=rsd[:, :]: e.reciprocal(out=o_, in_=o_), [("gl_rsd", u)], [("gl_rsd", u)])
        for vc in range(VCH):
            f0 = h * HV + vc * 128
            fc = f0 // 128
            r0 = 2 * DK + DV + f0
            P.dma(v[vc][:, 0:T], projT[1][r0:r0 + 128, :], [(projT[0], r0 // 128)], [K("v", vc)])
            cg = P.pcol_off["gng"] + fc
            cb = P.pcol_off["grb"] + fc
            P.stt("dve", o[vc][:, :], o[vc][:, :], pc[:, cg:cg + 1], rsd[:, :], ALU.mult, ALU.mult,
                  [K("o", vc), ("gl_rsd", u)], [K("o", vc)])
            P.act(v[vc][:, 0:T], v[vc][:, 0:T], AF.Silu, [K("v", vc)], [K("v", vc)], bias=pc[:, cb:cb + 1])
            oi = (h * VCH + vc) % 2
            P.tt("pool", obf[oi][:, :], o[vc][:, 0:T], v[vc][:, 0:T], ALU.mult, [K("o", vc), K("v", vc)],
                 [("gl_obf", u, oi)])
            P.dma(Xgo[1][f0:f0 + 128, :], obf[oi][:, :], [("gl_obf", u, oi)], [(Xgo[0], fc)])
    P.free_to(mark)


def wrelayout(W):
    K, N = W.shape
    KC, MC = (K + 127) // 128, (N + 127) // 128
    if K != KC * 128 or N != MC * 128:
        Wp = np.zeros((KC * 128, MC * 128), np.float32)
        Wp[:K, :N] = W
    else:
        Wp = np.asarray(W, np.float32)
    return np.ascontiguousarray(Wp.reshape(KC, 128, MC, 128).transpose(2, 1, 0, 3).reshape(MC, 128, KC * 128))


def pad128(v):
    v = np.asarray(v, np.float32).reshape(-1)
    n = ((v.size + 127) // 128) * 128
    if n != v.size:
        v = np.concatenate([v, np.zeros(n - v.size, np.float32)])
    return v


def host_prep(cfg, inp):
    W = {}
    W["wr"] = wrelayout(inp["rw_wr"][0]); W["wk"] = wrelayout(inp["rw_wk"][0]); W["wv"] = wrelayout(inp["rw_wv"][0])
    W["wo"] = wrelayout(inp["rw_wo"][0])
    W["w1"] = wrelayout(inp["rw_w1"][0]); W["w2"] = wrelayout(inp["rw_w2"][0])
    W["a1"] = wrelayout(inp["rw_a1"][0]); W["a2"] = wrelayout(inp["rw_a2"][0])
    W["g1"] = wrelayout(inp["rw_g1"][0]); W["g2"] = wrelayout(inp["rw_g2"][0])
    W["gin"] = wrelayout(inp["gla_w_in"][0]); W["ga1"] = wrelayout(inp["gla_a1"][0])
    W["ga2"] = wrelayout(inp["gla_a2"][0]); W["gwo"] = wrelayout(inp["gla_wo"][0])
    for l in range(2):
        W[f"up{l}"] = wrelayout(inp["ffn_up"][l]); W[f"gate{l}"] = wrelayout(inp["ffn_gate"][l])
        W[f"down{l}"] = wrelayout(inp["ffn_down"][l])
    vecs = []
    for l in range(2):
        for j in range(4):
            vecs.append((f"ng{l}_{j}", inp["norm_g"][l, j]))
        for j in range(3):
            vecs.append((f"cw{l}_{j}", inp["ffn_conv"][l, j]))
        vecs.append((f"cb{l}", inp["ffn_conv_b"][l]))
    vecs.append(("mu", inp["rw_mu"][0].reshape(-1)))
    for nm in ("w0", "a0", "k_k", "k_a", "lnx_g", "lnx_b", "r_k"):
        vecs.append((nm, inp["rw_" + nm][0].reshape(-1)))
    vecs.append(("gab", inp["gla_a_b"][0]))
    vecs.append(("grb", inp["gla_r_b"][0]))
    vecs.append(("gng", np.tile(np.asarray(inp["gla_norm_g"][0]), cfg.GH)))
    pc, off = pcol_pack([(n, pad128(v)) for n, v in vecs])
    W["pcols"] = pc
    return W, off


def make_consts():
    c = np.zeros((128, 640), np.float32)
    c[:, 0:128] = np.eye(128)
    for p in range(128):
        for q in range(128):
            if p // 64 == q // 64:
                c[p, 128 + q] = 1.0
    s_ = np.arange(128) % 64
    t_ = np.arange(64)
    strictU = (t_[None, :] > s_[:, None]).astype(np.float32)
    inclU = (t_[None, :] >= s_[:, None]).astype(np.float32)
    c[:, 256:320] = strictU
    c[:, 320:384] = inclU
    c[:, 384:448] = (s_[:, None] > t_[None, :]).astype(np.float32)
    c[:, 448:512] = (s_[:, None] == t_[None, :]).astype(np.float32)
    c[:, 512:576] = inclU
    return c


def make_xT(cfg, inp, b):
    h = np.concatenate([np.asarray(inp["meta"], np.float32), np.asarray(inp["x"][b], np.float32)], axis=0)
    return np.ascontiguousarray(h.T)


def build(cfg, off, ncols, wshapes, upto=99, dbg=None):
    P = Prog(cfg)
    P.pcol_off = off
    D, T, F, DC, FC = cfg.D, cfg.T, cfg.F, cfg.DC, cfg.FC
    xT = P.dram("xT", [D, T], F32, "ExternalInput")
    pcd = P.dram("pcols", [128, ncols], F32, "ExternalInput")
    wd = {n: P.dram(n, list(s), F32, "ExternalInput") for n, s in wshapes.items() if n != "pcols"}
    outT = P.dram("outT", [D, T], F32, "ExternalOutput")
    P.pcols = P.sb("pcols_sb", [128, ncols + 64], F32)
    cst = P.dram("consts", [128, 640], F32, "ExternalInput")
    P.cst = P.sb("consts_sb", [128, 640], F32)
    P.ident = P.cst[:, 0:128]
    P.b64 = P.cst[:, 128:256]
    P.mask12 = P.cst[:, 256:384]
    P.maskL = P.cst[:, 384:448]
    P.identst = P.cst[:, 448:512]
    P.maskG = P.cst[:, 512:576]
    off = dict(off)
    off["ngab"] = ncols
    off["omka"] = ncols + 16
    P.pcol_off = off
    P.ones = P.sb("ones_sb", [128, 128], F32)
    P.sq = [P.sb(f"sq{i}", [128, 449], F32) for i in range(3)]
    P.sqn = 0
    P.psum = [P.ps(f"bank{i}", [128, 512], F32) for i in range(8)]
    P.dma(P.pcols[:, 0:ncols], pcd, [], [("pcols",)])
    P.memset("dve", P.ones[:, :], 1.0, [("ones",)])
    P.dma(P.cst[:, :], cst, [], [("cst",)])
    setup_consts(P)
    P.identbf = P.sb("identbf", [128, 64], BF16)
    P.copy("dve", P.identbf[:, :], P.identst[:, :], [("cst",)], [("identbf",)])
    P.identbf2 = P.sb("identbf2", [128, 128], BF16)
    P.copy("dve", P.identbf2[:, :], P.ident[:, :], [("cst",)], [("identbf2",)])
    P.R.barrier()
    S = {}

    def scratch(name, rows, dt):
        S[name] = P.dram("s_" + name, [rows, T], dt)
        return (name, S[name])

    def job(wname, out, odt=F32, func=AF.Copy, bias=None, scale=None):
        return dict(w=wd[wname], wname=wname, MC=wshapes[wname][0], out=out[1], oname=out[0], odt=odt, func=func,
                    bias=bias, scale=scale)

    hT = scratch("hT", D, F32)
    stage = [0]

    def done():
        stage[0] += 1
        return stage[0] >= upto

    def finish():
        toks = []
        if dbg is not None:
            name, rows, dt = dbg
            dd = P.dram("dbg", [rows, T], dt, "ExternalOutput")
            nch = (rows + 127) // 128
            for c in range(nch):
                r0, r1 = c * 128, min(rows, (c + 1) * 128)
                toks.append(P.dma(dd[r0:r1, :], S[name][r0:r1, :], [(name, c)], [("dbg", c)]))
        hv = S["hT"]
        for c in range(DC):
            toks.append(P.dma(outT[c * 128:(c + 1) * 128, :], hv[c * 128:(c + 1) * 128, :], [("hT", c)], [("outT", c)]))
        P.R.final_wait("sp", toks)
        P.R.emit()
        return P.nc

    Xs = [scratch(f"X{i}", D, BF16) for i in range(6)]
    phase_norm(P, "xT", xT, None, None, "ng0_0", "mix6", Xs, mu="mu", store_h=False)
    for c in range(DC):
        P.dma(S["hT"][c * 128:(c + 1) * 128, :], xT[c * 128:(c + 1) * 128, :], [], [("hT", c)])
    if done():
        return finish()
    rT, kT, vT = scratch("rT", D, F32), scratch("kT", D, F32), scratch("vT", D, F32)
    phase_gemm(P, [job("wr", rT)], Xs[0], DC)
    if done():
        return finish()
    phase_gemm(P, [job("wk", kT)], Xs[2], DC)
    phase_gemm(P, [job("wv", vT)], Xs[3], DC)
    h1w = scratch("h1w", 128 * wshapes["w1"][0], BF16)
    h1a = scratch("h1a", 128 * wshapes["a1"][0], BF16)
    h1g = scratch("h1g", 128 * wshapes["g1"][0], BF16)
    phase_gemm(P, [job("w1", h1w, BF16, AF.Tanh)], Xs[1], DC)
    phase_gemm(P, [job("a1", h1a, BF16)], Xs[4], DC)
    phase_gemm(P, [job("g1", h1g, BF16, AF.Sigmoid)], Xs[5], DC)
    swT, aT, gT = scratch("swT", D, F32), scratch("aT", D, F32), scratch("gT", D, F32)
    phase_gemm(P, [job("w2", swT, F32, AF.Sigmoid, bias="w0")], h1w, wshapes["w1"][0])
    phase_gemm(P, [job("a2", aT, F32, AF.Sigmoid, bias="a0")], h1a, wshapes["a1"][0])
    phase_gemm(P, [job("g2", gT)], h1g, wshapes["g1"][0])
    if done():
        return finish()
    Xo = scratch("Xo", D, BF16)
    phase_rwkv(P, rT, kT, vT, swT, aT, gT, Xo)
    if done():
        return finish()
    mixT = scratch("mixT", D, F32)
    phase_gemm(P, [job("wo", mixT)], Xo, DC)
    if done():
        return finish()
    for l in range(2):
        if l == 1:
            Xg = scratch("Xg", D, BF16)
            phase_norm(P, "hT", S["hT"], ("fT", S["fT"]), "ng0_3", "ng1_0", "plain", [Xg])
            if done():
                return finish()
            projT = scratch("projT", cfg.GIN, F32)
            phase_gemm(P, [job("gin", projT)], Xg, DC)
            h1ga = scratch("h1ga", 128 * wshapes["ga1"][0], BF16)
            phase_gemm(P, [job("ga1", h1ga, BF16)], Xg, DC)
            gaT = scratch("gaT", cfg.DK, F32)
            phase_gemm(P, [job("ga2", gaT, F32, AF.Exp, bias="ngab", scale=-1.0)], h1ga, wshapes["ga1"][0])
            if done():
                return finish()
            Xgo = scratch("Xgo", D, BF16)
            phase_gla(P, projT, gaT, Xgo)
            if done():
                return finish()
            phase_gemm(P, [job("gwo", mixT)], Xgo, DC)
            if done():
                return finish()
        Xf = scratch(f"Xf{l}", D, BF16)
        phase_norm(P, "hT", S["hT"], mixT, f"ng{l}_1", f"ng{l}_2", "plain", [Xf])
        if done():
            return finish()
        uT, zT = scratch(f"uT{l}", F, F32), scratch(f"zT{l}", F, F32)
        phase_gemm(P, [job(f"up{l}", uT), job(f"gate{l}", zT)], Xf, DC)
        if done():
            return finish()
        Xd = scratch(f"Xd{l}", F, BF16)
        phase_ffnact(P, l, uT, zT, Xd)
        if done():
            return finish()
        if l == 0:
            fT = scratch("fT", D, F32)
        blocks = cfg.down_blocks
        phase_gemm(P, [job(f"down{l}", fT)], Xd, FC, tblocks=blocks)
        if done():
            return finish()
    phase_norm(P, "hT", S["hT"], ("fT", S["fT"]), "ng1_3", None, "none", [])
    return finish()


NCORES = 4


def kernel(**inputs):
    import os
    inp = {k: np.asarray(v) for k, v in inputs.items()}
    cfg = Cfg()
    B = inp["x"].shape[0]
    W, off = host_prep(cfg, inp)
    wshapes = {k: v.shape for k, v in W.items()}
    upto = int(os.environ.get("K_UPTO", "99"))
    nc = build(cfg, off, W["pcols"].shape[1], wshapes, upto=upto)
    W["consts"] = make_consts()
    in_maps = []
    for c in range(NCORES):
        m = dict(W)
        m["xT"] = make_xT(cfg, inp, c % B)
        in_maps.append(m)
    res = run_bass_kernel_spmd(nc, in_maps, core_ids=list(range(NCORES)))
    out = np.stack([np.ascontiguousarray(res.results[b]["outT"].T[cfg.NMETA:]) for b in range(B)], axis=0)
    return out.astype(np.float32)
```
